# Optimizing a Trainium2 kernel written in Bass

```python
import jax, jax.numpy as jnp
from jax import lax
import numpy as np

D_MODEL = 2048
BATCH = 8
SEQ = 4096
DEPTH = 4

GRID_W = 64
CTX_LEN = 256
EPS = 1e-6
D_FF = 11 * D_MODEL // 4
N_MOD = 9
N_BRANCH = 3
A_HEADS = 4
A_W = D_MODEL // 2
A_DH = A_W // A_HEADS
B_DK = 128
B_W = D_MODEL // 2
B_HEADS = B_W // B_DK
C_W = D_MODEL // 2
C_BLOCKS = 16
C_DB = C_W // C_BLOCKS
CONV_W = 4
CONV_PAD = (2, 1)
RG_C = 8.0
CHUNK = 64
IN_SPLITS = (A_W, A_W, A_W, A_W, 4 * A_HEADS,
             B_W, B_W, B_W, B_W, B_W,
             C_W, C_W,
             N_BRANCH * D_MODEL)
IN_OFFSETS = tuple(int(v) for v in np.cumsum(IN_SPLITS)[:-1])
IN_WIDTH = int(sum(IN_SPLITS))

kernel_name = 'hybrid_mlstm_hgrn2_rglru_flow_block'


def rmsnorm(x, g):
    xf = x.astype(jnp.float32)
    y = xf * lax.rsqrt(jnp.mean(xf * xf, axis=-1, keepdims=True) + EPS)
    return y.astype(x.dtype) * g


def head_rms(x):
    return x * lax.rsqrt(jnp.mean(x * x, axis=-1, keepdims=True) + EPS)


def adaln(x, g, shift, scale):
    return rmsnorm(x, g) * (1 + scale) + shift


def swiglu(x, w_i, w_o):
    gate, up = jnp.split(x @ w_i, 2, axis=-1)
    return (jax.nn.silu(gate) * up) @ w_o


def to_heads(t, n):
    b, s, _ = t.shape
    return t.reshape(b, s, n, -1).transpose(0, 2, 1, 3).astype(jnp.float32)


def from_heads(t):
    b, n, s, d = t.shape
    return t.transpose(0, 2, 1, 3).reshape(b, s, n * d)


def to_chunks(t):
    b, n, s = t.shape[:3]
    t = t.reshape(b, n, s // CHUNK, CHUNK, *t.shape[3:])
    return jnp.moveaxis(t, 2, 0)


def from_chunks(t):
    t = jnp.moveaxis(t, 0, 2)
    b, n, nc, l = t.shape[:4]
    return t.reshape(b, n, nc * l, *t.shape[4:])


def mlstm_scan(q, k, v, logf, logi, state):
    mask = jnp.tril(jnp.ones((CHUNK, CHUNK), dtype=bool))

    def step(carry, xs):
        cmat, nvec, m = carry
        qc, kc, vc, fc, ic = xs
        b = jnp.cumsum(fc, axis=-1)
        dmat = jnp.where(mask, b[..., :, None] - b[..., None, :] + ic[..., None, :], -jnp.inf)
        inter = b + m[..., None]
        m_t = jnp.maximum(inter, jnp.max(dmat, axis=-1))
        w_inter = jnp.exp(inter - m_t)
        s = jnp.einsum('bhtd,bhsd->bhts', qc, kc) * jnp.exp(dmat - m_t[..., None])
        num = (w_inter[..., None] * jnp.einsum('bhtd,bhde->bhte', qc, cmat)
               + jnp.einsum('bhts,bhse->bhte', s, vc))
        den = w_inter * jnp.einsum('bhtd,bhd->bht', qc, nvec) + jnp.sum(s, axis=-1)
        h_out = num / jnp.maximum(jnp.abs(den), jnp.exp(-m_t))[..., None]
        b_last = b[..., -1]
        wlog = b_last[..., None] - b + ic
        m_new = jnp.maximum(b_last + m, jnp.max(wlog, axis=-1))
        decay = jnp.exp(b_last + m - m_new)
        ws = jnp.exp(wlog - m_new[..., None])[..., None]
        cmat = decay[..., None, None] * cmat + jnp.einsum('bhsd,bhse->bhde', kc * ws, vc)
        nvec = decay[..., None] * nvec + jnp.sum(kc * ws, axis=2)
        return (cmat, nvec, m_new), h_out

    state, hs = lax.scan(step, state, tuple(to_chunks(t) for t in (q, k, v, logf, logi)))
    return from_chunks(hs), state


def hgrn_scan(q, k, v, g, state):
    mask = jnp.tril(jnp.ones((CHUNK, CHUNK), dtype=bool))[:, :, None]

    def step(s_mat, xs):
        qc, kc, vc, gc = xs
        b = jnp.cumsum(gc, axis=2)
        diff = jnp.where(mask, b[:, :, :, None, :] - b[:, :, None, :, :], -jnp.inf)
        att = jnp.sum(qc[:, :, :, None, :] * kc[:, :, None, :, :] * jnp.exp(diff), axis=-1)
        o = (jnp.einsum('bhtd,bhde->bhte', qc * jnp.exp(b), s_mat)
             + jnp.einsum('bhts,bhse->bhte', att, vc))
        b_last = b[:, :, -1:, :]
        s_mat = (jnp.exp(b_last[:, :, 0, :, None]) * s_mat
                 + jnp.einsum('bhsd,bhse->bhde', kc * jnp.exp(b_last - b), vc))
        return s_mat, o

    state, os_ = lax.scan(step, state, tuple(to_chunks(t) for t in (q, k, v, g)))
    return from_chunks(os_), state


def rglru_scan(a, bx, h0):
    bx = bx.at[:, 0].add(a[:, 0] * h0)

    def combine(e1, e2):
        a1, b1 = e1
        a2, b2 = e2
        return a1 * a2, a2 * b1 + b2

    _, hs = lax.associative_scan(combine, (a, bx), axis=1)
    return hs, hs[:, -1]


def run_bidir(scan_fn, axis, init, ctx_f, lat_f, ctx_b, lat_b):
    flip = lambda t: jnp.flip(t, axis)
    oc_f, s_f = scan_fn(*ctx_f, init)
    ol_f, _ = scan_fn(*lat_f, s_f)
    oc_b, s_b = scan_fn(*[flip(t) for t in ctx_b], init)
    ol_b, _ = scan_fn(*[flip(t) for t in lat_b], s_b)
    return oc_f + flip(oc_b), ol_f + flip(ol_b)


def token_mixers(xn_l, xn_c, rows, w_in_l, gate_b_l, lb_l, conv_w_l, conv_b_l, rgw_l, rgb_l,
                 lam_l, branch_w_l, w_out_l, need_ctx):
    f32 = jnp.float32
    nb = xn_l.shape[0]
    pl = jnp.split(xn_l @ w_in_l, IN_OFFSETS, axis=-1)
    pc = jnp.split(xn_c @ w_in_l, IN_OFFSETS, axis=-1)

    def mlstm_inputs(p):
        bsz, s = p[0].shape[:2]
        q = to_heads(p[0], A_HEADS)
        k = to_heads(p[1], A_HEADS) * (A_DH ** -0.5)
        v = to_heads(p[2], A_HEADS)
        g = p[4].astype(f32).reshape(bsz, s, 2, 2, A_HEADS) + gate_b_l
        g = g.transpose(2, 3, 0, 4, 1)
        return ((q, k, v, jax.nn.log_sigmoid(g[0, 1]), g[0, 0]),
                (q, k, v, jax.nn.log_sigmoid(g[1, 1]), g[1, 0]))

    (acf, acb), (alf, alb) = mlstm_inputs(pc), mlstm_inputs(pl)
    a_init = (jnp.zeros((nb, A_HEADS, A_DH, A_DH), f32), jnp.zeros((nb, A_HEADS, A_DH), f32),
              jnp.zeros((nb, A_HEADS), f32))
    hc_a, hl_a = run_bidir(mlstm_scan, 2, a_init, acf, alf, acb, alb)

    def mlstm_out(hsum, p):
        return (from_heads(head_rms(hsum)) * jax.nn.sigmoid(p[3].astype(f32))).astype(p[3].dtype)

    def hgrn_inputs(p):
        q = jax.nn.silu(to_heads(p[5], B_HEADS))
        v = to_heads(p[8], B_HEADS)
        dirs = []
        for d, fpre in enumerate((p[6], p[7])):
            f = lb_l[d] + (1 - lb_l[d]) * jax.nn.sigmoid(fpre.astype(f32))
            f = to_heads(f, B_HEADS)
            dirs.append((q, 1 - f, v, jnp.log(f)))
        return dirs

    (bcf, bcb), (blf, blb) = hgrn_inputs(pc), hgrn_inputs(pl)
    b_init = jnp.zeros((nb, B_HEADS, B_DK, B_DK), f32)
    hc_b, hl_b = run_bidir(hgrn_scan, 2, b_init, bcf, blf, bcb, blb)

    def hgrn_out(osum, p):
        return (from_heads(head_rms(osum)) * jax.nn.silu(p[9].astype(f32))).astype(p[9].dtype)

    def to_cols(t):
        b, s, ch = t.shape
        return t.reshape(b, rows, GRID_W, ch).transpose(0, 2, 1, 3).reshape(b, s, ch)

    def from_cols(t):
        b, s, ch = t.shape
        return t.reshape(b, GRID_W, rows, ch).transpose(0, 2, 1, 3).reshape(b, s, ch)

    def conv(t):
        y = lax.conv_general_dilated(t, conv_w_l[:, None, :], window_strides=(1,), padding=(CONV_PAD,),
                                     dimension_numbers=('NWC', 'WIO', 'NWC'), feature_group_count=C_W)
        return y + conv_b_l

    def rg_inputs(t):
        bsz, s, _ = t.shape
        tf = t.astype(f32)
        blk = tf.reshape(bsz, s, C_BLOCKS, C_DB)
        dirs = []
        for d in range(2):
            r = jax.nn.sigmoid(jnp.einsum('btni,nij->btnj', blk, rgw_l[d, 0]).reshape(bsz, s, C_W) + rgb_l[d, 0])
            i = jax.nn.sigmoid(jnp.einsum('btni,nij->btnj', blk, rgw_l[d, 1]).reshape(bsz, s, C_W) + rgb_l[d, 1])
            log_a = -RG_C * r * jax.nn.softplus(-lam_l[d].astype(f32))
            dirs.append((jnp.exp(log_a), jnp.sqrt(-jnp.expm1(2 * log_a)) * (i * tf)))
        return dirs

    (ccf, ccb), (clf, clb) = rg_inputs(conv(pc[10])), rg_inputs(conv(to_cols(pl[10])))
    hc_c, hl_c = run_bidir(rglru_scan, 1, jnp.zeros((nb, C_W), f32), ccf, clf, ccb, clb)

    def merge(ya, yb, yc, p):
        ga, gb, gc = jnp.split(jax.nn.sigmoid(p[12]), N_BRANCH, axis=-1)
        merged = ga * (ya @ branch_w_l[0]) + gb * (yb @ branch_w_l[1]) + gc * (yc @ branch_w_l[2])
        return merged @ w_out_l

    yc_l = (from_cols(hl_c) * jax.nn.gelu(pl[11].astype(f32))).astype(pl[11].dtype)
    y_lat = merge(mlstm_out(hl_a, pl), hgrn_out(hl_b, pl), yc_l, pl)
    y_ctx = None
    if need_ctx:
        yc_c = (hc_c * jax.nn.gelu(pc[11].astype(f32))).astype(pc[11].dtype)
        y_ctx = merge(mlstm_out(hc_a, pc), hgrn_out(hc_b, pc), yc_c, pc)
    return y_lat, y_ctx


def setup_inputs(seed: int = 0) -> dict:
    key = jax.random.key(seed)
    ks = jax.random.split(key, 24)
    f32 = jnp.float32
    nrm = lambda k, shape, s: jax.random.normal(k, shape, f32) * s
    x = nrm(ks[0], (BATCH, SEQ, D_MODEL), 1.0)
    c = nrm(ks[1], (BATCH, D_MODEL), 1.0)
    ctx = nrm(ks[2], (BATCH, CTX_LEN, D_MODEL), 1.0)
    c_ctx = nrm(ks[3], (D_MODEL,), 1.0)
    mod_w = nrm(ks[4], (DEPTH, D_MODEL, N_MOD * D_MODEL), 0.5 * D_MODEL ** -0.5)
    mod_b = nrm(ks[5], (DEPTH, N_MOD * D_MODEL), 0.02)
    norm_g = 1.0 + nrm(ks[6], (DEPTH, 3, D_MODEL), 0.02)
    ffn_w_in = nrm(ks[7], (DEPTH, 2, D_MODEL, 2 * D_FF), D_MODEL ** -0.5)
    ffn_w_out = nrm(ks[8], (DEPTH, 2, D_FF, D_MODEL), D_FF ** -0.5)
    w_in = nrm(ks[9], (DEPTH, D_MODEL, IN_WIDTH), D_MODEL ** -0.5)
    i_bias = nrm(ks[10], (DEPTH, 2, 1, A_HEADS), 0.1)
    f_bias = jnp.linspace(3.0, 6.0, A_HEADS, dtype=f32) + nrm(ks[11], (DEPTH, 2, 1, A_HEADS), 0.1)
    mlstm_gate_b = jnp.concatenate([i_bias, f_bias], axis=2)
    hgrn_lb_logits = nrm(ks[12], (DEPTH, 2, B_W), 0.1)
    conv_w = nrm(ks[13], (DEPTH, CONV_W, C_W), CONV_W ** -0.5)
    conv_b = nrm(ks[14], (DEPTH, C_W), 0.02)
    rg_gate_w = nrm(ks[15], (DEPTH, 2, 2, C_BLOCKS, C_DB, C_DB), C_DB ** -0.5)
    rg_gate_b = nrm(ks[16], (DEPTH, 2, 2, C_W), 0.02)
    u = jax.random.uniform(ks[17], (DEPTH, 2, C_W), f32, 0.9, 0.999)
    base = u ** (1.0 / RG_C)
    rg_lambda = jnp.log(base) - jnp.log1p(-base)
    branch_w = nrm(ks[18], (DEPTH, N_BRANCH, A_W, D_MODEL), A_W ** -0.5)
    w_out = nrm(ks[19], (DEPTH, D_MODEL, D_MODEL), D_MODEL ** -0.5)
    final_g = 1.0 + nrm(ks[20], (D_MODEL,), 0.02)
    return {'x': x, 'c': c, 'ctx': ctx, 'c_ctx': c_ctx, 'mod_w': mod_w, 'mod_b': mod_b,
            'norm_g': norm_g, 'ffn_w_in': ffn_w_in, 'ffn_w_out': ffn_w_out, 'w_in': w_in,
            'mlstm_gate_b': mlstm_gate_b, 'hgrn_lb_logits': hgrn_lb_logits, 'conv_w': conv_w,
            'conv_b': conv_b, 'rg_gate_w': rg_gate_w, 'rg_gate_b': rg_gate_b, 'rg_lambda': rg_lambda,
            'branch_w': branch_w, 'w_out': w_out, 'final_g': final_g}


def reference(x, c, ctx, c_ctx, mod_w, mod_b, norm_g, ffn_w_in, ffn_w_out, w_in, mlstm_gate_b,
              hgrn_lb_logits, conv_w, conv_b, rg_gate_w, rg_gate_b, rg_lambda, branch_w, w_out, final_g):
    rows = x.shape[1] // GRID_W
    lb_p = jax.nn.softmax(hgrn_lb_logits.astype(jnp.float32), axis=0)
    lb_all = jnp.cumsum(lb_p, axis=0) - lb_p[0:1]
    sc = jax.nn.silu(c)[:, None, :]
    scc = jax.nn.silu(c_ctx)[None, None, :]
    h, hc = x, ctx
    for layer in range(DEPTH):
        last = layer == DEPTH - 1
        ml = jnp.split(sc @ mod_w[layer] + mod_b[layer], N_MOD, axis=-1)
        mc = jnp.split(scc @ mod_w[layer] + mod_b[layer], N_MOD, axis=-1)
        h = h + 0.5 * ml[2] * swiglu(adaln(h, norm_g[layer, 0], ml[0], ml[1]), ffn_w_in[layer, 0], ffn_w_out[layer, 0])
        hc = hc + 0.5 * mc[2] * swiglu(adaln(hc, norm_g[layer, 0], mc[0], mc[1]), ffn_w_in[layer, 0], ffn_w_out[layer, 0])
        y_lat, y_ctx = token_mixers(adaln(h, norm_g[layer, 1], ml[3], ml[4]), adaln(hc, norm_g[layer, 1], mc[3], mc[4]),
                                    rows, w_in[layer], mlstm_gate_b[layer], lb_all[layer], conv_w[layer],
                                    conv_b[layer], rg_gate_w[layer], rg_gate_b[layer], rg_lambda[layer],
                                    branch_w[layer], w_out[layer], not last)
        h = h + ml[5] * y_lat
        h = h + 0.5 * ml[8] * swiglu(adaln(h, norm_g[layer, 2], ml[6], ml[7]), ffn_w_in[layer, 1], ffn_w_out[layer, 1])
        if not last:
            hc = hc + mc[5] * y_ctx
            hc = hc + 0.5 * mc[8] * swiglu(adaln(hc, norm_g[layer, 2], mc[6], mc[7]), ffn_w_in[layer, 1], ffn_w_out[layer, 1])
    return rmsnorm(h, final_g)
```

```python
import numpy as np
from contextlib import ExitStack
import concourse.bass as bass
import concourse.mybir as mybir
from concourse.bass_utils import run_bass_kernel_spmd

F32 = mybir.dt.float32
AF = mybir.ActivationFunctionType
ALU = mybir.AluOpType

D = 2048
KC = D // 128
DFF = 5632
FC = DFF // 128
NMOD = 9
EPS = 1e-6
INW = 17424
NFMC = 80
NTMB = 28
NTM = NTMB * 256 + 16
FM_COLS = [(0, 1024), (4112, 1024), (9232, 1024), (10256, 1024), (11280, 6144)]
TM_COLS = [(1024, 1024), (2048, 1024), (3072, 1024), (5136, 1024), (6160, 1024), (7184, 1024), (8208, 1024)]


F32R_ = mybir.dt.float32r


class Sched:
    def __init__(self, nc, es, n_dma_sems=6):
        self.nc = nc
        self.eng = {"pe": nc.tensor, "act": nc.scalar, "dve": nc.vector, "pool": nc.gpsimd, "sp": nc.sync}
        self.semh = {}
        self.cnt = {}
        for e in self.eng:
            self.semh[e] = es.enter_context(nc.semaphore("s_" + e))
            self.cnt[e] = 0
        self.dq = {}
        for q in ("sp", "pool", "act"):
            lst = []
            for i in range(n_dma_sems):
                k = "d_%s%d" % (q, i)
                self.semh[k] = es.enter_context(nc.semaphore(k))
                self.cnt[k] = 0
                lst.append(k)
            self.dq[q] = [lst, 0]
        self.known = {e: {} for e in self.eng}
        self.state = {}
        self.ninst = 0

    IGN = frozenset(["PTfm", "Ptm", "HAB", "YT", "hT", "LBD", "out"])

    def _deps(self, reads, writes):
        reads = [k for k in reads if k not in self.IGN]
        writes = [k for k in writes if k not in self.IGN]
        deps = []
        for k in reads:
            st = self.state.get(k)
            if st and st[0]:
                deps.append(st[0])
        for k in writes:
            st = self.state.get(k)
            if st:
                if st[0]:
                    deps.append(st[0])
                deps.extend(st[1].items())
        return deps

    def _wait(self, e, deps):
        kn = self.known[e]
        for (sk, val) in deps:
            if kn.get(sk, 0) < val:
                self.eng[e].wait_ge(self.semh[sk], val)
                kn[sk] = val

    def _commit(self, tok, reads, writes):
        reads = [k for k in reads if k not in self.IGN]
        writes = [k for k in writes if k not in self.IGN]
        for k in writes:
            self.state[k] = [tok, {}]
        for k in reads:
            if k in writes:
                continue
            st = self.state.setdefault(k, [None, {}])
            if st[1].get(tok[0], 0) < tok[1]:
                st[1][tok[0]] = tok[1]

    def op(self, e, emit, reads=(), writes=()):
        self._wait(e, self._deps(reads, writes))
        ins = emit(self.eng[e])
        self.cnt[e] += 1
        ins.then_inc(self.semh[e], 1)
        self._commit((e, self.cnt[e]), reads, writes)
        self.ninst += 1

    def dma(self, q, out, in_, reads=(), writes=()):
        lst, rr = self.dq[q]
        sk = lst[rr % len(lst)]
        self.dq[q][1] = rr + 1
        deps = self._deps(reads, writes)
        if self.cnt[sk] > 0:
            deps.append((sk, self.cnt[sk]))
        self._wait(q, deps)
        if out.dtype == F32R_:
            self.nc.dge_precook = False
            ins = self.eng[q].dma_start(out=out, in_=in_)
            self.nc.dge_precook = True
        else:
            ins = self.eng[q].dma_start(out=out, in_=in_)
        self.cnt[sk] += 16
        ins.then_inc(self.semh[sk], 16)
        self._commit((sk, self.cnt[sk]), reads, writes)
        self.ninst += 1

    def barrier(self):
        toks = [(k, v) for k, v in self.cnt.items() if v > 0]
        for e in self.eng:
            self._wait(e, toks)
        self.state = {}

    def finish(self, keys):
        self.barrier()


class Cfg:
    def __init__(self, SEQ=4096, CTX=256, DEPTH=4, G=512):
        self.SEQ, self.CTX, self.DEPTH, self.G = SEQ, CTX, DEPTH, G
        self.T = SEQ + CTX
        self.rows = SEQ // 64
        self.groups = [(0, CTX, 1)] + [(CTX + i * G, G, 0) for i in range(SEQ // G)]


F32R = mybir.dt.float32r


def RR(ap):
    return ap.bitcast(F32R)


import os
FLAGS = {"proj": os.environ.get("RR_PROJ", "1") == "1", "merge": os.environ.get("RR_MERGE", "1") == "1"}


def RRp(ap):
    return RR(ap) if FLAGS["proj"] else ap


def RRm(ap):
    return RR(ap) if FLAGS["merge"] else ap


def col(t, i):
    return t[:, i:i + 1]


def build(cfg, stop_after=None):
    nc = bass.Bass("TRN2", target_bir_lowering=False)
    L, T, G = cfg.DEPTH, cfg.T, cfg.G
    dt = nc.dram_tensor
    I = {}
    def inp(name, shape):
        I[name] = dt(name, list(shape), F32, kind="ExternalInput")
        return I[name]
    xT = inp("xT", [D, T])
    cvec = inp("cvec", [128, KC * 2])
    modw = inp("modw", [L, 144, 128, KC * 128])
    modb = inp("modb", [128, L * 144])
    normg = inp("normg", [128, L * 3 * KC])
    w1t = inp("w1t", [L, 2, 88, 128, KC * 128])
    w2t = inp("w2t", [L, 2, KC, 128, FC * 128])
    fing = inp("fing", [128, KC])
    wpf = inp("wpf", [L, NFMC, 128, KC * 128])
    wpt = inp("wpt", [L, NTMB, 128, KC * 256])
    wpg = inp("wpg", [L, 128, KC * 16])
    convw = inp("convw", [128, L * 4 * 8])
    convb = inp("convb", [128, L * 8])
    rgb = inp("rgb", [128, L * 4 * 8])
    rglam = inp("rglam", [128, L * 2 * 8])
    rgw = inp("rgw", [L, 4, 8, 128, 128])
    lblog = inp("lblog", [128, L * 2 * 1024])
    mgb = inp("mgb", [128, L * 16])
    wbr = inp("wbr", [L, 3, KC, 128, 8 * 128])
    wo = inp("wo", [L, KC, 128, KC * 128])
    YT = dt("YT", [3 * 1024, T], F32, kind="Internal")
    HAB = dt("HAB", [4, T, 1024], F32, kind="Internal")
    LBD = dt("LBD", [L * 2, 128, 1024], F32, kind="Internal")
    PTfm = dt("PTfm", [NFMC * 128, T], F32, kind="Internal")
    Ptm = dt("Ptm", [T, NTM], F32, kind="Internal")
    consts = inp("consts", [128, 1024])
    out = dt("out", [D, cfg.SEQ], F32, kind="ExternalOutput")
    hT = dt("hT", [D, T], F32, kind="Internal")

    es = ExitStack()
    with es:
        S = Sched(nc, es)
        uid = [0]
        def alloc(stack, name, shape):
            uid[0] += 1
            return stack.enter_context(nc.sbuf_tensor("%s_%d" % (name, uid[0]), list(shape), F32))
        sb = lambda name, shape: alloc(es, name, shape)
        cst = sb("cst", [128, 1024])
        ones = cst[:, 0:128]
        ident = cst[:, 128:256]
        sv = sb("sv", [128, KC * 2])
        modt = sb("modt", [128, 144 * 2])
        modbt = sb("modbt", [128, L * 144])
        ngt = sb("ngt", [128, L * 3 * KC])
        fgt = sb("fgt", [128, KC])
        AB = sb("AB", [128, 3 * 2 * 3 * KC])
        ps = [es.enter_context(nc.psum_tensor("ps%d" % i, [128, 512], F32)) for i in range(8)]

        S.dma("sp", RR(cst[:]), RR(consts.ap()), writes=["cst"])
        S.dma("sp", sv[:], cvec.ap(), writes=["sv"])
        S.dma("sp", modbt[:], modb.ap(), writes=["modbt"])
        S.dma("sp", ngt[:], normg.ap(), writes=["ngt"])
        S.dma("sp", fgt[:], fing.ap(), writes=["fgt"])
        S.op("act", lambda e: e.activation(out=sv[:], in_=sv[:], func=AF.Silu), reads=["sv"], writes=["sv"])

        def ABv(s, stream, which):
            o = ((s * 2 + stream) * 3 + which) * KC
            return AB[:, o:o + KC]

        def phase_mod(l, es2):
            wb = [alloc(es2, "mw%d" % i, [128, KC * 128]) for i in range(2)]
            for c in range(144):
                b = wb[c % 2]
                S.dma("sp", b[:], modw.ap()[l, c], writes=["mw%d" % (c % 2)])
                p = ps[c % 2]
                def mm(e, b=b, p=p):
                    for kc in range(KC):
                        r = e.matmul(p[:, 0:2], lhsT=b[:, kc * 128:(kc + 1) * 128], rhs=sv[:, 2 * kc:2 * kc + 2],
                                     start=(kc == 0), stop=(kc == KC - 1))
                    return r
                S.op("pe", mm, reads=["mw%d" % (c % 2), "sv"], writes=["ps%d" % (c % 2)])
                S.op("dve", lambda e, p=p, c=c: e.tensor_scalar(out=modt[:, 2 * c:2 * c + 2], in0=p[:, 0:2],
                                                                scalar1=col(modbt, l * 144 + c), scalar2=None, op0=ALU.add),
                     reads=["ps%d" % (c % 2), "modbt"], writes=["modt"])
            mv = modt[:].rearrange("p (m k s) -> p m k s", m=NMOD, k=KC)
            for s in range(3):
                for st in range(2):
                    g = ngt[:, (l * 3 + s) * KC:(l * 3 + s + 1) * KC]
                    S.op("dve", lambda e, s=s, st=st, g=g: e.scalar_tensor_tensor(
                        out=ABv(s, st, 0), in0=mv[:, 3 * s + 1, :, st], scalar=1.0, in1=g, op0=ALU.add, op1=ALU.mult),
                        reads=["modt", "ngt"], writes=["AB"])
                    S.op("dve", lambda e, s=s, st=st: e.tensor_copy(out=ABv(s, st, 1), in_=mv[:, 3 * s, :, st]),
                         reads=["modt"], writes=["AB"])
                    S.op("dve", lambda e, s=s, st=st: e.tensor_scalar(
                        out=ABv(s, st, 2), in0=mv[:, 3 * s + 2, :, st], scalar1=(1.0 if s == 1 else 0.5), scalar2=None,
                        op0=ALU.mult), reads=["modt"], writes=["AB"])

        def adaln(hx, xn, rstd, n, s, st, pss, tag, sqkeys=("sqb",)):
            S.op("act", lambda e: e.activation(out=RR(xn[:, 0:KC * n]), in_=hx[:, 0:KC * n], func=AF.Square),
                 reads=["hx"], writes=list(sqkeys))
            def mm(e):
                for kc in range(KC):
                    r = e.matmul(pss[:, 0:n], lhsT=RR(ones), rhs=RR(xn[:, kc * n:(kc + 1) * n]), start=(kc == 0), stop=(kc == KC - 1))
                return r
            S.op("pe", mm, reads=list(sqkeys) + ["cst"], writes=[tag])
            S.op("dve", lambda e: e.tensor_scalar(out=rstd[:, 0:n], in0=pss[:, 0:n], scalar1=1.0 / D, scalar2=EPS,
                                                  op0=ALU.mult, op1=ALU.add), reads=[tag], writes=["rstd"])
            S.op("act", lambda e: e.activation(out=rstd[:, 0:n], in_=rstd[:, 0:n], func=AF.Sqrt), reads=["rstd"], writes=["rstd"])
            S.op("dve", lambda e: e.reciprocal(out=rstd[:, 0:n], in_=rstd[:, 0:n]), reads=["rstd"], writes=["rstd"])
            for kc in range(KC):
                S.op("dve", lambda e, kc=kc: e.scalar_tensor_tensor(
                    out=RR(hx[:, kc * n:(kc + 1) * n]), in0=hx[:, kc * n:(kc + 1) * n], scalar=col(ABv(s, st, 0), kc),
                    in1=rstd[:, 0:n], op0=ALU.mult, op1=ALU.mult), reads=["hx", "rstd", "AB"], writes=["xn%d" % kc])
                S.op("act", lambda e, kc=kc: e.activation(
                    out=RR(hx[:, kc * n:(kc + 1) * n]), in_=hx[:, kc * n:(kc + 1) * n], func=AF.Identity,
                    bias=col(ABv(s, st, 1), kc)), reads=["xn%d" % kc, "AB"], writes=["xn%d" % kc])
            return ["xn%d" % kc for kc in range(KC)]

        def load_h(hx, src, t0, n):
            S.dma("pool", RR(hx[:, 0:KC * n]).rearrange("p (k t) -> p k t", k=KC),
                  RR(src.ap()[:, t0:t0 + n]).rearrange("(k p) t -> p k t", p=128), reads=["hT"], writes=["hx"] + ["xn%d" % kc for kc in range(KC)])

        def phase_ffn(l, w, s, src, es2, skip_ctx=False):
            hx = alloc(es2, "hx", [128, KC * G])
            act = alloc(es2, "actb", [128, FC * G])
            rstd = alloc(es2, "rstd", [128, G])
            sg = [alloc(es2, "sg%d" % i, [128, G]) for i in range(2)]
            ho = [alloc(es2, "ho%d" % i, [128, G]) for i in range(2)]
            hr = [alloc(es2, "hr%d" % i, [128, G]) for i in range(2)]
            wb = [alloc(es2, "wb%d" % i, [128, FC * 128]) for i in range(2)]
            xn = hx
            for (t0, n, st) in cfg.groups:
                if skip_ctx and st == 1:
                    continue
                load_h(hx, src, t0, n)
                xk = adaln(hx, act, rstd, n, s, st, ps[7], "ps7", sqkeys=["act%d" % j for j in range(KC)])
                for j in range(FC):
                    b = wb[j % 2]; bk = "wb%d" % (j % 2)
                    S.dma("sp", RR(b[:, 0:2048]), RR(w1t.ap()[l, w, j]), writes=[bk + "a"])
                    S.dma("act", RR(b[:, 2048:4096]), RR(w1t.ap()[l, w, FC + j]), writes=[bk + "b"])
                    pg, pu = ps[(j % 2) * 2], ps[(j % 2) * 2 + 1]
                    kg, ku = "ps%d" % ((j % 2) * 2), "ps%d" % ((j % 2) * 2 + 1)
                    def mm(e, b=b, p=pg, o=0):
                        for kc in range(KC):
                            r = e.matmul(p[:, 0:n], lhsT=RR(b[:, o + kc * 128:o + (kc + 1) * 128]), rhs=RR(xn[:, kc * n:(kc + 1) * n]),
                                         start=(kc == 0), stop=(kc == KC - 1))
                        return r
                    S.op("pe", mm, reads=[bk + "a"] + xk, writes=[kg])
                    S.op("pe", lambda e, b=b, pu=pu: mm(e, b, pu, 2048), reads=[bk + "b"] + xk, writes=[ku])
                    sgt = sg[j % 2]; sk = "sg%d" % (j % 2)
                    S.op("act", lambda e, sgt=sgt, pg=pg: e.activation(out=sgt[:, 0:n], in_=pg[:, 0:n], func=AF.Silu),
                         reads=[kg], writes=[sk])
                    S.op("dve", lambda e, sgt=sgt, pu=pu, j=j: e.tensor_tensor(out=RR(act[:, j * n:(j + 1) * n]), in0=pu[:, 0:n],
                                                                           in1=sgt[:, 0:n], op=ALU.mult),
                         reads=[sk, ku], writes=["act%d" % j])
                ak = ["act%d" % j for j in range(FC)]
                for fo in range(KC):
                    b = wb[fo % 2]; bk = "wb%d" % (fo % 2)
                    S.dma("sp", RR(b[:, 0:2048]), RR(w2t.ap()[l, w, fo][:, 0:2048]), writes=[bk + "a"])
                    S.dma("act", RR(b[:, 2048:5632]), RR(w2t.ap()[l, w, fo][:, 2048:5632]), writes=[bk + "b", bk + "c"])
                    hrt = hr[fo % 2]; rk = "hr%d" % (fo % 2)
                    S.dma("pool", hrt[:, 0:n], src.ap()[fo * 128:(fo + 1) * 128, t0:t0 + n], writes=[rk])
                    p = ps[4 + fo % 2]; pk = "ps%d" % (4 + fo % 2)
                    def mm2(e, b=b, p=p):
                        for j in range(FC):
                            r = e.matmul(p[:, 0:n], lhsT=RR(b[:, j * 128:(j + 1) * 128]), rhs=RR(act[:, j * n:(j + 1) * n]),
                                         start=(j == 0), stop=(j == FC - 1))
                        return r
                    S.op("pe", mm2, reads=[bk + "a", bk + "b", bk + "c"] + ak, writes=[pk])
                    hot = ho[fo % 2]; hk = "ho%d" % (fo % 2)
                    S.op("dve", lambda e, hot=hot, p=p, fo=fo, hrt=hrt: e.scalar_tensor_tensor(
                        out=hot[:, 0:n], in0=p[:, 0:n], scalar=col(ABv(s, st, 2), fo), in1=hrt[:, 0:n],
                        op0=ALU.mult, op1=ALU.add), reads=[pk, rk, "AB"], writes=[hk])
                    S.dma("pool", hT.ap()[fo * 128:(fo + 1) * 128, t0:t0 + n], hot[:, 0:n], reads=[hk], writes=["hT"])

        def phase_proj(l, es2):
            hx = alloc(es2, "hx", [128, KC * G])
            xn = alloc(es2, "xn", [128, KC * G])
            rstd = alloc(es2, "rstd", [128, G])
            wb = [alloc(es2, "wb%d" % i, [128, KC * 256]) for i in range(2)]
            wg = alloc(es2, "wg", [128, KC * 16])
            so = [alloc(es2, "so%d" % i, [128, max(G, 256)]) for i in range(2)]
            S.dma("sp", wg[:], wpg.ap()[l], writes=["wg"])
            for (t0, n, st) in cfg.groups:
                load_h(hx, hT, t0, n)
                xk = adaln(hx, xn, rstd, n, 1, st, ps[7], "ps7")
                xn_ = hx
                it = 0
                for c in range(NFMC):
                    b = wb[it % 2]; bk = "wb%d" % (it % 2); p = ps[it % 2]; pk = "ps%d" % (it % 2)
                    S.dma("sp" if c % 2 == 0 else "act", RRp(b[:, 0:2048]), RRp(wpf.ap()[l, c]), writes=[bk])
                    def mm(e, b=b, p=p):
                        for kc in range(KC):
                            r = e.matmul(p[:, 0:n], lhsT=RRp(b[:, kc * 128:(kc + 1) * 128]), rhs=RRp(xn_[:, kc * n:(kc + 1) * n]),
                                         start=(kc == 0), stop=(kc == KC - 1))
                        return r
                    S.op("pe", mm, reads=[bk] + xk, writes=[pk])
                    sot = so[it % 2]; sk = "so%d" % (it % 2)
                    S.op("act", lambda e, sot=sot, p=p: e.activation(out=sot[:, 0:n], in_=p[:, 0:n], func=AF.Copy), reads=[pk], writes=[sk])
                    S.dma("pool", PTfm.ap()[c * 128:(c + 1) * 128, t0:t0 + n], sot[:, 0:n], reads=[sk], writes=["PTfm"])
                    it += 1
                for cb in range(NTMB + 1):
                    gate = (cb == NTMB)
                    ncol = 16 if gate else 256
                    if gate:
                        b = wg; bk = "wg"
                    else:
                        b = wb[it % 2]; bk = "wb%d" % (it % 2)
                        S.dma("sp", RRp(b[:, 0:2048]), RRp(wpt.ap()[l, cb][:, 0:2048]), writes=[bk])
                        S.dma("act", RRp(b[:, 2048:4096]), RRp(wpt.ap()[l, cb][:, 2048:4096]), writes=[bk + "h"])
                    for ts in range(n // 128):
                        p = ps[2 + it % 2]; pk = "ps%d" % (2 + it % 2)
                        def mm(e, b=b, p=p, ts=ts, ncol=ncol):
                            for kc in range(KC):
                                if ncol >= 256:
                                    r = e.matmul(p[:, 0:ncol], lhsT=RRp(xn_[:, kc * n + ts * 128:kc * n + (ts + 1) * 128]),
                                                 rhs=RRp(b[:, kc * ncol:(kc + 1) * ncol]), start=(kc == 0), stop=(kc == KC - 1))
                                else:
                                    r = e.matmul(p[:, 0:ncol], lhsT=xn_[:, kc * n + ts * 128:kc * n + (ts + 1) * 128],
                                                 rhs=b[:, kc * ncol:(kc + 1) * ncol], start=(kc == 0), stop=(kc == KC - 1))
                            return r
                        S.op("pe", mm, reads=[bk, bk + "h"] + xk, writes=[pk])
                        sot = so[it % 2]; sk = "so%d" % (it % 2)
                        S.op("act", lambda e, sot=sot, p=p, ncol=ncol: e.activation(out=sot[:, 0:ncol], in_=p[:, 0:ncol], func=AF.Copy),
                             reads=[pk], writes=[sk])
                        S.dma("pool", Ptm.ap()[t0 + ts * 128:t0 + (ts + 1) * 128, cb * 256:cb * 256 + ncol], sot[:, 0:ncol],
                              reads=[sk], writes=["Ptm"])
                        it += 1

        def rev(ap2d, n):
            a = ap2d.ap
            return bass.AP(ap2d.tensor, ap2d.offset + (n - 1) * a[-1][0], [list(a[0]), [-a[-1][0], n]])

        def chunk_order(d):
            nct, nlt = cfg.CTX // 128, cfg.SEQ // 128
            if d == 0:
                return list(range(nct + nlt))
            return list(range(nct - 1, -1, -1)) + list(range(nct + nlt - 1, nct - 1, -1))

        def phase_lb(es2):
            lg = alloc(es2, "lg", [128, L * 2048])
            sm = alloc(es2, "sm", [128, 2048])
            cum = alloc(es2, "cum", [128, 2048])
            S.dma("sp", lg[:], lblog.ap(), writes=["lg"])
            S.op("dve", lambda e: e.tensor_copy(out=sm[:], in_=lg[:, 0:2048]), reads=["lg"], writes=["sm"])
            for k in range(1, L):
                S.op("dve", lambda e, k=k: e.tensor_tensor(out=sm[:], in0=sm[:], in1=lg[:, k * 2048:(k + 1) * 2048], op=ALU.max),
                     reads=["lg", "sm"], writes=["sm"])
            for k in range(L):
                S.op("dve", lambda e, k=k: e.tensor_tensor(out=lg[:, k * 2048:(k + 1) * 2048], in0=lg[:, k * 2048:(k + 1) * 2048], in1=sm[:],
                                                           op=ALU.subtract), reads=["lg", "sm"], writes=["lg"])
            S.op("act", lambda e: e.activation(out=lg[:], in_=lg[:], func=AF.Exp), reads=["lg"], writes=["lg"])
            S.op("dve", lambda e: e.tensor_copy(out=sm[:], in_=lg[:, 0:2048]), reads=["lg", "sm"], writes=["sm"])
            for k in range(1, L):
                S.op("dve", lambda e, k=k: e.tensor_tensor(out=sm[:], in0=sm[:], in1=lg[:, k * 2048:(k + 1) * 2048], op=ALU.add),
                     reads=["lg", "sm"], writes=["sm"])
            S.op("dve", lambda e: e.reciprocal(out=sm[:], in_=sm[:]), reads=["sm"], writes=["sm"])
            S.op("pool", lambda e: e.memset(cum[:], 0.0), writes=["cum"])
            for l in range(L):
                if l > 0:
                    S.op("dve", lambda e, l=l: e.tensor_tensor(out=lg[:, l * 2048:(l + 1) * 2048], in0=lg[:, l * 2048:(l + 1) * 2048],
                                                               in1=sm[:], op=ALU.mult), reads=["lg", "sm"], writes=["lg"])
                    S.op("dve", lambda e, l=l: e.tensor_tensor(out=cum[:], in0=cum[:], in1=lg[:, l * 2048:(l + 1) * 2048], op=ALU.add),
                         reads=["lg", "cum"], writes=["cum"])
                S.op("dve", lambda e, l=l: e.tensor_scalar(out=lg[:, l * 2048:(l + 1) * 2048], in0=cum[:], scalar1=-1.0, scalar2=1.0,
                                                           op0=ALU.mult, op1=ALU.add), reads=["cum", "lg"], writes=["lg"])
                for d in range(2):
                    S.dma("sp", LBD.ap()[l * 2 + d], lg[:, l * 2048 + d * 1024:l * 2048 + (d + 1) * 1024], reads=["lg"], writes=["LBD"])

        def phase_rglru(l, es2):
            T_, CT, SQ = cfg.T, cfg.CTX, cfg.SEQ
            A_, B_, C_, D_, E_, F_ = [alloc(es2, "rb%d" % i, [128, T_ + 8]) for i in range(6)]
            bdw = alloc(es2, "bdw", [128, 4 * 128])
            cw = alloc(es2, "cw", [128, L * 32]); cb = alloc(es2, "cb", [128, L * 8])
            gb = alloc(es2, "gb", [128, L * 32]); lam = alloc(es2, "lam", [128, L * 16])
            S.dma("sp", cw[:], convw.ap(), writes=["cw"]); S.dma("sp", cb[:], convb.ap(), writes=["cb"])
            S.dma("sp", gb[:], rgb.ap(), writes=["gb"]); S.dma("sp", lam[:], rglam.ap(), writes=["lam"])
            S.op("act", lambda e: e.activation(out=lam[:], in_=lam[:], func=AF.Exp, scale=-1.0), reads=["lam"], writes=["lam"])
            S.op("dve", lambda e: e.tensor_scalar(out=lam[:], in0=lam[:], scalar1=1.0, scalar2=None, op0=ALU.add), reads=["lam"], writes=["lam"])
            S.op("act", lambda e: e.activation(out=lam[:], in_=lam[:], func=AF.Ln), reads=["lam"], writes=["lam"])
            S.op("dve", lambda e: e.tensor_scalar(out=lam[:], in0=lam[:], scalar1=-8.0, scalar2=None, op0=ALU.mult), reads=["lam"], writes=["lam"])
            for cc in range(8):
                S.dma("sp", A_[:, 0:T_], PTfm.ap()[(16 + cc) * 128:(17 + cc) * 128, :], reads=["PTfm"], writes=["rA"])
                S.dma("sp", F_[:, 0:T_], PTfm.ap()[(24 + cc) * 128:(25 + cc) * 128, :], reads=["PTfm"], writes=["rF"])
                S.dma("sp", bdw[:].rearrange("p (a m) -> p a m", a=4), rgw.ap()[l, :, cc].rearrange("a p m -> p a m"), writes=["bdw"])
                S.op("pool", lambda e: e.memset(B_[:], 0.0), writes=["rB"])
                S.op("dve", lambda e: e.tensor_copy(out=B_[:, 2:2 + CT], in_=A_[:, 0:CT]), reads=["rA", "rB"], writes=["rB"])
                S.op("dve", lambda e: e.tensor_copy(out=B_[:, CT + 5:CT + 5 + SQ].rearrange("p (w r) -> p w r", w=64),
                                                    in_=A_[:, CT:T_].rearrange("p (r w) -> p w r", w=64)), reads=["rA", "rB"], writes=["rB"])
                for (oo, oi, n) in ((0, 0, CT), (CT, CT + 3, SQ)):
                    for j in range(4):
                        wj = col(cw, (l * 4 + j) * 8 + cc)
                        if j == 0:
                            S.op("dve", lambda e, oo=oo, oi=oi, n=n, wj=wj: e.tensor_scalar(
                                out=C_[:, oo:oo + n], in0=B_[:, oi:oi + n], scalar1=wj, scalar2=col(cb, l * 8 + cc),
                                op0=ALU.mult, op1=ALU.add), reads=["rB", "cw", "cb"], writes=["rC"])
                        else:
                            S.op("dve", lambda e, oo=oo, oi=oi, n=n, wj=wj, j=j: e.scalar_tensor_tensor(
                                out=C_[:, oo:oo + n], in0=B_[:, oi + j:oi + j + n], scalar=wj, in1=C_[:, oo:oo + n],
                                op0=ALU.mult, op1=ALU.add), reads=["rB", "rC", "cw"], writes=["rC"])
                for d in range(2):
                    for ri, dst, dk in ((0, A_, "rA"), (1, B_, "rB")):
                        for i, tt in enumerate(range(0, T_, 512)):
                            n = min(512, T_ - tt)
                            p = ps[i % 2]; pk = "ps%d" % (i % 2)
                            S.op("pe", lambda e, p=p, tt=tt, n=n, ri=ri: e.matmul(
                                p[:, 0:n], lhsT=bdw[:, (d * 2 + ri) * 128:(d * 2 + ri + 1) * 128], rhs=C_[:, tt:tt + n], start=True, stop=True),
                                reads=["bdw", "rC"], writes=[pk])
                            S.op("act", lambda e, p=p, tt=tt, n=n, ri=ri, dst=dst: e.activation(
                                out=dst[:, tt:tt + n], in_=p[:, 0:n], func=AF.Sigmoid, bias=col(gb, ((l * 2 + d) * 2 + ri) * 8 + cc)),
                                reads=[pk, "gb"], writes=[dk])
                    S.op("act", lambda e: e.activation(out=A_[:, 0:T_], in_=A_[:, 0:T_], func=AF.Exp, scale=col(lam, (l * 2 + d) * 8 + cc)),
                         reads=["rA", "lam"], writes=["rA"])
                    S.op("dve", lambda e: e.tensor_tensor(out=B_[:, 0:T_], in0=B_[:, 0:T_], in1=C_[:, 0:T_], op=ALU.mult),
                         reads=["rB", "rC"], writes=["rB"])
                    hd, hk = (D_, "rD") if d == 0 else (E_, "rE")
                    S.op("pool", lambda e, hd=hd: e.tensor_tensor(out=hd[:, 0:T_], in0=A_[:, 0:T_], in1=A_[:, 0:T_], op=ALU.mult),
                         reads=["rA"], writes=[hk])
                    S.op("dve", lambda e, hd=hd: e.tensor_scalar(out=hd[:, 0:T_], in0=hd[:, 0:T_], scalar1=-1.0, scalar2=1.0,
                                                                 op0=ALU.mult, op1=ALU.add), reads=[hk], writes=[hk])
                    S.op("act", lambda e, hd=hd: e.activation(out=hd[:, 0:T_], in_=hd[:, 0:T_], func=AF.Sqrt), reads=[hk], writes=[hk])
                    S.op("dve", lambda e, hd=hd: e.tensor_tensor(out=B_[:, 0:T_], in0=B_[:, 0:T_], in1=hd[:, 0:T_], op=ALU.mult),
                         reads=["rB", hk], writes=["rB"])
                    if d == 0:
                        S.op("dve", lambda e: e.tensor_tensor_scan(out=D_[:, 0:T_], data0=A_[:, 0:T_], data1=B_[:, 0:T_], initial=0.0,
                                                                   op0=ALU.mult, op1=ALU.add), reads=["rA", "rB", "rD"], writes=["rD"])
                    else:
                        S.op("dve", lambda e: e.tensor_tensor_scan(out=rev(E_[:, 0:CT], CT), data0=rev(A_[:, 0:CT], CT),
                                                                   data1=rev(B_[:, 0:CT], CT), initial=0.0, op0=ALU.mult, op1=ALU.add),
                             reads=["rA", "rB", "rE"], writes=["rE"])
                        S.op("dve", lambda e: e.tensor_tensor_scan(out=rev(E_[:, CT:T_], SQ), data0=rev(A_[:, CT:T_], SQ),
                                                                   data1=rev(B_[:, CT:T_], SQ), initial=E_[:, 0:1], op0=ALU.mult, op1=ALU.add),
                             reads=["rA", "rB", "rE"], writes=["rE"])
                S.op("dve", lambda e: e.tensor_tensor(out=D_[:, 0:T_], in0=D_[:, 0:T_], in1=E_[:, 0:T_], op=ALU.add), reads=["rD", "rE"], writes=["rD"])
                S.op("act", lambda e: e.activation(out=F_[:, 0:T_], in_=F_[:, 0:T_], func=AF.Gelu), reads=["rF"], writes=["rF"])
                S.op("dve", lambda e: e.tensor_tensor(out=E_[:, 0:CT], in0=D_[:, 0:CT], in1=F_[:, 0:CT], op=ALU.mult), reads=["rD", "rF", "rE"], writes=["rE"])
                S.op("dve", lambda e: e.tensor_tensor(out=E_[:, CT:T_].rearrange("p (r w) -> p r w", w=64),
                                                      in0=D_[:, CT:T_].rearrange("p (w r) -> p r w", w=64),
                                                      in1=F_[:, CT:T_].rearrange("p (r w) -> p r w", w=64), op=ALU.mult),
                     reads=["rD", "rF", "rE"], writes=["rE"])
                S.dma("pool", YT.ap()[2048 + cc * 128:2048 + (cc + 1) * 128, :], E_[:, 0:T_], reads=["rE"], writes=["YT"])

        def ap3(t2d, off, dims, npart=128):
            return bass.AP(t2d.tensor, t2d.offset + off, [[t2d.ap[0][0], npart]] + [list(x) for x in dims])

        def phase_hgrn(l, es2):
            P2 = lambda name, shape: [alloc(es2, "%s%d" % (name, i), shape) for i in range(2)]
            ftm = P2("ftm", [32, 1024]); ktm = P2("ktm", [32, 1024]); gtm = P2("gtm", [32, 1024]); vtm = P2("vtm", [32, 1024])
            otl = P2("otl", [32, 1024]); qt = P2("qt", [128, 256]); ex = P2("ex", [128, 272]); ktT = P2("ktT", [128, 256])
            attm = P2("attm", [32, 256]); qfm = P2("qfm", [128, 1024])
            oml = alloc(es2, "oml", [128, 1024]); Sst = alloc(es2, "Sst", [128, 1024]); Sp = alloc(es2, "Sp", [128, 1024])
            id32 = cst[0:32, 128:160]
            items = []
            for d in range(2):
                for cn, ci in enumerate(chunk_order(d)):
                    for sn, sj in enumerate(range(4) if d == 0 else range(3, -1, -1)):
                        items.append((d, ci, sj, cn == 0 and sn == 0, sn == 0))
            cpar = [0]

            def stageA(it, p):
                d, ci, sj, first_of_dir, first_of_chunk = it
                RX = cst[0:32, 772:806] if d == 0 else cst[0:32, 806:840]
                TR = RX[:, 0:32]
                MK = cst[0:32, 256:288] if d == 0 else cst[0:32, 384:416]
                k = lambda nm: "%s%d" % (nm, p)
                if first_of_dir:
                    S.dma("sp", oml[:], LBD.ap()[l * 2 + d], reads=["LBD"], writes=["oml"])
                if first_of_chunk:
                    cpar[0] ^= 1
                    q = qfm[cpar[0]]; qk = "qfm%d" % cpar[0]
                    S.dma("pool", q[:].rearrange("p (h t) -> p h t", h=8),
                          PTfm.ap()[1024:2048, ci * 128:(ci + 1) * 128].rearrange("(h p) t -> p h t", p=128), reads=["PTfm"], writes=[qk])
                    S.op("act", lambda e: e.activation(out=q[:], in_=q[:], func=AF.Silu), reads=[qk], writes=[qk])
                q = qfm[cpar[0]]; qk = "qfm%d" % cpar[0]
                t0 = ci * 128 + sj * 32
                f_, k_, g_, v_, x_, qt_, kT_, at_ = ftm[p], ktm[p], gtm[p], vtm[p], ex[p], qt[p], ktT[p], attm[p]
                S.dma("sp", f_[:], Ptm.ap()[t0:t0 + 32, (3 + d) * 1024:(4 + d) * 1024], reads=["Ptm"], writes=[k("ftm")])
                S.dma("sp", v_[:], Ptm.ap()[t0:t0 + 32, 5 * 1024:6 * 1024], reads=["Ptm"], writes=[k("vtm")])
                S.op("act", lambda e: e.activation(out=f_[:], in_=f_[:], func=AF.Sigmoid, scale=-1.0), reads=[k("ftm")], writes=[k("ftm")])
                S.op("dve", lambda e: e.tensor_tensor(out=k_[:], in0=f_[:], in1=oml[0:32, :], op=ALU.mult), reads=[k("ftm"), "oml"], writes=[k("ktm")])
                S.op("dve", lambda e: e.tensor_scalar(out=g_[:], in0=k_[:], scalar1=-1.0, scalar2=1.0, op0=ALU.mult, op1=ALU.add),
                     reads=[k("ktm")], writes=[k("gtm")])
                S.op("act", lambda e: e.activation(out=g_[:], in_=g_[:], func=AF.Ln), reads=[k("gtm")], writes=[k("gtm")])
                for hf in range(2):
                    S.op("pe", lambda e, hf=hf: e.matmul(ps[hf][0:32, 0:512], lhsT=TR, rhs=g_[:, hf * 512:(hf + 1) * 512], start=True, stop=True),
                         reads=[k("gtm"), "cst"], writes=["ps%d" % hf])
                    S.op("act", lambda e, hf=hf: e.activation(out=f_[:, hf * 512:(hf + 1) * 512], in_=ps[hf][0:32, 0:512], func=AF.Exp, scale=-1.0),
                         reads=["ps%d" % hf], writes=[k("ftm")])
                S.op("dve", lambda e: e.tensor_tensor(out=k_[:], in0=k_[:], in1=f_[:], op=ALU.mult), reads=[k("ktm"), k("ftm")], writes=[k("ktm")])
                def mmrx(e):
                    for h in range(8):
                        r = e.matmul(ps[2][:, h * 34:(h + 1) * 34], lhsT=g_[:, h * 128:(h + 1) * 128], rhs=RX, start=True, stop=True)
                    return r
                S.op("pe", mmrx, reads=[k("gtm"), "cst"], writes=["ps2"])
                S.op("act", lambda e: e.activation(out=x_[:], in_=ps[2][:, 0:272], func=AF.Exp), reads=["ps2"], writes=[k("ex")])
                S.op("dve", lambda e: e.tensor_tensor(out=qt_[:].rearrange("p (h t) -> p h t", h=8),
                                                      in0=q[:].rearrange("p (h t) -> p h t", h=8)[:, :, sj * 32:(sj + 1) * 32],
                                                      in1=x_[:].rearrange("p (h t) -> p h t", h=8)[:, :, 0:32], op=ALU.mult),
                     reads=[qk, k("ex")], writes=[k("qt")])
                def mmtr(e):
                    for h in range(8):
                        r = e.matmul(ps[3][:, h * 32:(h + 1) * 32], lhsT=k_[:, h * 128:(h + 1) * 128], rhs=id32, start=True, stop=True)
                    return r
                S.op("pe", mmtr, reads=[k("ktm"), "cst"], writes=["ps3"])
                S.op("act", lambda e: e.activation(out=kT_[:], in_=ps[3][:, 0:256], func=AF.Copy), reads=["ps3"], writes=[k("ktT")])
                def mmat(e):
                    for h in range(8):
                        r = e.matmul(ps[4][0:32, h * 32:(h + 1) * 32], lhsT=kT_[:, h * 32:(h + 1) * 32], rhs=qt_[:, h * 32:(h + 1) * 32], start=True, stop=True)
                    return r
                S.op("pe", mmat, reads=[k("ktT"), k("qt")], writes=["ps4"])
                S.op("dve", lambda e: e.tensor_tensor(out=at_[:].rearrange("p (h t) -> p h t", h=8),
                                                      in0=ps[4][0:32, 0:256].rearrange("p (h t) -> p h t", h=8),
                                                      in1=ap3(MK, 0, [[0, 8], [1, 32]], npart=32), op=ALU.mult), reads=["ps4", "cst"], writes=[k("attm")])

            def stageB(it, p):
                d, ci, sj, first_of_dir, first_of_chunk = it
                k = lambda nm: "%s%d" % (nm, p)
                t0 = ci * 128 + sj * 32
                k_, v_, x_, qt_, at_, o_ = ktm[p], vtm[p], ex[p], qt[p], attm[p], otl[p]
                if first_of_dir:
                    S.op("pool", lambda e: e.memset(Sst[:], 0.0), writes=["Sst"])
                S.op("dve", lambda e: e.tensor_tensor(out=Sp[:].rearrange("p (h e) -> p h e", h=8),
                                                      in0=Sst[:].rearrange("p (h e) -> p h e", h=8),
                                                      in1=ap3(x_[:], 32, [[34, 8], [0, 128]]), op=ALU.mult), reads=["Sst", k("ex")], writes=["Sp"])
                def mmo(e):
                    for h in range(8):
                        pp = ps[5 + h // 4]; c0 = (h % 4) * 128
                        e.matmul(pp[0:32, c0:c0 + 128], lhsT=at_[:, h * 32:(h + 1) * 32], rhs=v_[:, h * 128:(h + 1) * 128], start=True, stop=False)
                        r = e.matmul(pp[0:32, c0:c0 + 128], lhsT=qt_[:, h * 32:(h + 1) * 32], rhs=Sp[:, h * 128:(h + 1) * 128], start=False, stop=True)
                    return r
                S.op("pe", mmo, reads=[k("attm"), k("vtm"), k("qt"), "Sp"], writes=["ps5", "ps6"])
                for hf in range(2):
                    S.op("act", lambda e, hf=hf: e.activation(out=o_[:, hf * 512:(hf + 1) * 512], in_=ps[5 + hf][0:32, 0:512], func=AF.Copy),
                         reads=["ps%d" % (5 + hf)], writes=[k("otl")])
                S.dma("pool", HAB.ap()[2 + d, t0:t0 + 32, :], o_[:], reads=[k("otl")], writes=["HAB"])
                for hf in range(2):
                    def mmu(e, hf=hf):
                        for h in range(hf * 4, hf * 4 + 4):
                            c0 = (h % 4) * 128
                            r = e.matmul(ps[7][:, c0:c0 + 128], lhsT=k_[:, h * 128:(h + 1) * 128], rhs=v_[:, h * 128:(h + 1) * 128], start=True, stop=True)
                        return r
                    S.op("pe", mmu, reads=[k("ktm"), k("vtm")], writes=["ps7"])
                    S.op("dve", lambda e, hf=hf: e.tensor_tensor(out=Sp[:, hf * 512:(hf + 1) * 512], in0=Sp[:, hf * 512:(hf + 1) * 512],
                                                                 in1=ps[7][:, 0:512], op=ALU.add), reads=["Sp", "ps7"], writes=["Sp"])
                S.op("dve", lambda e: e.tensor_tensor(out=Sst[:].rearrange("p (h e) -> p h e", h=8),
                                                      in0=Sp[:].rearrange("p (h e) -> p h e", h=8),
                                                      in1=ap3(x_[:], 33, [[34, 8], [0, 128]]), op=ALU.mult), reads=["Sp", k("ex")], writes=["Sst"])

            prev = None
            for n_, it in enumerate(items):
                stageA(it, n_ % 2)
                if prev is not None:
                    stageB(*prev)
                prev = (it, n_ % 2)
            stageB(*prev)

        def phase_mlstm(l, es2):
            P2 = lambda name, shape: [alloc(es2, "%s%d" % (name, i), shape) for i in range(2)]
            ktm = P2("aktm", [128, 1024]); vext = P2("vext", [128, 4 * 257]); qfm = P2("aqfm", [128, 1024]); gt = P2("gt", [128, 16])
            lf = P2("lf", [128, 4]); gi = P2("gi", [128, 4]); cbt = P2("cbt", [128, 4]); wl = P2("wl", [128, 4]); dec = P2("dec", [128, 4])
            otl = P2("aotl", [128, 1024])
            lfr = P2("lfr", [128, 128]); Dm = P2("Dm", [128, 128]); eB = P2("eB", [128, 128]); qt = P2("aqt", [128, 256]); kT = P2("kT", [128, 256])
            W = P2("W", [128, 128]); kh = P2("kh", [128, 256]); rec = P2("rec", [128, 1])
            bias = alloc(es2, "gbias", [128, L * 16]); Cx = alloc(es2, "Cx", [128, 8 * 257])
            S.dma("sp", bias[:], mgb.ap(), writes=["gbias"])
            for i in range(2):
                S.op("pool", lambda e, i=i: e.memset(vext[i][:], 1.0), writes=["vext%d" % i])
            items = []
            for d in range(2):
                for cn, ci in enumerate(chunk_order(d)):
                    for h in range(4):
                        items.append((d, ci, h, cn == 0 and h == 0))
            cpar = [0]

            def chunk_setup(d, ci, c):
                t0 = ci * 128
                kc = lambda nm: "%s%d" % (nm, c)
                S.dma("sp", ktm[c][:], Ptm.ap()[t0:t0 + 128, 0:1024], reads=["Ptm"], writes=[kc("aktm")])
                S.dma("sp", vext[c][:].rearrange("p (h e) -> p h e", h=4)[:, :, 0:256],
                      Ptm.ap()[t0:t0 + 128, 1024:2048].rearrange("t (h e) -> t h e", h=4), reads=["Ptm"], writes=[kc("vext")])
                S.dma("sp", gt[c][:], Ptm.ap()[t0:t0 + 128, NTMB * 256:NTMB * 256 + 16], reads=["Ptm"], writes=[kc("gt")])
                S.dma("pool", qfm[c][:].rearrange("p (h t) -> p h t", h=8),
                      PTfm.ap()[0:1024, t0:t0 + 128].rearrange("(h p) t -> p h t", p=128), reads=["PTfm"], writes=[kc("aqfm")])
                g_, lf_, gi_, cb_, wl_, dc_ = gt[c], lf[c], gi[c], cbt[c], wl[c], dec[c]
                TRI = cst[:, 256:384] if d == 0 else cst[:, 384:512]
                S.op("dve", lambda e: e.tensor_tensor(out=g_[:], in0=g_[:], in1=bias[:, l * 16:(l + 1) * 16], op=ALU.add), reads=[kc("gt"), "gbias"], writes=[kc("gt")])
                S.op("dve", lambda e: e.tensor_copy(out=gi_[:], in_=g_[:, d * 8:d * 8 + 4]), reads=[kc("gt")], writes=[kc("gi")])
                S.op("act", lambda e: e.activation(out=lf_[:], in_=g_[:, d * 8 + 4:d * 8 + 8], func=AF.Exp, scale=-1.0), reads=[kc("gt")], writes=[kc("lf")])
                S.op("dve", lambda e: e.tensor_scalar(out=lf_[:], in0=lf_[:], scalar1=1.0, scalar2=None, op0=ALU.add), reads=[kc("lf")], writes=[kc("lf")])
                S.op("act", lambda e: e.activation(out=lf_[:], in_=lf_[:], func=AF.Ln), reads=[kc("lf")], writes=[kc("lf")])
                S.op("dve", lambda e: e.tensor_scalar(out=lf_[:], in0=lf_[:], scalar1=-1.0, scalar2=None, op0=ALU.mult), reads=[kc("lf")], writes=[kc("lf")])
                def mmb(e):
                    e.matmul(ps[0][:, 0:4], lhsT=TRI, rhs=lf_[:], start=True, stop=True)
                    return e.matmul(ps[0][:, 4:8], lhsT=ones, rhs=lf_[:], start=True, stop=True)
                S.op("pe", mmb, reads=[kc("lf"), "cst"], writes=["ps0"])
                S.op("dve", lambda e: e.tensor_tensor(out=cb_[:], in0=gi_[:], in1=ps[0][:, 0:4], op=ALU.subtract), reads=[kc("gi"), "ps0"], writes=[kc("cbt")])
                S.op("dve", lambda e: e.tensor_tensor(out=wl_[:], in0=cb_[:], in1=ps[0][:, 4:8], op=ALU.add), reads=[kc("cbt"), "ps0"], writes=[kc("wl")])
                S.op("act", lambda e: e.activation(out=wl_[:], in_=wl_[:], func=AF.Exp), reads=[kc("wl")], writes=[kc("wl")])
                S.op("dve", lambda e: e.tensor_scalar(out=wl_[:], in0=wl_[:], scalar1=1.0 / 16.0, scalar2=None, op0=ALU.mult), reads=[kc("wl")], writes=[kc("wl")])
                S.op("act", lambda e: e.activation(out=dc_[:], in_=ps[0][:, 4:8], func=AF.Exp), reads=["ps0"], writes=[kc("dec")])

            def stageA(it, p):
                d, ci, h, first_of_dir = it
                if h == 0:
                    cpar[0] ^= 1
                    chunk_setup(d, ci, cpar[0])
                c = cpar[0]
                kc = lambda nm: "%s%d" % (nm, c)
                k = lambda nm: "%s%d" % (nm, p)
                TRI = cst[:, 256:384] if d == 0 else cst[:, 384:512]
                k_, q_, lf_, cb_, wl_ = ktm[c], qfm[c], lf[c], cbt[c], wl[c]
                lfr_, Dm_, eB_, qt_, kT_, W_, kh_ = lfr[p], Dm[p], eB[p], qt[p], kT[p], W[p], kh[p]
                S.op("dve", lambda e: e.tensor_scalar(out=lfr_[:], in0=ones, scalar1=lf_[:, h:h + 1], scalar2=None, op0=ALU.mult),
                     reads=[kc("lf"), "cst"], writes=[k("lfr")])
                S.op("pe", lambda e: e.matmul(ps[1][:, 0:128], lhsT=lfr_[:], rhs=TRI, start=True, stop=True), reads=[k("lfr"), "cst"], writes=["ps1"])
                S.op("act", lambda e: e.activation(out=Dm_[:], in_=ps[1][:, 0:128], func=AF.Exp, bias=cb_[:, h:h + 1]),
                     reads=["ps1", kc("cbt")], writes=[k("Dm")])
                S.op("dve", lambda e: e.tensor_tensor(out=Dm_[:], in0=Dm_[:], in1=TRI, op=ALU.mult), reads=[k("Dm"), "cst"], writes=[k("Dm")])
                S.op("act", lambda e: e.activation(out=eB_[:], in_=ps[1][:, 0:128], func=AF.Exp), reads=["ps1"], writes=[k("eB")])
                for dc in range(2):
                    qs = slice((h * 2 + dc) * 128, (h * 2 + dc + 1) * 128)
                    S.op("dve", lambda e, dc=dc, qs=qs: e.tensor_tensor(out=qt_[:, dc * 128:(dc + 1) * 128], in0=q_[:, qs], in1=eB_[:], op=ALU.mult),
                         reads=[kc("aqfm"), k("eB")], writes=[k("aqt")])
                    S.op("pe", lambda e, dc=dc: e.matmul(ps[2 + dc][:, 0:128], lhsT=k_[:, h * 256 + dc * 128:h * 256 + (dc + 1) * 128], rhs=ident,
                                                         start=True, stop=True), reads=[kc("aktm"), "cst"], writes=["ps%d" % (2 + dc)])
                    S.op("act", lambda e, dc=dc: e.activation(out=kT_[:, dc * 128:(dc + 1) * 128], in_=ps[2 + dc][:, 0:128], func=AF.Copy),
                         reads=["ps%d" % (2 + dc)], writes=[k("kT")])
                def mms(e):
                    e.matmul(ps[4][:, 0:128], lhsT=kT_[:, 0:128], rhs=q_[:, (h * 2) * 128:(h * 2 + 1) * 128], start=True, stop=False)
                    return e.matmul(ps[4][:, 0:128], lhsT=kT_[:, 128:256], rhs=q_[:, (h * 2 + 1) * 128:(h * 2 + 2) * 128], start=False, stop=True)
                S.op("pe", mms, reads=[k("kT"), kc("aqfm")], writes=["ps4"])
                S.op("dve", lambda e: e.scalar_tensor_tensor(out=W_[:], in0=ps[4][:, 0:128], scalar=1.0 / 16.0, in1=Dm_[:], op0=ALU.mult, op1=ALU.mult),
                     reads=["ps4", k("Dm")], writes=[k("W")])
                S.op("dve", lambda e: e.tensor_scalar(out=kh_[:], in0=k_[:, h * 256:(h + 1) * 256], scalar1=wl_[:, h:h + 1], scalar2=None,
                                                      op0=ALU.mult), reads=[kc("aktm"), kc("wl")], writes=[k("kh")])
                return c

            def stageB(it, p, c):
                d, ci, h, first_of_dir = it
                kc = lambda nm: "%s%d" % (nm, c)
                k = lambda nm: "%s%d" % (nm, p)
                v_, dc_, o_ = vext[c], dec[c], otl[c]
                qt_, W_, kh_, rec_ = qt[p], W[p], kh[p], rec[p]
                if first_of_dir:
                    S.op("pool", lambda e: e.memset(Cx[:], 0.0), writes=["Cx"])
                def mmn(e):
                    e.matmul(ps[5][:, 0:257], lhsT=W_[:], rhs=v_[:, h * 257:(h + 1) * 257], start=True, stop=False)
                    e.matmul(ps[5][:, 0:257], lhsT=qt_[:, 0:128], rhs=Cx[:, (h * 2) * 257:(h * 2 + 1) * 257], start=False, stop=False)
                    return e.matmul(ps[5][:, 0:257], lhsT=qt_[:, 128:256], rhs=Cx[:, (h * 2 + 1) * 257:(h * 2 + 2) * 257], start=False, stop=True)
                S.op("pe", mmn, reads=[k("W"), kc("vext"), k("aqt"), "Cx"], writes=["ps5"])
                S.op("act", lambda e: e.activation(out=rec_[:], in_=ps[5][:, 256:257], func=AF.Abs), reads=["ps5"], writes=[k("rec")])
                S.op("dve", lambda e: e.tensor_scalar(out=rec_[:], in0=rec_[:], scalar1=1.0, scalar2=None, op0=ALU.max), reads=[k("rec")], writes=[k("rec")])
                S.op("dve", lambda e: e.reciprocal(out=rec_[:], in_=rec_[:]), reads=[k("rec")], writes=[k("rec")])
                S.op("dve", lambda e: e.tensor_scalar(out=o_[:, h * 256:(h + 1) * 256], in0=ps[5][:, 0:256], scalar1=rec_[:, 0:1], scalar2=None,
                                                      op0=ALU.mult), reads=["ps5", k("rec")], writes=[kc("aotl")])
                for dc in range(2):
                    S.op("pe", lambda e, dc=dc: e.matmul(ps[6 + dc][:, 0:257], lhsT=kh_[:, dc * 128:(dc + 1) * 128], rhs=v_[:, h * 257:(h + 1) * 257],
                                                         start=True, stop=True), reads=[k("kh"), kc("vext")], writes=["ps%d" % (6 + dc)])
                    cs = slice((h * 2 + dc) * 257, (h * 2 + dc + 1) * 257)
                    S.op("dve", lambda e, dc=dc, cs=cs: e.scalar_tensor_tensor(out=Cx[:, cs], in0=Cx[:, cs], scalar=dc_[:, h:h + 1],
                                                                               in1=ps[6 + dc][:, 0:257], op0=ALU.mult, op1=ALU.add),
                         reads=["Cx", kc("dec"), "ps%d" % (6 + dc)], writes=["Cx"])
                if h == 3:
                    S.dma("pool", HAB.ap()[d, ci * 128:(ci + 1) * 128, :], o_[:], reads=[kc("aotl")], writes=["HAB"])

            prev = None
            for n_, it in enumerate(items):
                c = stageA(it, n_ % 2)
                if prev is not None:
                    stageB(*prev)
                prev = (it, n_ % 2, c)
            stageB(*prev)

        def phase_combine(l, es2):
            ha = alloc(es2, "ha", [128, 1024]); hb = alloc(es2, "hb", [128, 1024]); og = alloc(es2, "og", [128, 1024]); sq = alloc(es2, "sq", [128, 1024])
            ss = alloc(es2, "ss", [128, 8]); yo = [alloc(es2, "yo%d" % i, [128, 128]) for i in range(2)]
            for br, (nh, ogblk, fn) in enumerate(((4, 2, AF.Sigmoid), (8, 6, AF.Silu))):
                hd = 1024 // nh
                for ti in range(cfg.T // 128):
                    t0 = ti * 128
                    S.dma("sp", ha[:], HAB.ap()[br * 2, t0:t0 + 128, :], reads=["HAB"], writes=["ha"])
                    S.dma("sp", hb[:], HAB.ap()[br * 2 + 1, t0:t0 + 128, :], reads=["HAB"], writes=["hb"])
                    S.dma("pool", og[:], Ptm.ap()[t0:t0 + 128, ogblk * 1024:(ogblk + 1) * 1024], reads=["Ptm"], writes=["og"])
                    S.op("dve", lambda e: e.tensor_tensor(out=ha[:], in0=ha[:], in1=hb[:], op=ALU.add), reads=["ha", "hb"], writes=["ha"])
                    S.op("pool", lambda e: e.tensor_tensor(out=sq[:], in0=ha[:], in1=ha[:], op=ALU.mult), reads=["ha"], writes=["sq"])
                    S.op("dve", lambda e, nh=nh: e.tensor_reduce(out=ss[:, 0:nh], in_=sq[:].rearrange("p (h e) -> p h e", h=nh),
                                                                 axis=mybir.AxisListType.X, op=ALU.add), reads=["sq"], writes=["ss"])
                    S.op("dve", lambda e, nh=nh, hd=hd: e.tensor_scalar(out=ss[:, 0:nh], in0=ss[:, 0:nh], scalar1=1.0 / hd, scalar2=EPS,
                                                                        op0=ALU.mult, op1=ALU.add), reads=["ss"], writes=["ss"])
                    S.op("act", lambda e, nh=nh: e.activation(out=ss[:, 0:nh], in_=ss[:, 0:nh], func=AF.Sqrt), reads=["ss"], writes=["ss"])
                    S.op("dve", lambda e, nh=nh: e.reciprocal(out=ss[:, 0:nh], in_=ss[:, 0:nh]), reads=["ss"], writes=["ss"])
                    S.op("act", lambda e, fn=fn: e.activation(out=og[:], in_=og[:], func=fn), reads=["og"], writes=["og"])
                    for h in range(nh):
                        S.op("dve", lambda e, h=h, hd=hd: e.scalar_tensor_tensor(out=ha[:, h * hd:(h + 1) * hd], in0=ha[:, h * hd:(h + 1) * hd],
                                                                                scalar=ss[:, h:h + 1], in1=og[:, h * hd:(h + 1) * hd],
                                                                                op0=ALU.mult, op1=ALU.mult), reads=["ha", "ss", "og"], writes=["ha"])
                    for fc in range(8):
                        p = ps[fc % 2]; pk = "ps%d" % (fc % 2); y = yo[fc % 2]; yk = "yo%d" % (fc % 2)
                        S.op("pe", lambda e, p=p, fc=fc: e.matmul(p[:, 0:128], lhsT=ha[:, fc * 128:(fc + 1) * 128], rhs=ident, start=True, stop=True),
                             reads=["ha", "cst"], writes=[pk])
                        S.op("act", lambda e, p=p, y=y: e.activation(out=y[:], in_=p[:, 0:128], func=AF.Copy), reads=[pk], writes=[yk])
                        S.dma("pool", YT.ap()[br * 1024 + fc * 128:br * 1024 + (fc + 1) * 128, t0:t0 + 128], y[:], reads=[yk], writes=["YT"])

        def phase_merge(l, es2, skip_ctx):
            yt = alloc(es2, "yt", [128, 24 * G]); mT = alloc(es2, "mT", [128, KC * G])
            hr = [alloc(es2, "hrm%d" % i, [128, G]) for i in range(2)]
            mg = [alloc(es2, "mg%d" % i, [128, 3 * G]) for i in range(2)]
            wb = [alloc(es2, "wbm%d" % i, [128, 3 * 1024]) for i in range(2)]
            ho = [alloc(es2, "hom%d" % i, [128, G]) for i in range(2)]
            for (t0, n, st) in cfg.groups:
                if skip_ctx and st == 1:
                    continue
                S.dma("sp", RRm(yt[:, 0:24 * n]).rearrange("p (c t) -> p c t", c=24), RRm(YT.ap()[:, t0:t0 + n]).rearrange("(c p) t -> p c t", p=128),
                      reads=["YT"], writes=["yt"])
                for dc in range(KC):
                    b = wb[dc % 2]; bk = "wbm%d" % (dc % 2); g = mg[dc % 2]; gk = "mg%d" % (dc % 2)
                    for br in range(3):
                        S.dma("sp" if br != 1 else "act", RRm(b[:, br * 1024:(br + 1) * 1024]), RRm(wbr.ap()[l, br, dc]), writes=[bk + str(br)])
                        S.dma("pool", g[:, br * n:(br + 1) * n], PTfm.ap()[(32 + br * 16 + dc) * 128:(33 + br * 16 + dc) * 128, t0:t0 + n],
                              reads=["PTfm"], writes=[gk + str(br)])
                    S.op("act", lambda e, g=g: e.activation(out=g[:, 0:3 * n], in_=g[:, 0:3 * n], func=AF.Sigmoid),
                         reads=[gk + str(i) for i in range(3)], writes=[gk + str(i) for i in range(3)])
                    for br in range(3):
                        p = ps[br]; pk = "ps%d" % br
                        def mm(e, b=b, p=p, br=br):
                            for fc in range(8):
                                r = e.matmul(p[:, 0:n], lhsT=RRm(b[:, br * 1024 + fc * 128:br * 1024 + (fc + 1) * 128]),
                                             rhs=RRm(yt[:, (br * 8 + fc) * n:(br * 8 + fc + 1) * n]), start=(fc == 0), stop=(fc == 7))
                            return r
                        S.op("pe", mm, reads=[bk + str(br), "yt"], writes=[pk])
                        if br == 0:
                            S.op("dve", lambda e, p=p, g=g, dc=dc: e.tensor_tensor(out=RRm(mT[:, dc * n:(dc + 1) * n]), in0=p[:, 0:n], in1=g[:, 0:n], op=ALU.mult),
                                 reads=[pk, gk + "0"], writes=["mT%d" % dc])
                        else:
                            S.op("dve", lambda e, p=p, g=g, br=br: e.tensor_tensor(out=g[:, br * n:(br + 1) * n], in0=p[:, 0:n], in1=g[:, br * n:(br + 1) * n], op=ALU.mult),
                                 reads=[pk, gk + str(br)], writes=[gk + str(br)])
                            S.op("dve", lambda e, g=g, br=br, dc=dc: e.tensor_tensor(out=RRm(mT[:, dc * n:(dc + 1) * n]), in0=mT[:, dc * n:(dc + 1) * n],
                                                                                 in1=g[:, br * n:(br + 1) * n], op=ALU.add),
                                 reads=[gk + str(br), "mT%d" % dc], writes=["mT%d" % dc])
                mk_ = ["mT%d" % dc for dc in range(KC)]
                for fo in range(KC):
                    b = wb[fo % 2]; bk = "wbm%d" % (fo % 2)
                    S.dma("sp", RRm(b[:, 0:1024]), RRm(wo.ap()[l, fo][:, 0:1024]), writes=[bk + "0"])
                    S.dma("act", RRm(b[:, 1024:2048]), RRm(wo.ap()[l, fo][:, 1024:2048]), writes=[bk + "1"])
                    p = ps[4 + fo % 2]; pk = "ps%d" % (4 + fo % 2)
                    def mm2(e, b=b, p=p):
                        for dc in range(KC):
                            r = e.matmul(p[:, 0:n], lhsT=RRm(b[:, dc * 128:(dc + 1) * 128]), rhs=RRm(mT[:, dc * n:(dc + 1) * n]), start=(dc == 0), stop=(dc == KC - 1))
                        return r
                    S.op("pe", mm2, reads=[bk + "0", bk + "1"] + mk_, writes=[pk])
                    hot = ho[fo % 2]; hk = "hom%d" % (fo % 2)
                    hrt = hr[fo % 2]; rk = "hrm%d" % (fo % 2)
                    S.dma("pool", hrt[:, 0:n], hT.ap()[fo * 128:(fo + 1) * 128, t0:t0 + n], writes=[rk])
                    S.op("dve", lambda e, hot=hot, p=p, fo=fo, hrt=hrt: e.scalar_tensor_tensor(
                        out=hot[:, 0:n], in0=p[:, 0:n], scalar=col(ABv(1, st, 2), fo), in1=hrt[:, 0:n],
                        op0=ALU.mult, op1=ALU.add), reads=[pk, rk, "AB"], writes=[hk])
                    S.dma("pool", hT.ap()[fo * 128:(fo + 1) * 128, t0:t0 + n], hot[:, 0:n], reads=[hk], writes=["hT"])

        def phase_final(es2):
            hx = alloc(es2, "hx", [128, KC * G])
            xn = alloc(es2, "xn", [128, KC * G])
            rstd = alloc(es2, "rstd", [128, G])
            for (t0, n, st) in cfg.groups:
                if st == 1:
                    continue
                load_h(hx, hT, t0, n)
                S.op("act", lambda e: e.activation(out=xn[:, 0:KC * n], in_=hx[:, 0:KC * n], func=AF.Square),
                     reads=["hx"], writes=["xn"])
                def mm(e):
                    for kc in range(KC):
                        r = e.matmul(ps[7][:, 0:n], lhsT=ones, rhs=xn[:, kc * n:(kc + 1) * n], start=(kc == 0), stop=(kc == KC - 1))
                    return r
                S.op("pe", mm, reads=["xn", "cst"], writes=["ps7"])
                S.op("dve", lambda e: e.tensor_scalar(out=rstd[:, 0:n], in0=ps[7][:, 0:n], scalar1=1.0 / D, scalar2=EPS,
                                                      op0=ALU.mult, op1=ALU.add), reads=["ps7"], writes=["rstd"])
                S.op("act", lambda e: e.activation(out=rstd[:, 0:n], in_=rstd[:, 0:n], func=AF.Sqrt), reads=["rstd"], writes=["rstd"])
                S.op("dve", lambda e: e.reciprocal(out=rstd[:, 0:n], in_=rstd[:, 0:n]), reads=["rstd"], writes=["rstd"])
                for kc in range(KC):
                    S.op("dve", lambda e, kc=kc: e.scalar_tensor_tensor(
                        out=xn[:, kc * n:(kc + 1) * n], in0=hx[:, kc * n:(kc + 1) * n], scalar=col(fgt, kc),
                        in1=rstd[:, 0:n], op0=ALU.mult, op1=ALU.mult), reads=["hx", "rstd", "fgt", "xn"], writes=["xn"])
                S.dma("pool", out.ap()[:, t0 - cfg.CTX:t0 - cfg.CTX + n].rearrange("(k p) t -> p k t", p=128),
                      xn[:, 0:KC * n].rearrange("p (k t) -> p k t", k=KC), reads=["xn"], writes=["out"])

        with ExitStack() as es2:
            phase_lb(es2)
            S.barrier()
        for l in range(L):
            last = (l == L - 1)
            with ExitStack() as es2:
                phase_mod(l, es2)
                S.barrier()
            with ExitStack() as es2:
                phase_ffn(l, 0, 0, xT if l == 0 else hT, es2)
                S.barrier()
            if stop_after == "ffn1":
                break
            with ExitStack() as es2:
                phase_proj(l, es2)
                S.barrier()
            if stop_after == "proj":
                break
            for ph in (phase_rglru, phase_hgrn, phase_mlstm, phase_combine):
                with ExitStack() as es2:
                    ph(l, es2)
                    S.barrier()
            with ExitStack() as es2:
                phase_merge(l, es2, last)
                S.barrier()
            with ExitStack() as es2:
                phase_ffn(l, 1, 2, hT, es2, skip_ctx=last)
                S.barrier()
        with ExitStack() as es2:
            phase_final(es2)
            S.barrier()
        print("instructions emitted:", S.ninst)
    return nc


def wblocks(w, ncols=128):
    K, N = w.shape
    return np.ascontiguousarray(w.reshape(K // 128, 128, N // ncols, ncols).transpose(2, 1, 0, 3)).reshape(N // ncols, 128, (K // 128) * ncols)


def pvec(v):
    v = np.asarray(v, np.float32)
    lead = v.shape[:-1]
    r = v.reshape(*lead, v.shape[-1] // 128, 128)
    r = np.moveaxis(r, -1, 0)
    return np.ascontiguousarray(r).reshape(128, -1)


def make_consts():
    c = np.zeros((128, 1024), np.float32)
    c[:, 0:128] = 1.0
    c[:, 128:256] = np.eye(128, dtype=np.float32)
    i = np.arange(128)
    triU = (i[:, None] <= i[None, :]).astype(np.float32)
    triL = (i[:, None] >= i[None, :]).astype(np.float32)
    c[:, 256:384] = triU
    c[:, 384:512] = triL
    lo = (i <= 63).astype(np.float32)[:, None]
    hi = (i >= 64).astype(np.float32)[:, None]
    c[:, 512:640] = triU - lo
    c[:, 640:641] = lo
    c[:, 641:642] = hi
    c[:, 642:770] = triL - hi
    c[:, 770:771] = hi
    c[:, 771:772] = lo
    j = np.arange(32)
    tU = (j[:, None] <= j[None, :]).astype(np.float32)
    tL = (j[:, None] >= j[None, :]).astype(np.float32)
    lo32 = (j <= 15).astype(np.float32)[:, None]
    hi32 = (j >= 16).astype(np.float32)[:, None]
    c[0:32, 772:804] = tU - lo32
    c[0:32, 804:805] = lo32
    c[0:32, 805:806] = hi32
    c[0:32, 806:838] = tL - hi32
    c[0:32, 838:839] = hi32
    c[0:32, 839:840] = lo32
    return c


def prep_shared(inp, cfg):
    L = cfg.DEPTH
    sh = {}
    sh["modw"] = np.stack([wblocks(inp["mod_w"][l]) for l in range(L)])
    sh["modb"] = pvec(inp["mod_b"][:L])
    sh["normg"] = pvec(inp["norm_g"][:L])
    sh["w1t"] = np.stack([np.stack([wblocks(inp["ffn_w_in"][l, w]) for w in range(2)]) for l in range(L)])
    sh["w2t"] = np.stack([np.stack([wblocks(inp["ffn_w_out"][l, w]) for w in range(2)]) for l in range(L)])
    sh["fing"] = pvec(inp["final_g"])
    win = inp["w_in"]
    sh["wpf"] = np.stack([np.concatenate([wblocks(win[l][:, o:o + w]) for (o, w) in FM_COLS]) for l in range(L)])
    sh["wpt"] = np.stack([np.concatenate([wblocks(win[l][:, o:o + w], 256) for (o, w) in TM_COLS]) for l in range(L)])
    sh["wpg"] = np.stack([wblocks(win[l][:, 4096:4112], 16)[0] for l in range(L)])
    sh["consts"] = make_consts()
    sh["convw"] = pvec(inp["conv_w"][:L])
    sh["convb"] = pvec(inp["conv_b"][:L])
    sh["rgb"] = pvec(inp["rg_gate_b"][:L])
    sh["rglam"] = pvec(inp["rg_lambda"][:L])
    rg = inp["rg_gate_w"][:L].reshape(L, 4, 8, 2, 64, 64)
    bd = np.zeros((L, 4, 8, 128, 128), np.float32)
    bd[:, :, :, 0:64, 0:64] = rg[:, :, :, 0]
    bd[:, :, :, 64:128, 64:128] = rg[:, :, :, 1]
    sh["rgw"] = bd
    sh["lblog"] = np.ascontiguousarray(np.broadcast_to(inp["hgrn_lb_logits"][:L].reshape(1, -1), (128, L * 2 * 1024)))
    sh["mgb"] = np.ascontiguousarray(np.broadcast_to(inp["mlstm_gate_b"][:L].reshape(1, -1), (128, L * 16)))
    sh["wbr"] = np.stack([np.stack([wblocks(inp["branch_w"][l, br]) for br in range(3)]) for l in range(L)])
    sh["wo"] = np.stack([wblocks(inp["w_out"][l]) for l in range(L)])
    return sh


def prep_batch(inp, cfg, b):
    m = {}
    m["xT"] = np.ascontiguousarray(np.concatenate([inp["ctx"][b].T, inp["x"][b].T], axis=1))
    cv = np.stack([inp["c"][b], inp["c_ctx"]], axis=-1)
    m["cvec"] = np.ascontiguousarray(cv.reshape(KC, 128, 2).transpose(1, 0, 2)).reshape(128, KC * 2)
    return m


def kernel(**inputs):
    inp = {k: np.asarray(v) for k, v in inputs.items()}
    cfg = Cfg()
    nc = build(cfg)
    sh = prep_shared(inp, cfg)
    in_maps = []
    for b in range(8):
        m = dict(sh)
        m.update(prep_batch(inp, cfg, b))
        in_maps.append(m)
    res = run_bass_kernel_spmd(nc, in_maps, core_ids=list(range(8)))
    out = np.stack([np.ascontiguousarray(res.results[b]["out"].T) for b in range(8)])
    return out.astype(np.float32)
```

```python
import numpy as np
from contextlib import ExitStack
import concourse.bass as bass
import concourse.mybir as mybir
from concourse.bass_utils import run_bass_kernel_spmd

F32 = mybir.dt.float32
AF = mybir.ActivationFunctionType
ALU = mybir.AluOpType

D = 2048
KC = D // 128
DFF = 5632
FC = DFF // 128
NMOD = 9
EPS = 1e-6
INW = 17424
NFMC = 80
NTMB = 28
NTM = NTMB * 256 + 16
FM_COLS = [(0, 1024), (4112, 1024), (9232, 1024), (10256, 1024), (11280, 6144)]
TM_COLS = [(1024, 1024), (2048, 1024), (3072, 1024), (5136, 1024), (6160, 1024), (7184, 1024), (8208, 1024)]


F32R_ = mybir.dt.float32r


class Sched:
    def __init__(self, nc, es, n_dma_sems=6):
        self.nc = nc
        self.eng = {"pe": nc.tensor, "act": nc.scalar, "dve": nc.vector, "pool": nc.gpsimd, "sp": nc.sync}
        self.semh = {}
        self.cnt = {}
        for e in self.eng:
            self.semh[e] = es.enter_context(nc.semaphore("s_" + e))
            self.cnt[e] = 0
        self.dq = {}
        for q in ("sp", "pool", "act"):
            lst = []
            for i in range(n_dma_sems):
                k = "d_%s%d" % (q, i)
                self.semh[k] = es.enter_context(nc.semaphore(k))
                self.cnt[k] = 0
                lst.append(k)
            self.dq[q] = [lst, 0]
        self.known = {e: {} for e in self.eng}
        self.state = {}
        self.ninst = 0

    IGN = frozenset(["PTfm", "Ptm", "HAB", "YT", "hT", "LBD", "out"])

    def _deps(self, reads, writes):
        reads = [k for k in reads if k not in self.IGN]
        writes = [k for k in writes if k not in self.IGN]
        deps = []
        for k in reads:
            st = self.state.get(k)
            if st and st[0]:
                deps.append(st[0])
        for k in writes:
            st = self.state.get(k)
            if st:
                if st[0]:
                    deps.append(st[0])
                deps.extend(st[1].items())
        return deps

    def _wait(self, e, deps):
        kn = self.known[e]
        for (sk, val) in deps:
            if kn.get(sk, 0) < val:
                self.eng[e].wait_ge(self.semh[sk], val)
                kn[sk] = val

    def _commit(self, tok, reads, writes):
        reads = [k for k in reads if k not in self.IGN]
        writes = [k for k in writes if k not in self.IGN]
        for k in writes:
            self.state[k] = [tok, {}]
        for k in reads:
            if k in writes:
                continue
            st = self.state.setdefault(k, [None, {}])
            if st[1].get(tok[0], 0) < tok[1]:
                st[1][tok[0]] = tok[1]

    def op(self, e, emit, reads=(), writes=()):
        self._wait(e, self._deps(reads, writes))
        ins = emit(self.eng[e])
        self.cnt[e] += 1
        ins.then_inc(self.semh[e], 1)
        self._commit((e, self.cnt[e]), reads, writes)
        self.ninst += 1

    def dma(self, q, out, in_, reads=(), writes=()):
        lst, rr = self.dq[q]
        sk = lst[rr % len(lst)]
        self.dq[q][1] = rr + 1
        deps = self._deps(reads, writes)
        if self.cnt[sk] > 0:
            deps.append((sk, self.cnt[sk]))
        self._wait(q, deps)
        if out.dtype == F32R_:
            self.nc.dge_precook = False
            ins = self.eng[q].dma_start(out=out, in_=in_)
            self.nc.dge_precook = True
        else:
            ins = self.eng[q].dma_start(out=out, in_=in_)
        self.cnt[sk] += 16
        ins.then_inc(self.semh[sk], 16)
        self._commit((sk, self.cnt[sk]), reads, writes)
        self.ninst += 1

    def barrier(self):
        toks = [(k, v) for k, v in self.cnt.items() if v > 0]
        for e in self.eng:
            self._wait(e, toks)
        self.state = {}

    def finish(self, keys):
        self.barrier()


class Cfg:
    def __init__(self, SEQ=4096, CTX=256, DEPTH=4, G=512):
        self.SEQ, self.CTX, self.DEPTH, self.G = SEQ, CTX, DEPTH, G
        self.T = SEQ + CTX
        self.rows = SEQ // 64
        self.groups = [(0, CTX, 1)] + [(CTX + i * G, G, 0) for i in range(SEQ // G)]
        self.G2 = min(1024, SEQ) if SEQ % 1024 == 0 or SEQ < 1024 else G
        self.groups2 = [(0, CTX, 1)] + [(CTX + i * self.G2, self.G2, 0) for i in range(SEQ // self.G2)]


F32R = mybir.dt.float32r


def RR(ap):
    return ap.bitcast(F32R)


import os
FLAGS = {"proj": os.environ.get("RR_PROJ", "1") == "1", "merge": os.environ.get("RR_MERGE", "1") == "1"}


def RRp(ap):
    return RR(ap) if FLAGS["proj"] else ap


def RRm(ap):
    return RR(ap) if FLAGS["merge"] else ap


def col(t, i):
    return t[:, i:i + 1]


def build(cfg, stop_after=None):
    nc = bass.Bass("TRN2", target_bir_lowering=False)
    L, T, G = cfg.DEPTH, cfg.T, cfg.G
    dt = nc.dram_tensor
    I = {}
    def inp(name, shape):
        I[name] = dt(name, list(shape), F32, kind="ExternalInput")
        return I[name]
    xT = inp("xT", [D, T])
    cvec = inp("cvec", [128, KC * 2])
    modw = inp("modw", [L, 144, 128, KC * 128])
    modb = inp("modb", [128, L * 144])
    normg = inp("normg", [128, L * 3 * KC])
    w1t = inp("w1t", [L, 2, 88, 128, KC * 128])
    w2t = inp("w2t", [L, 2, KC, 128, FC * 128])
    fing = inp("fing", [128, KC])
    wpf = inp("wpf", [L, NFMC, 128, KC * 128])
    wpt = inp("wpt", [L, NTMB, 128, KC * 256])
    wpg = inp("wpg", [L, 128, KC * 16])
    convw = inp("convw", [128, L * 4 * 8])
    convb = inp("convb", [128, L * 8])
    rgb = inp("rgb", [128, L * 4 * 8])
    rglam = inp("rglam", [128, L * 2 * 8])
    rgw = inp("rgw", [L, 4, 8, 128, 128])
    lblog = inp("lblog", [128, L * 2 * 1024])
    mgb = inp("mgb", [128, L * 16])
    wbr = inp("wbr", [L, 3, KC, 128, 8 * 128])
    wo = inp("wo", [L, KC, 128, KC * 128])
    YT = dt("YT", [3 * 1024, T], F32, kind="Internal")
    HAB = dt("HAB", [4, T, 1024], F32, kind="Internal")
    LBD = dt("LBD", [L * 2, 128, 1024], F32, kind="Internal")
    PTfm = dt("PTfm", [NFMC * 128, T], F32, kind="Internal")
    Ptm = dt("Ptm", [T, NTM], F32, kind="Internal")
    consts = inp("consts", [128, 1024])
    out = dt("out", [D, cfg.SEQ], F32, kind="ExternalOutput")
    hT = dt("hT", [D, T], F32, kind="Internal")

    es = ExitStack()
    with es:
        S = Sched(nc, es)
        uid = [0]
        def alloc(stack, name, shape):
            uid[0] += 1
            return stack.enter_context(nc.sbuf_tensor("%s_%d" % (name, uid[0]), list(shape), F32))
        sb = lambda name, shape: alloc(es, name, shape)
        cst = sb("cst", [128, 1024])
        ones = cst[:, 0:128]
        ident = cst[:, 128:256]
        sv = sb("sv", [128, KC * 2])
        modt = sb("modt", [128, 144 * 2])
        modbt = sb("modbt", [128, L * 144])
        ngt = sb("ngt", [128, L * 3 * KC])
        fgt = sb("fgt", [128, KC])
        AB = sb("AB", [128, 3 * 2 * 3 * KC])
        ps = [es.enter_context(nc.psum_tensor("ps%d" % i, [128, 512], F32)) for i in range(8)]

        S.dma("sp", RR(cst[:]), RR(consts.ap()), writes=["cst"])
        S.dma("sp", sv[:], cvec.ap(), writes=["sv"])
        S.dma("sp", modbt[:], modb.ap(), writes=["modbt"])
        S.dma("sp", ngt[:], normg.ap(), writes=["ngt"])
        S.dma("sp", fgt[:], fing.ap(), writes=["fgt"])
        S.op("act", lambda e: e.activation(out=sv[:], in_=sv[:], func=AF.Silu), reads=["sv"], writes=["sv"])

        def ABv(s, stream, which):
            o = ((s * 2 + stream) * 3 + which) * KC
            return AB[:, o:o + KC]

        def phase_mod(l, es2):
            wb = [alloc(es2, "mw%d" % i, [128, KC * 128]) for i in range(2)]
            for c in range(144):
                b = wb[c % 2]
                S.dma("sp", b[:], modw.ap()[l, c], writes=["mw%d" % (c % 2)])
                p = ps[c % 2]
                def mm(e, b=b, p=p):
                    for kc in range(KC):
                        r = e.matmul(p[:, 0:2], lhsT=b[:, kc * 128:(kc + 1) * 128], rhs=sv[:, 2 * kc:2 * kc + 2],
                                     start=(kc == 0), stop=(kc == KC - 1))
                    return r
                S.op("pe", mm, reads=["mw%d" % (c % 2), "sv"], writes=["ps%d" % (c % 2)])
                S.op("dve", lambda e, p=p, c=c: e.tensor_scalar(out=modt[:, 2 * c:2 * c + 2], in0=p[:, 0:2],
                                                                scalar1=col(modbt, l * 144 + c), scalar2=None, op0=ALU.add),
                     reads=["ps%d" % (c % 2), "modbt"], writes=["modt"])
            mv = modt[:].rearrange("p (m k s) -> p m k s", m=NMOD, k=KC)
            for s in range(3):
                for st in range(2):
                    g = ngt[:, (l * 3 + s) * KC:(l * 3 + s + 1) * KC]
                    S.op("dve", lambda e, s=s, st=st, g=g: e.scalar_tensor_tensor(
                        out=ABv(s, st, 0), in0=mv[:, 3 * s + 1, :, st], scalar=1.0, in1=g, op0=ALU.add, op1=ALU.mult),
                        reads=["modt", "ngt"], writes=["AB"])
                    S.op("dve", lambda e, s=s, st=st: e.tensor_copy(out=ABv(s, st, 1), in_=mv[:, 3 * s, :, st]),
                         reads=["modt"], writes=["AB"])
                    S.op("dve", lambda e, s=s, st=st: e.tensor_scalar(
                        out=ABv(s, st, 2), in0=mv[:, 3 * s + 2, :, st], scalar1=(1.0 if s == 1 else 0.5), scalar2=None,
                        op0=ALU.mult), reads=["modt"], writes=["AB"])

        def adaln(hx, xn, rstd, n, s, st, pss, tag, sqkeys=("sqb",)):
            S.op("act", lambda e: e.activation(out=RR(xn[:, 0:KC * n]), in_=hx[:, 0:KC * n], func=AF.Square),
                 reads=["hx"], writes=list(sqkeys))
            plist = pss if isinstance(pss, (list, tuple)) else [pss]
            tags = tag if isinstance(tag, (list, tuple)) else [tag]
            for hi, h0 in enumerate(range(0, n, 512)):
                w_ = min(512, n - h0)
                def mm(e, hi=hi, h0=h0, w_=w_):
                    for kc in range(KC):
                        r = e.matmul(plist[hi][:, 0:w_], lhsT=RR(ones), rhs=RR(xn[:, kc * n + h0:kc * n + h0 + w_]), start=(kc == 0), stop=(kc == KC - 1))
                    return r
                S.op("pe", mm, reads=list(sqkeys) + ["cst"], writes=[tags[hi]])
                S.op("dve", lambda e, hi=hi, h0=h0, w_=w_: e.tensor_scalar(out=rstd[:, h0:h0 + w_], in0=plist[hi][:, 0:w_], scalar1=1.0 / D, scalar2=EPS,
                                                                      op0=ALU.mult, op1=ALU.add), reads=[tags[hi], "rstd"], writes=["rstd"])
            S.op("act", lambda e: e.activation(out=rstd[:, 0:n], in_=rstd[:, 0:n], func=AF.Sqrt), reads=["rstd"], writes=["rstd"])
            S.op("dve", lambda e: e.reciprocal(out=rstd[:, 0:n], in_=rstd[:, 0:n]), reads=["rstd"], writes=["rstd"])
            for kc in range(KC):
                S.op("dve", lambda e, kc=kc: e.scalar_tensor_tensor(
                    out=RR(hx[:, kc * n:(kc + 1) * n]), in0=hx[:, kc * n:(kc + 1) * n], scalar=col(ABv(s, st, 0), kc),
                    in1=rstd[:, 0:n], op0=ALU.mult, op1=ALU.mult), reads=["hx", "rstd", "AB"], writes=["xn%d" % kc])
                S.op("act", lambda e, kc=kc: e.activation(
                    out=RR(hx[:, kc * n:(kc + 1) * n]), in_=hx[:, kc * n:(kc + 1) * n], func=AF.Identity,
                    bias=col(ABv(s, st, 1), kc)), reads=["xn%d" % kc, "AB"], writes=["xn%d" % kc])
            return ["xn%d" % kc for kc in range(KC)]

        def load_h(hx, src, t0, n):
            S.dma("pool", RR(hx[:, 0:KC * n]).rearrange("p (k t) -> p k t", k=KC),
                  RR(src.ap()[:, t0:t0 + n]).rearrange("(k p) t -> p k t", p=128), reads=["hT"], writes=["hx"] + ["xn%d" % kc for kc in range(KC)])

        def phase_ffn(l, w, s, src, es2, skip_ctx=False):
            hx = alloc(es2, "hx", [128, KC * G])
            act = alloc(es2, "actb", [128, FC * G])
            rstd = alloc(es2, "rstd", [128, G])
            sg = [alloc(es2, "sg%d" % i, [128, G]) for i in range(2)]
            ho = [alloc(es2, "ho%d" % i, [128, G]) for i in range(2)]
            hr = [alloc(es2, "hr%d" % i, [128, G]) for i in range(2)]
            wb = [alloc(es2, "wb%d" % i, [128, FC * 128]) for i in range(2)]
            xn = hx
            for (t0, n, st) in cfg.groups:
                if skip_ctx and st == 1:
                    continue
                load_h(hx, src, t0, n)
                xk = adaln(hx, act, rstd, n, s, st, ps[7], "ps7", sqkeys=["act%d" % j for j in range(KC)])
                for j in range(FC):
                    b = wb[j % 2]; bk = "wb%d" % (j % 2)
                    S.dma("sp", RR(b[:, 0:2048]), RR(w1t.ap()[l, w, j]), writes=[bk + "a"])
                    S.dma("sp", RR(b[:, 2048:4096]), RR(w1t.ap()[l, w, FC + j]), writes=[bk + "b"])
                    pg, pu = ps[(j % 2) * 2], ps[(j % 2) * 2 + 1]
                    kg, ku = "ps%d" % ((j % 2) * 2), "ps%d" % ((j % 2) * 2 + 1)
                    def mm(e, b=b, p=pg, o=0):
                        for kc in range(KC):
                            r = e.matmul(p[:, 0:n], lhsT=RR(b[:, o + kc * 128:o + (kc + 1) * 128]), rhs=RR(xn[:, kc * n:(kc + 1) * n]),
                                         start=(kc == 0), stop=(kc == KC - 1))
                        return r
                    S.op("pe", mm, reads=[bk + "a"] + xk, writes=[kg])
                    S.op("pe", lambda e, b=b, pu=pu: mm(e, b, pu, 2048), reads=[bk + "b"] + xk, writes=[ku])
                    sgt = sg[j % 2]; sk = "sg%d" % (j % 2)
                    S.op("act", lambda e, sgt=sgt, pg=pg: e.activation(out=sgt[:, 0:n], in_=pg[:, 0:n], func=AF.Silu),
                         reads=[kg], writes=[sk])
                    S.op("dve", lambda e, sgt=sgt, pu=pu, j=j: e.tensor_tensor(out=RR(act[:, j * n:(j + 1) * n]), in0=pu[:, 0:n],
                                                                           in1=sgt[:, 0:n], op=ALU.mult),
                         reads=[sk, ku], writes=["act%d" % j])
                ak = ["act%d" % j for j in range(FC)]
                for fo in range(KC):
                    b = wb[fo % 2]; bk = "wb%d" % (fo % 2)
                    S.dma("sp", RR(b[:]), RR(w2t.ap()[l, w, fo]), writes=[bk + "a", bk + "b"])
                    hrt = hr[fo % 2]; rk = "hr%d" % (fo % 2)
                    S.dma("pool", hrt[:, 0:n], src.ap()[fo * 128:(fo + 1) * 128, t0:t0 + n], writes=[rk])
                    p = ps[4 + fo % 2]; pk = "ps%d" % (4 + fo % 2)
                    def mm2(e, b=b, p=p):
                        for j in range(FC):
                            r = e.matmul(p[:, 0:n], lhsT=RR(b[:, j * 128:(j + 1) * 128]), rhs=RR(act[:, j * n:(j + 1) * n]),
                                         start=(j == 0), stop=(j == FC - 1))
                        return r
                    S.op("pe", mm2, reads=[bk + "a", bk + "b"] + ak, writes=[pk])
                    hot = ho[fo % 2]; hk = "ho%d" % (fo % 2)
                    S.op("dve", lambda e, hot=hot, p=p, fo=fo, hrt=hrt: e.scalar_tensor_tensor(
                        out=hot[:, 0:n], in0=p[:, 0:n], scalar=col(ABv(s, st, 2), fo), in1=hrt[:, 0:n],
                        op0=ALU.mult, op1=ALU.add), reads=[pk, rk, "AB"], writes=[hk])
                    S.dma("pool", hT.ap()[fo * 128:(fo + 1) * 128, t0:t0 + n], hot[:, 0:n], reads=[hk], writes=["hT"])

        def phase_proj(l, es2):
            G2 = cfg.G2
            hx = alloc(es2, "hx", [128, KC * G2])
            xn = alloc(es2, "xn", [128, KC * G2])
            rstd = alloc(es2, "rstd", [128, G2])
            wb = [alloc(es2, "wb%d" % i, [128, KC * 256]) for i in range(2)]
            wg = alloc(es2, "wg", [128, KC * 16])
            so = [alloc(es2, "so%d" % i, [128, max(G2, 256)]) for i in range(2)]
            S.dma("sp", wg[:], wpg.ap()[l], writes=["wg"])
            for (t0, n, st) in cfg.groups2:
                load_h(hx, hT, t0, n)
                xk = adaln(hx, xn, rstd, n, 1, st, [ps[7], ps[6]], ["ps7", "ps6"])
                xn_ = hx
                it = 0
                for c in range(NFMC):
                    b = wb[it % 2]; bk = "wb%d" % (it % 2)
                    S.dma("sp", RRp(b[:, 0:2048]), RRp(wpf.ap()[l, c]), writes=[bk])
                    sot = so[it % 2]; sk = "so%d" % (it % 2)
                    for hi, h0 in enumerate(range(0, n, 512)):
                        w_ = min(512, n - h0)
                        p = ps[(it % 2) * 2 + hi]; pk = "ps%d" % ((it % 2) * 2 + hi)
                        def mm(e, b=b, p=p, h0=h0, w_=w_):
                            for kc in range(KC):
                                r = e.matmul(p[:, 0:w_], lhsT=RRp(b[:, kc * 128:(kc + 1) * 128]), rhs=RRp(xn_[:, kc * n + h0:kc * n + h0 + w_]),
                                             start=(kc == 0), stop=(kc == KC - 1))
                            return r
                        S.op("pe", mm, reads=[bk] + xk, writes=[pk])
                        S.op("act", lambda e, sot=sot, p=p, h0=h0, w_=w_: e.activation(out=sot[:, h0:h0 + w_], in_=p[:, 0:w_], func=AF.Copy),
                             reads=[pk, sk], writes=[sk])
                    S.dma("pool", PTfm.ap()[c * 128:(c + 1) * 128, t0:t0 + n], sot[:, 0:n], reads=[sk], writes=["PTfm"])
                    it += 1
                for cb in range(NTMB + 1):
                    gate = (cb == NTMB)
                    ncol = 16 if gate else 256
                    if gate:
                        b = wg; bk = "wg"
                    else:
                        b = wb[it % 2]; bk = "wb%d" % (it % 2)
                        S.dma("sp", RRp(b[:]), RRp(wpt.ap()[l, cb]), writes=[bk])
                    for ts in range(n // 128):
                        p = ps[4 + it % 2]; pk = "ps%d" % (4 + it % 2)
                        def mm(e, b=b, p=p, ts=ts, ncol=ncol):
                            for kc in range(KC):
                                if ncol >= 256:
                                    r = e.matmul(p[:, 0:ncol], lhsT=RRp(xn_[:, kc * n + ts * 128:kc * n + (ts + 1) * 128]),
                                                 rhs=RRp(b[:, kc * ncol:(kc + 1) * ncol]), start=(kc == 0), stop=(kc == KC - 1))
                                else:
                                    r = e.matmul(p[:, 0:ncol], lhsT=xn_[:, kc * n + ts * 128:kc * n + (ts + 1) * 128],
                                                 rhs=b[:, kc * ncol:(kc + 1) * ncol], start=(kc == 0), stop=(kc == KC - 1))
                            return r
                        S.op("pe", mm, reads=[bk] + xk, writes=[pk])
                        sot = so[it % 2]; sk = "so%d" % (it % 2)
                        S.op("act", lambda e, sot=sot, p=p, ncol=ncol: e.activation(out=sot[:, 0:ncol], in_=p[:, 0:ncol], func=AF.Copy),
                             reads=[pk], writes=[sk])
                        S.dma("pool", Ptm.ap()[t0 + ts * 128:t0 + (ts + 1) * 128, cb * 256:cb * 256 + ncol], sot[:, 0:ncol],
                              reads=[sk], writes=["Ptm"])
                        it += 1

        def rev(ap2d, n):
            a = ap2d.ap
            return bass.AP(ap2d.tensor, ap2d.offset + (n - 1) * a[-1][0], [list(a[0]), [-a[-1][0], n]])

        def chunk_order(d):
            nct, nlt = cfg.CTX // 128, cfg.SEQ // 128
            if d == 0:
                return list(range(nct + nlt))
            return list(range(nct - 1, -1, -1)) + list(range(nct + nlt - 1, nct - 1, -1))

        def phase_lb(es2):
            lg = alloc(es2, "lg", [128, L * 2048])
            sm = alloc(es2, "sm", [128, 2048])
            cum = alloc(es2, "cum", [128, 2048])
            S.dma("sp", lg[:], lblog.ap(), writes=["lg"])
            S.op("dve", lambda e: e.tensor_copy(out=sm[:], in_=lg[:, 0:2048]), reads=["lg"], writes=["sm"])
            for k in range(1, L):
                S.op("dve", lambda e, k=k: e.tensor_tensor(out=sm[:], in0=sm[:], in1=lg[:, k * 2048:(k + 1) * 2048], op=ALU.max),
                     reads=["lg", "sm"], writes=["sm"])
            for k in range(L):
                S.op("dve", lambda e, k=k: e.tensor_tensor(out=lg[:, k * 2048:(k + 1) * 2048], in0=lg[:, k * 2048:(k + 1) * 2048], in1=sm[:],
                                                           op=ALU.subtract), reads=["lg", "sm"], writes=["lg"])
            S.op("act", lambda e: e.activation(out=lg[:], in_=lg[:], func=AF.Exp), reads=["lg"], writes=["lg"])
            S.op("dve", lambda e: e.tensor_copy(out=sm[:], in_=lg[:, 0:2048]), reads=["lg", "sm"], writes=["sm"])
            for k in range(1, L):
                S.op("dve", lambda e, k=k: e.tensor_tensor(out=sm[:], in0=sm[:], in1=lg[:, k * 2048:(k + 1) * 2048], op=ALU.add),
                     reads=["lg", "sm"], writes=["sm"])
            S.op("dve", lambda e: e.reciprocal(out=sm[:], in_=sm[:]), reads=["sm"], writes=["sm"])
            S.op("pool", lambda e: e.memset(cum[:], 0.0), writes=["cum"])
            for l in range(L):
                if l > 0:
                    S.op("dve", lambda e, l=l: e.tensor_tensor(out=lg[:, l * 2048:(l + 1) * 2048], in0=lg[:, l * 2048:(l + 1) * 2048],
                                                               in1=sm[:], op=ALU.mult), reads=["lg", "sm"], writes=["lg"])
                    S.op("dve", lambda e, l=l: e.tensor_tensor(out=cum[:], in0=cum[:], in1=lg[:, l * 2048:(l + 1) * 2048], op=ALU.add),
                         reads=["lg", "cum"], writes=["cum"])
                S.op("dve", lambda e, l=l: e.tensor_scalar(out=lg[:, l * 2048:(l + 1) * 2048], in0=cum[:], scalar1=-1.0, scalar2=1.0,
                                                           op0=ALU.mult, op1=ALU.add), reads=["cum", "lg"], writes=["lg"])
                for d in range(2):
                    S.dma("sp", LBD.ap()[l * 2 + d], lg[:, l * 2048 + d * 1024:l * 2048 + (d + 1) * 1024], reads=["lg"], writes=["LBD"])

        def phase_rglru(l, es2):
            T_, CT, SQ = cfg.T, cfg.CTX, cfg.SEQ
            A_, B_, C_, D_, E_, F_ = [alloc(es2, "rb%d" % i, [128, T_ + 8]) for i in range(6)]
            bdw = alloc(es2, "bdw", [128, 4 * 128])
            cw = alloc(es2, "cw", [128, L * 32]); cb = alloc(es2, "cb", [128, L * 8])
            gb = alloc(es2, "gb", [128, L * 32]); lam = alloc(es2, "lam", [128, L * 16])
            S.dma("sp", cw[:], convw.ap(), writes=["cw"]); S.dma("sp", cb[:], convb.ap(), writes=["cb"])
            S.dma("sp", gb[:], rgb.ap(), writes=["gb"]); S.dma("sp", lam[:], rglam.ap(), writes=["lam"])
            S.op("act", lambda e: e.activation(out=lam[:], in_=lam[:], func=AF.Exp, scale=-1.0), reads=["lam"], writes=["lam"])
            S.op("dve", lambda e: e.tensor_scalar(out=lam[:], in0=lam[:], scalar1=1.0, scalar2=None, op0=ALU.add), reads=["lam"], writes=["lam"])
            S.op("act", lambda e: e.activation(out=lam[:], in_=lam[:], func=AF.Ln), reads=["lam"], writes=["lam"])
            S.op("dve", lambda e: e.tensor_scalar(out=lam[:], in0=lam[:], scalar1=-8.0, scalar2=None, op0=ALU.mult), reads=["lam"], writes=["lam"])
            for cc in range(8):
                S.dma("sp", A_[:, 0:T_], PTfm.ap()[(16 + cc) * 128:(17 + cc) * 128, :], reads=["PTfm"], writes=["rA"])
                S.dma("sp", F_[:, 0:T_], PTfm.ap()[(24 + cc) * 128:(25 + cc) * 128, :], reads=["PTfm"], writes=["rF"])
                S.dma("sp", bdw[:].rearrange("p (a m) -> p a m", a=4), rgw.ap()[l, :, cc].rearrange("a p m -> p a m"), writes=["bdw"])
                S.op("pool", lambda e: e.memset(B_[:], 0.0), writes=["rB"])
                S.op("dve", lambda e: e.tensor_copy(out=B_[:, 2:2 + CT], in_=A_[:, 0:CT]), reads=["rA", "rB"], writes=["rB"])
                S.op("dve", lambda e: e.tensor_copy(out=B_[:, CT + 5:CT + 5 + SQ].rearrange("p (w r) -> p w r", w=64),
                                                    in_=A_[:, CT:T_].rearrange("p (r w) -> p w r", w=64)), reads=["rA", "rB"], writes=["rB"])
                for (oo, oi, n) in ((0, 0, CT), (CT, CT + 3, SQ)):
                    for j in range(4):
                        wj = col(cw, (l * 4 + j) * 8 + cc)
                        if j == 0:
                            S.op("dve", lambda e, oo=oo, oi=oi, n=n, wj=wj: e.tensor_scalar(
                                out=C_[:, oo:oo + n], in0=B_[:, oi:oi + n], scalar1=wj, scalar2=col(cb, l * 8 + cc),
                                op0=ALU.mult, op1=ALU.add), reads=["rB", "cw", "cb"], writes=["rC"])
                        else:
                            S.op("dve", lambda e, oo=oo, oi=oi, n=n, wj=wj, j=j: e.scalar_tensor_tensor(
                                out=C_[:, oo:oo + n], in0=B_[:, oi + j:oi + j + n], scalar=wj, in1=C_[:, oo:oo + n],
                                op0=ALU.mult, op1=ALU.add), reads=["rB", "rC", "cw"], writes=["rC"])
                for d in range(2):
                    for ri, dst, dk in ((0, A_, "rA"), (1, B_, "rB")):
                        for i, tt in enumerate(range(0, T_, 512)):
                            n = min(512, T_ - tt)
                            p = ps[i % 2]; pk = "ps%d" % (i % 2)
                            S.op("pe", lambda e, p=p, tt=tt, n=n, ri=ri: e.matmul(
                                p[:, 0:n], lhsT=bdw[:, (d * 2 + ri) * 128:(d * 2 + ri + 1) * 128], rhs=C_[:, tt:tt + n], start=True, stop=True),
                                reads=["bdw", "rC"], writes=[pk])
                            S.op("act", lambda e, p=p, tt=tt, n=n, ri=ri, dst=dst: e.activation(
                                out=dst[:, tt:tt + n], in_=p[:, 0:n], func=AF.Sigmoid, bias=col(gb, ((l * 2 + d) * 2 + ri) * 8 + cc)),
                                reads=[pk, "gb"], writes=[dk])
                    S.op("act", lambda e: e.activation(out=A_[:, 0:T_], in_=A_[:, 0:T_], func=AF.Exp, scale=col(lam, (l * 2 + d) * 8 + cc)),
                         reads=["rA", "lam"], writes=["rA"])
                    S.op("dve", lambda e: e.tensor_tensor(out=B_[:, 0:T_], in0=B_[:, 0:T_], in1=C_[:, 0:T_], op=ALU.mult),
                         reads=["rB", "rC"], writes=["rB"])
                    hd, hk = (D_, "rD") if d == 0 else (E_, "rE")
                    S.op("pool", lambda e, hd=hd: e.tensor_tensor(out=hd[:, 0:T_], in0=A_[:, 0:T_], in1=A_[:, 0:T_], op=ALU.mult),
                         reads=["rA"], writes=[hk])
                    S.op("dve", lambda e, hd=hd: e.tensor_scalar(out=hd[:, 0:T_], in0=hd[:, 0:T_], scalar1=-1.0, scalar2=1.0,
                                                                 op0=ALU.mult, op1=ALU.add), reads=[hk], writes=[hk])
                    S.op("act", lambda e, hd=hd: e.activation(out=hd[:, 0:T_], in_=hd[:, 0:T_], func=AF.Sqrt), reads=[hk], writes=[hk])
                    S.op("dve", lambda e, hd=hd: e.tensor_tensor(out=B_[:, 0:T_], in0=B_[:, 0:T_], in1=hd[:, 0:T_], op=ALU.mult),
                         reads=["rB", hk], writes=["rB"])
                    if d == 0:
                        S.op("dve", lambda e: e.tensor_tensor_scan(out=D_[:, 0:T_], data0=A_[:, 0:T_], data1=B_[:, 0:T_], initial=0.0,
                                                                   op0=ALU.mult, op1=ALU.add), reads=["rA", "rB", "rD"], writes=["rD"])
                    else:
                        S.op("dve", lambda e: e.tensor_tensor_scan(out=rev(E_[:, 0:CT], CT), data0=rev(A_[:, 0:CT], CT),
                                                                   data1=rev(B_[:, 0:CT], CT), initial=0.0, op0=ALU.mult, op1=ALU.add),
                             reads=["rA", "rB", "rE"], writes=["rE"])
                        S.op("dve", lambda e: e.tensor_tensor_scan(out=rev(E_[:, CT:T_], SQ), data0=rev(A_[:, CT:T_], SQ),
                                                                   data1=rev(B_[:, CT:T_], SQ), initial=E_[:, 0:1], op0=ALU.mult, op1=ALU.add),
                             reads=["rA", "rB", "rE"], writes=["rE"])
                S.op("dve", lambda e: e.tensor_tensor(out=D_[:, 0:T_], in0=D_[:, 0:T_], in1=E_[:, 0:T_], op=ALU.add), reads=["rD", "rE"], writes=["rD"])
                S.op("act", lambda e: e.activation(out=F_[:, 0:T_], in_=F_[:, 0:T_], func=AF.Gelu), reads=["rF"], writes=["rF"])
                S.op("dve", lambda e: e.tensor_tensor(out=E_[:, 0:CT], in0=D_[:, 0:CT], in1=F_[:, 0:CT], op=ALU.mult), reads=["rD", "rF", "rE"], writes=["rE"])
                S.op("dve", lambda e: e.tensor_tensor(out=E_[:, CT:T_].rearrange("p (r w) -> p r w", w=64),
                                                      in0=D_[:, CT:T_].rearrange("p (w r) -> p r w", w=64),
                                                      in1=F_[:, CT:T_].rearrange("p (r w) -> p r w", w=64), op=ALU.mult),
                     reads=["rD", "rF", "rE"], writes=["rE"])
                S.dma("pool", YT.ap()[2048 + cc * 128:2048 + (cc + 1) * 128, :], E_[:, 0:T_], reads=["rE"], writes=["YT"])

        def ap3(t2d, off, dims, npart=128):
            return bass.AP(t2d.tensor, t2d.offset + off, [[t2d.ap[0][0], npart]] + [list(x) for x in dims])

        def phase_hgrn(l, es2):
            P2 = lambda name, shape: [alloc(es2, "%s%d" % (name, i), shape) for i in range(2)]
            ftm = P2("ftm", [32, 1024]); ktm = P2("ktm", [32, 1024]); gtm = P2("gtm", [32, 1024]); vtm = P2("vtm", [32, 1024])
            otl = P2("otl", [32, 1024]); qt = P2("qt", [128, 256]); ex = P2("ex", [128, 272]); ktT = P2("ktT", [128, 256])
            attm = P2("attm", [32, 256]); qfm = P2("qfm", [128, 1024])
            oml = alloc(es2, "oml", [128, 1024]); Sst = alloc(es2, "Sst", [128, 1024]); Sp = alloc(es2, "Sp", [128, 1024])
            id32 = cst[0:32, 128:160]
            items = []
            for d in range(2):
                for cn, ci in enumerate(chunk_order(d)):
                    for sn, sj in enumerate(range(4) if d == 0 else range(3, -1, -1)):
                        items.append((d, ci, sj, cn == 0 and sn == 0, sn == 0))
            cpar = [0]

            def stageA(it, p):
                d, ci, sj, first_of_dir, first_of_chunk = it
                RX = cst[0:32, 772:806] if d == 0 else cst[0:32, 806:840]
                TR = RX[:, 0:32]
                MK = cst[0:32, 256:288] if d == 0 else cst[0:32, 384:416]
                k = lambda nm: "%s%d" % (nm, p)
                if first_of_dir:
                    S.dma("sp", oml[:], LBD.ap()[l * 2 + d], reads=["LBD"], writes=["oml"])
                if first_of_chunk:
                    cpar[0] ^= 1
                    q = qfm[cpar[0]]; qk = "qfm%d" % cpar[0]
                    S.dma("pool", q[:].rearrange("p (h t) -> p h t", h=8),
                          PTfm.ap()[1024:2048, ci * 128:(ci + 1) * 128].rearrange("(h p) t -> p h t", p=128), reads=["PTfm"], writes=[qk])
                    S.op("act", lambda e: e.activation(out=q[:], in_=q[:], func=AF.Silu), reads=[qk], writes=[qk])
                q = qfm[cpar[0]]; qk = "qfm%d" % cpar[0]
                t0 = ci * 128 + sj * 32
                f_, k_, g_, v_, x_, qt_, kT_, at_ = ftm[p], ktm[p], gtm[p], vtm[p], ex[p], qt[p], ktT[p], attm[p]
                S.dma("sp", f_[:], Ptm.ap()[t0:t0 + 32, (3 + d) * 1024:(4 + d) * 1024], reads=["Ptm"], writes=[k("ftm")])
                S.dma("sp", v_[:], Ptm.ap()[t0:t0 + 32, 5 * 1024:6 * 1024], reads=["Ptm"], writes=[k("vtm")])
                S.op("act", lambda e: e.activation(out=f_[:], in_=f_[:], func=AF.Sigmoid, scale=-1.0), reads=[k("ftm")], writes=[k("ftm")])
                S.op("dve", lambda e: e.tensor_tensor(out=k_[:], in0=f_[:], in1=oml[0:32, :], op=ALU.mult), reads=[k("ftm"), "oml"], writes=[k("ktm")])
                S.op("dve", lambda e: e.tensor_scalar(out=g_[:], in0=k_[:], scalar1=-1.0, scalar2=1.0, op0=ALU.mult, op1=ALU.add),
                     reads=[k("ktm")], writes=[k("gtm")])
                S.op("act", lambda e: e.activation(out=g_[:], in_=g_[:], func=AF.Ln), reads=[k("gtm")], writes=[k("gtm")])
                for hf in range(2):
                    S.op("pe", lambda e, hf=hf: e.matmul(ps[hf][0:32, 0:512], lhsT=TR, rhs=g_[:, hf * 512:(hf + 1) * 512], start=True, stop=True),
                         reads=[k("gtm"), "cst"], writes=["ps%d" % hf])
                    S.op("act", lambda e, hf=hf: e.activation(out=f_[:, hf * 512:(hf + 1) * 512], in_=ps[hf][0:32, 0:512], func=AF.Exp, scale=-1.0),
                         reads=["ps%d" % hf], writes=[k("ftm")])
                S.op("dve", lambda e: e.tensor_tensor(out=k_[:], in0=k_[:], in1=f_[:], op=ALU.mult), reads=[k("ktm"), k("ftm")], writes=[k("ktm")])
                def mmrx(e):
                    for h in range(8):
                        r = e.matmul(ps[2][:, h * 34:(h + 1) * 34], lhsT=g_[:, h * 128:(h + 1) * 128], rhs=RX, start=True, stop=True)
                    return r
                S.op("pe", mmrx, reads=[k("gtm"), "cst"], writes=["ps2"])
                S.op("act", lambda e: e.activation(out=x_[:], in_=ps[2][:, 0:272], func=AF.Exp), reads=["ps2"], writes=[k("ex")])
                S.op("dve", lambda e: e.tensor_tensor(out=qt_[:].rearrange("p (h t) -> p h t", h=8),
                                                      in0=q[:].rearrange("p (h t) -> p h t", h=8)[:, :, sj * 32:(sj + 1) * 32],
                                                      in1=x_[:].rearrange("p (h t) -> p h t", h=8)[:, :, 0:32], op=ALU.mult),
                     reads=[qk, k("ex")], writes=[k("qt")])
                def mmtr(e):
                    for h in range(8):
                        r = e.matmul(ps[3][:, h * 32:(h + 1) * 32], lhsT=k_[:, h * 128:(h + 1) * 128], rhs=id32, start=True, stop=True)
                    return r
                S.op("pe", mmtr, reads=[k("ktm"), "cst"], writes=["ps3"])
                S.op("act", lambda e: e.activation(out=kT_[:], in_=ps[3][:, 0:256], func=AF.Copy), reads=["ps3"], writes=[k("ktT")])
                def mmat(e):
                    for h in range(8):
                        r = e.matmul(ps[4][0:32, h * 32:(h + 1) * 32], lhsT=kT_[:, h * 32:(h + 1) * 32], rhs=qt_[:, h * 32:(h + 1) * 32], start=True, stop=True)
                    return r
                S.op("pe", mmat, reads=[k("ktT"), k("qt")], writes=["ps4"])
                S.op("dve", lambda e: e.tensor_tensor(out=at_[:].rearrange("p (h t) -> p h t", h=8),
                                                      in0=ps[4][0:32, 0:256].rearrange("p (h t) -> p h t", h=8),
                                                      in1=ap3(MK, 0, [[0, 8], [1, 32]], npart=32), op=ALU.mult), reads=["ps4", "cst"], writes=[k("attm")])

            def stageB(it, p):
                d, ci, sj, first_of_dir, first_of_chunk = it
                k = lambda nm: "%s%d" % (nm, p)
                t0 = ci * 128 + sj * 32
                k_, v_, x_, qt_, at_, o_ = ktm[p], vtm[p], ex[p], qt[p], attm[p], otl[p]
                if first_of_dir:
                    S.op("pool", lambda e: e.memset(Sst[:], 0.0), writes=["Sst"])
                S.op("dve", lambda e: e.tensor_tensor(out=Sp[:].rearrange("p (h e) -> p h e", h=8),
                                                      in0=Sst[:].rearrange("p (h e) -> p h e", h=8),
                                                      in1=ap3(x_[:], 32, [[34, 8], [0, 128]]), op=ALU.mult), reads=["Sst", k("ex")], writes=["Sp"])
                def mmo(e):
                    for h in range(8):
                        pp = ps[5 + h // 4]; c0 = (h % 4) * 128
                        e.matmul(pp[0:32, c0:c0 + 128], lhsT=at_[:, h * 32:(h + 1) * 32], rhs=v_[:, h * 128:(h + 1) * 128], start=True, stop=False)
                        r = e.matmul(pp[0:32, c0:c0 + 128], lhsT=qt_[:, h * 32:(h + 1) * 32], rhs=Sp[:, h * 128:(h + 1) * 128], start=False, stop=True)
                    return r
                S.op("pe", mmo, reads=[k("attm"), k("vtm"), k("qt"), "Sp"], writes=["ps5", "ps6"])
                for hf in range(2):
                    S.op("act", lambda e, hf=hf: e.activation(out=o_[:, hf * 512:(hf + 1) * 512], in_=ps[5 + hf][0:32, 0:512], func=AF.Copy),
                         reads=["ps%d" % (5 + hf)], writes=[k("otl")])
                S.dma("pool", HAB.ap()[2 + d, t0:t0 + 32, :], o_[:], reads=[k("otl")], writes=["HAB"])
                for hf in range(2):
                    def mmu(e, hf=hf):
                        for h in range(hf * 4, hf * 4 + 4):
                            c0 = (h % 4) * 128
                            r = e.matmul(ps[7][:, c0:c0 + 128], lhsT=k_[:, h * 128:(h + 1) * 128], rhs=v_[:, h * 128:(h + 1) * 128], start=True, stop=True)
                        return r
                    S.op("pe", mmu, reads=[k("ktm"), k("vtm")], writes=["ps7"])
                    S.op("dve", lambda e, hf=hf: e.tensor_tensor(out=Sp[:, hf * 512:(hf + 1) * 512], in0=Sp[:, hf * 512:(hf + 1) * 512],
                                                                 in1=ps[7][:, 0:512], op=ALU.add), reads=["Sp", "ps7"], writes=["Sp"])
                S.op("dve", lambda e: e.tensor_tensor(out=Sst[:].rearrange("p (h e) -> p h e", h=8),
                                                      in0=Sp[:].rearrange("p (h e) -> p h e", h=8),
                                                      in1=ap3(x_[:], 33, [[34, 8], [0, 128]]), op=ALU.mult), reads=["Sp", k("ex")], writes=["Sst"])

            prev = None
            for n_, it in enumerate(items):
                stageA(it, n_ % 2)
                if prev is not None:
                    stageB(*prev)
                prev = (it, n_ % 2)
            stageB(*prev)

        def phase_mlstm(l, es2):
            P2 = lambda name, shape: [alloc(es2, "%s%d" % (name, i), shape) for i in range(2)]
            ktm = P2("aktm", [128, 1024]); vext = P2("vext", [128, 4 * 257]); qfm = P2("aqfm", [128, 1024]); gt = P2("gt", [128, 16])
            lf = P2("lf", [128, 4]); gi = P2("gi", [128, 4]); cbt = P2("cbt", [128, 4]); wl = P2("wl", [128, 4]); dec = P2("dec", [128, 4])
            otl = P2("aotl", [128, 1024])
            lfr = P2("lfr", [128, 128]); Dm = P2("Dm", [128, 128]); eB = P2("eB", [128, 128]); qt = P2("aqt", [128, 256]); kT = P2("kT", [128, 256])
            W = P2("W", [128, 128]); kh = P2("kh", [128, 256]); rec = P2("rec", [128, 1])
            bias = alloc(es2, "gbias", [128, L * 16]); Cx = alloc(es2, "Cx", [128, 8 * 257])
            S.dma("sp", bias[:], mgb.ap(), writes=["gbias"])
            for i in range(2):
                S.op("pool", lambda e, i=i: e.memset(vext[i][:], 1.0), writes=["vext%d" % i])
            items = []
            for d in range(2):
                for cn, ci in enumerate(chunk_order(d)):
                    for h in range(4):
                        items.append((d, ci, h, cn == 0 and h == 0))
            cpar = [0]

            def chunk_setup(d, ci, c):
                t0 = ci * 128
                kc = lambda nm: "%s%d" % (nm, c)
                S.dma("sp", ktm[c][:], Ptm.ap()[t0:t0 + 128, 0:1024], reads=["Ptm"], writes=[kc("aktm")])
                S.dma("sp", vext[c][:].rearrange("p (h e) -> p h e", h=4)[:, :, 0:256],
                      Ptm.ap()[t0:t0 + 128, 1024:2048].rearrange("t (h e) -> t h e", h=4), reads=["Ptm"], writes=[kc("vext")])
                S.dma("sp", gt[c][:], Ptm.ap()[t0:t0 + 128, NTMB * 256:NTMB * 256 + 16], reads=["Ptm"], writes=[kc("gt")])
                S.dma("pool", qfm[c][:].rearrange("p (h t) -> p h t", h=8),
                      PTfm.ap()[0:1024, t0:t0 + 128].rearrange("(h p) t -> p h t", p=128), reads=["PTfm"], writes=[kc("aqfm")])
                g_, lf_, gi_, cb_, wl_, dc_ = gt[c], lf[c], gi[c], cbt[c], wl[c], dec[c]
                TRI = cst[:, 256:384] if d == 0 else cst[:, 384:512]
                S.op("dve", lambda e: e.tensor_tensor(out=g_[:], in0=g_[:], in1=bias[:, l * 16:(l + 1) * 16], op=ALU.add), reads=[kc("gt"), "gbias"], writes=[kc("gt")])
                S.op("dve", lambda e: e.tensor_copy(out=gi_[:], in_=g_[:, d * 8:d * 8 + 4]), reads=[kc("gt")], writes=[kc("gi")])
                S.op("act", lambda e: e.activation(out=lf_[:], in_=g_[:, d * 8 + 4:d * 8 + 8], func=AF.Exp, scale=-1.0), reads=[kc("gt")], writes=[kc("lf")])
                S.op("dve", lambda e: e.tensor_scalar(out=lf_[:], in0=lf_[:], scalar1=1.0, scalar2=None, op0=ALU.add), reads=[kc("lf")], writes=[kc("lf")])
                S.op("act", lambda e: e.activation(out=lf_[:], in_=lf_[:], func=AF.Ln), reads=[kc("lf")], writes=[kc("lf")])
                S.op("dve", lambda e: e.tensor_scalar(out=lf_[:], in0=lf_[:], scalar1=-1.0, scalar2=None, op0=ALU.mult), reads=[kc("lf")], writes=[kc("lf")])
                def mmb(e):
                    e.matmul(ps[0][:, 0:4], lhsT=TRI, rhs=lf_[:], start=True, stop=True)
                    return e.matmul(ps[0][:, 4:8], lhsT=ones, rhs=lf_[:], start=True, stop=True)
                S.op("pe", mmb, reads=[kc("lf"), "cst"], writes=["ps0"])
                S.op("dve", lambda e: e.tensor_tensor(out=cb_[:], in0=gi_[:], in1=ps[0][:, 0:4], op=ALU.subtract), reads=[kc("gi"), "ps0"], writes=[kc("cbt")])
                S.op("dve", lambda e: e.tensor_tensor(out=wl_[:], in0=cb_[:], in1=ps[0][:, 4:8], op=ALU.add), reads=[kc("cbt"), "ps0"], writes=[kc("wl")])
                S.op("act", lambda e: e.activation(out=wl_[:], in_=wl_[:], func=AF.Exp), reads=[kc("wl")], writes=[kc("wl")])
                S.op("dve", lambda e: e.tensor_scalar(out=wl_[:], in0=wl_[:], scalar1=1.0 / 16.0, scalar2=None, op0=ALU.mult), reads=[kc("wl")], writes=[kc("wl")])
                S.op("act", lambda e: e.activation(out=dc_[:], in_=ps[0][:, 4:8], func=AF.Exp), reads=["ps0"], writes=[kc("dec")])

            def stageA(it, p):
                d, ci, h, first_of_dir = it
                if h == 0:
                    cpar[0] ^= 1
                    chunk_setup(d, ci, cpar[0])
                c = cpar[0]
                kc = lambda nm: "%s%d" % (nm, c)
                k = lambda nm: "%s%d" % (nm, p)
                TRI = cst[:, 256:384] if d == 0 else cst[:, 384:512]
                k_, q_, lf_, cb_, wl_ = ktm[c], qfm[c], lf[c], cbt[c], wl[c]
                lfr_, Dm_, eB_, qt_, kT_, W_, kh_ = lfr[p], Dm[p], eB[p], qt[p], kT[p], W[p], kh[p]
                S.op("dve", lambda e: e.tensor_scalar(out=lfr_[:], in0=ones, scalar1=lf_[:, h:h + 1], scalar2=None, op0=ALU.mult),
                     reads=[kc("lf"), "cst"], writes=[k("lfr")])
                S.op("pe", lambda e: e.matmul(ps[1][:, 0:128], lhsT=lfr_[:], rhs=TRI, start=True, stop=True), reads=[k("lfr"), "cst"], writes=["ps1"])
                S.op("act", lambda e: e.activation(out=Dm_[:], in_=ps[1][:, 0:128], func=AF.Exp, bias=cb_[:, h:h + 1]),
                     reads=["ps1", kc("cbt")], writes=[k("Dm")])
                S.op("dve", lambda e: e.tensor_tensor(out=Dm_[:], in0=Dm_[:], in1=TRI, op=ALU.mult), reads=[k("Dm"), "cst"], writes=[k("Dm")])
                S.op("act", lambda e: e.activation(out=eB_[:], in_=ps[1][:, 0:128], func=AF.Exp), reads=["ps1"], writes=[k("eB")])
                for dc in range(2):
                    qs = slice((h * 2 + dc) * 128, (h * 2 + dc + 1) * 128)
                    S.op("dve", lambda e, dc=dc, qs=qs: e.tensor_tensor(out=qt_[:, dc * 128:(dc + 1) * 128], in0=q_[:, qs], in1=eB_[:], op=ALU.mult),
                         reads=[kc("aqfm"), k("eB")], writes=[k("aqt")])
                    S.op("pe", lambda e, dc=dc: e.matmul(ps[2 + dc][:, 0:128], lhsT=k_[:, h * 256 + dc * 128:h * 256 + (dc + 1) * 128], rhs=ident,
                                                         start=True, stop=True), reads=[kc("aktm"), "cst"], writes=["ps%d" % (2 + dc)])
                    S.op("act", lambda e, dc=dc: e.activation(out=kT_[:, dc * 128:(dc + 1) * 128], in_=ps[2 + dc][:, 0:128], func=AF.Copy),
                         reads=["ps%d" % (2 + dc)], writes=[k("kT")])
                def mms(e):
                    e.matmul(ps[4][:, 0:128], lhsT=kT_[:, 0:128], rhs=q_[:, (h * 2) * 128:(h * 2 + 1) * 128], start=True, stop=False)
                    return e.matmul(ps[4][:, 0:128], lhsT=kT_[:, 128:256], rhs=q_[:, (h * 2 + 1) * 128:(h * 2 + 2) * 128], start=False, stop=True)
                S.op("pe", mms, reads=[k("kT"), kc("aqfm")], writes=["ps4"])
                S.op("dve", lambda e: e.scalar_tensor_tensor(out=W_[:], in0=ps[4][:, 0:128], scalar=1.0 / 16.0, in1=Dm_[:], op0=ALU.mult, op1=ALU.mult),
                     reads=["ps4", k("Dm")], writes=[k("W")])
                S.op("dve", lambda e: e.tensor_scalar(out=kh_[:], in0=k_[:, h * 256:(h + 1) * 256], scalar1=wl_[:, h:h + 1], scalar2=None,
                                                      op0=ALU.mult), reads=[kc("aktm"), kc("wl")], writes=[k("kh")])
                return c

            def stageB(it, p, c):
                d, ci, h, first_of_dir = it
                kc = lambda nm: "%s%d" % (nm, c)
                k = lambda nm: "%s%d" % (nm, p)
                v_, dc_, o_ = vext[c], dec[c], otl[c]
                qt_, W_, kh_, rec_ = qt[p], W[p], kh[p], rec[p]
                if first_of_dir:
                    S.op("pool", lambda e: e.memset(Cx[:], 0.0), writes=["Cx"])
                def mmn(e):
                    e.matmul(ps[5][:, 0:257], lhsT=W_[:], rhs=v_[:, h * 257:(h + 1) * 257], start=True, stop=False)
                    e.matmul(ps[5][:, 0:257], lhsT=qt_[:, 0:128], rhs=Cx[:, (h * 2) * 257:(h * 2 + 1) * 257], start=False, stop=False)
                    return e.matmul(ps[5][:, 0:257], lhsT=qt_[:, 128:256], rhs=Cx[:, (h * 2 + 1) * 257:(h * 2 + 2) * 257], start=False, stop=True)
                S.op("pe", mmn, reads=[k("W"), kc("vext"), k("aqt"), "Cx"], writes=["ps5"])
                S.op("act", lambda e: e.activation(out=rec_[:], in_=ps[5][:, 256:257], func=AF.Abs), reads=["ps5"], writes=[k("rec")])
                S.op("dve", lambda e: e.tensor_scalar(out=rec_[:], in0=rec_[:], scalar1=1.0, scalar2=None, op0=ALU.max), reads=[k("rec")], writes=[k("rec")])
                S.op("dve", lambda e: e.reciprocal(out=rec_[:], in_=rec_[:]), reads=[k("rec")], writes=[k("rec")])
                S.op("dve", lambda e: e.tensor_scalar(out=o_[:, h * 256:(h + 1) * 256], in0=ps[5][:, 0:256], scalar1=rec_[:, 0:1], scalar2=None,
                                                      op0=ALU.mult), reads=["ps5", k("rec")], writes=[kc("aotl")])
                for dc in range(2):
                    S.op("pe", lambda e, dc=dc: e.matmul(ps[6 + dc][:, 0:257], lhsT=kh_[:, dc * 128:(dc + 1) * 128], rhs=v_[:, h * 257:(h + 1) * 257],
                                                         start=True, stop=True), reads=[k("kh"), kc("vext")], writes=["ps%d" % (6 + dc)])
                    cs = slice((h * 2 + dc) * 257, (h * 2 + dc + 1) * 257)
                    S.op("dve", lambda e, dc=dc, cs=cs: e.scalar_tensor_tensor(out=Cx[:, cs], in0=Cx[:, cs], scalar=dc_[:, h:h + 1],
                                                                               in1=ps[6 + dc][:, 0:257], op0=ALU.mult, op1=ALU.add),
                         reads=["Cx", kc("dec"), "ps%d" % (6 + dc)], writes=["Cx"])
                if h == 3:
                    S.dma("pool", HAB.ap()[d, ci * 128:(ci + 1) * 128, :], o_[:], reads=[kc("aotl")], writes=["HAB"])

            prev = None
            for n_, it in enumerate(items):
                c = stageA(it, n_ % 2)
                if prev is not None:
                    stageB(*prev)
                prev = (it, n_ % 2, c)
            stageB(*prev)

        def phase_combine(l, es2):
            ha = alloc(es2, "ha", [128, 1024]); hb = alloc(es2, "hb", [128, 1024]); og = alloc(es2, "og", [128, 1024]); sq = alloc(es2, "sq", [128, 1024])
            ss = alloc(es2, "ss", [128, 8]); yo = [alloc(es2, "yo%d" % i, [128, 128]) for i in range(2)]
            for br, (nh, ogblk, fn) in enumerate(((4, 2, AF.Sigmoid), (8, 6, AF.Silu))):
                hd = 1024 // nh
                for ti in range(cfg.T // 128):
                    t0 = ti * 128
                    S.dma("sp", ha[:], HAB.ap()[br * 2, t0:t0 + 128, :], reads=["HAB"], writes=["ha"])
                    S.dma("sp", hb[:], HAB.ap()[br * 2 + 1, t0:t0 + 128, :], reads=["HAB"], writes=["hb"])
                    S.dma("pool", og[:], Ptm.ap()[t0:t0 + 128, ogblk * 1024:(ogblk + 1) * 1024], reads=["Ptm"], writes=["og"])
                    S.op("dve", lambda e: e.tensor_tensor(out=ha[:], in0=ha[:], in1=hb[:], op=ALU.add), reads=["ha", "hb"], writes=["ha"])
                    S.op("pool", lambda e: e.tensor_tensor(out=sq[:], in0=ha[:], in1=ha[:], op=ALU.mult), reads=["ha"], writes=["sq"])
                    S.op("dve", lambda e, nh=nh: e.tensor_reduce(out=ss[:, 0:nh], in_=sq[:].rearrange("p (h e) -> p h e", h=nh),
                                                                 axis=mybir.AxisListType.X, op=ALU.add), reads=["sq"], writes=["ss"])
                    S.op("dve", lambda e, nh=nh, hd=hd: e.tensor_scalar(out=ss[:, 0:nh], in0=ss[:, 0:nh], scalar1=1.0 / hd, scalar2=EPS,
                                                                        op0=ALU.mult, op1=ALU.add), reads=["ss"], writes=["ss"])
                    S.op("act", lambda e, nh=nh: e.activation(out=ss[:, 0:nh], in_=ss[:, 0:nh], func=AF.Sqrt), reads=["ss"], writes=["ss"])
                    S.op("dve", lambda e, nh=nh: e.reciprocal(out=ss[:, 0:nh], in_=ss[:, 0:nh]), reads=["ss"], writes=["ss"])
                    S.op("act", lambda e, fn=fn: e.activation(out=og[:], in_=og[:], func=fn), reads=["og"], writes=["og"])
                    for h in range(nh):
                        S.op("dve", lambda e, h=h, hd=hd: e.scalar_tensor_tensor(out=ha[:, h * hd:(h + 1) * hd], in0=ha[:, h * hd:(h + 1) * hd],
                                                                                scalar=ss[:, h:h + 1], in1=og[:, h * hd:(h + 1) * hd],
                                                                                op0=ALU.mult, op1=ALU.mult), reads=["ha", "ss", "og"], writes=["ha"])
                    for fc in range(8):
                        p = ps[fc % 2]; pk = "ps%d" % (fc % 2); y = yo[fc % 2]; yk = "yo%d" % (fc % 2)
                        S.op("pe", lambda e, p=p, fc=fc: e.matmul(p[:, 0:128], lhsT=ha[:, fc * 128:(fc + 1) * 128], rhs=ident, start=True, stop=True),
                             reads=["ha", "cst"], writes=[pk])
                        S.op("act", lambda e, p=p, y=y: e.activation(out=y[:], in_=p[:, 0:128], func=AF.Copy), reads=[pk], writes=[yk])
                        S.dma("pool", YT.ap()[br * 1024 + fc * 128:br * 1024 + (fc + 1) * 128, t0:t0 + 128], y[:], reads=[yk], writes=["YT"])

        def phase_merge(l, es2, skip_ctx):
            yt = alloc(es2, "yt", [128, 24 * G]); mT = alloc(es2, "mT", [128, KC * G])
            hr = [alloc(es2, "hrm%d" % i, [128, G]) for i in range(2)]
            mg = [alloc(es2, "mg%d" % i, [128, 3 * G]) for i in range(2)]
            wb = [alloc(es2, "wbm%d" % i, [128, 3 * 1024]) for i in range(2)]
            ho = [alloc(es2, "hom%d" % i, [128, G]) for i in range(2)]
            for (t0, n, st) in cfg.groups:
                if skip_ctx and st == 1:
                    continue
                S.dma("sp", RRm(yt[:, 0:24 * n]).rearrange("p (c t) -> p c t", c=24), RRm(YT.ap()[:, t0:t0 + n]).rearrange("(c p) t -> p c t", p=128),
                      reads=["YT"], writes=["yt"])
                for dc in range(KC):
                    b = wb[dc % 2]; bk = "wbm%d" % (dc % 2); g = mg[dc % 2]; gk = "mg%d" % (dc % 2)
                    for br in range(3):
                        S.dma("sp", RRm(b[:, br * 1024:(br + 1) * 1024]), RRm(wbr.ap()[l, br, dc]), writes=[bk + str(br)])
                        S.dma("pool", g[:, br * n:(br + 1) * n], PTfm.ap()[(32 + br * 16 + dc) * 128:(33 + br * 16 + dc) * 128, t0:t0 + n],
                              reads=["PTfm"], writes=[gk + str(br)])
                    S.op("act", lambda e, g=g: e.activation(out=g[:, 0:3 * n], in_=g[:, 0:3 * n], func=AF.Sigmoid),
                         reads=[gk + str(i) for i in range(3)], writes=[gk + str(i) for i in range(3)])
                    for br in range(3):
                        p = ps[br]; pk = "ps%d" % br
                        def mm(e, b=b, p=p, br=br):
                            for fc in range(8):
                                r = e.matmul(p[:, 0:n], lhsT=RRm(b[:, br * 1024 + fc * 128:br * 1024 + (fc + 1) * 128]),
                                             rhs=RRm(yt[:, (br * 8 + fc) * n:(br * 8 + fc + 1) * n]), start=(fc == 0), stop=(fc == 7))
                            return r
                        S.op("pe", mm, reads=[bk + str(br), "yt"], writes=[pk])
                        if br == 0:
                            S.op("dve", lambda e, p=p, g=g, dc=dc: e.tensor_tensor(out=RRm(mT[:, dc * n:(dc + 1) * n]), in0=p[:, 0:n], in1=g[:, 0:n], op=ALU.mult),
                                 reads=[pk, gk + "0"], writes=["mT%d" % dc])
                        else:
                            S.op("dve", lambda e, p=p, g=g, br=br: e.tensor_tensor(out=g[:, br * n:(br + 1) * n], in0=p[:, 0:n], in1=g[:, br * n:(br + 1) * n], op=ALU.mult),
                                 reads=[pk, gk + str(br)], writes=[gk + str(br)])
                            S.op("dve", lambda e, g=g, br=br, dc=dc: e.tensor_tensor(out=RRm(mT[:, dc * n:(dc + 1) * n]), in0=mT[:, dc * n:(dc + 1) * n],
                                                                                 in1=g[:, br * n:(br + 1) * n], op=ALU.add),
                                 reads=[gk + str(br), "mT%d" % dc], writes=["mT%d" % dc])
                mk_ = ["mT%d" % dc for dc in range(KC)]
                for fo in range(KC):
                    b = wb[fo % 2]; bk = "wbm%d" % (fo % 2)
                    S.dma("sp", RRm(b[:, 0:2048]), RRm(wo.ap()[l, fo]), writes=[bk + "0", bk + "1"])
                    p = ps[4 + fo % 2]; pk = "ps%d" % (4 + fo % 2)
                    def mm2(e, b=b, p=p):
                        for dc in range(KC):
                            r = e.matmul(p[:, 0:n], lhsT=RRm(b[:, dc * 128:(dc + 1) * 128]), rhs=RRm(mT[:, dc * n:(dc + 1) * n]), start=(dc == 0), stop=(dc == KC - 1))
                        return r
                    S.op("pe", mm2, reads=[bk + "0", bk + "1"] + mk_, writes=[pk])
                    hot = ho[fo % 2]; hk = "hom%d" % (fo % 2)
                    hrt = hr[fo % 2]; rk = "hrm%d" % (fo % 2)
                    S.dma("pool", hrt[:, 0:n], hT.ap()[fo * 128:(fo + 1) * 128, t0:t0 + n], writes=[rk])
                    S.op("dve", lambda e, hot=hot, p=p, fo=fo, hrt=hrt: e.scalar_tensor_tensor(
                        out=hot[:, 0:n], in0=p[:, 0:n], scalar=col(ABv(1, st, 2), fo), in1=hrt[:, 0:n],
                        op0=ALU.mult, op1=ALU.add), reads=[pk, rk, "AB"], writes=[hk])
                    S.dma("pool", hT.ap()[fo * 128:(fo + 1) * 128, t0:t0 + n], hot[:, 0:n], reads=[hk], writes=["hT"])

        def phase_final(es2):
            hx = alloc(es2, "hx", [128, KC * G])
            xn = alloc(es2, "xn", [128, KC * G])
            rstd = alloc(es2, "rstd", [128, G])
            for (t0, n, st) in cfg.groups:
                if st == 1:
                    continue
                load_h(hx, hT, t0, n)
                S.op("act", lambda e: e.activation(out=xn[:, 0:KC * n], in_=hx[:, 0:KC * n], func=AF.Square),
                     reads=["hx"], writes=["xn"])
                def mm(e):
                    for kc in range(KC):
                        r = e.matmul(ps[7][:, 0:n], lhsT=ones, rhs=xn[:, kc * n:(kc + 1) * n], start=(kc == 0), stop=(kc == KC - 1))
                    return r
                S.op("pe", mm, reads=["xn", "cst"], writes=["ps7"])
                S.op("dve", lambda e: e.tensor_scalar(out=rstd[:, 0:n], in0=ps[7][:, 0:n], scalar1=1.0 / D, scalar2=EPS,
                                                      op0=ALU.mult, op1=ALU.add), reads=["ps7"], writes=["rstd"])
                S.op("act", lambda e: e.activation(out=rstd[:, 0:n], in_=rstd[:, 0:n], func=AF.Sqrt), reads=["rstd"], writes=["rstd"])
                S.op("dve", lambda e: e.reciprocal(out=rstd[:, 0:n], in_=rstd[:, 0:n]), reads=["rstd"], writes=["rstd"])
                for kc in range(KC):
                    S.op("dve", lambda e, kc=kc: e.scalar_tensor_tensor(
                        out=xn[:, kc * n:(kc + 1) * n], in0=hx[:, kc * n:(kc + 1) * n], scalar=col(fgt, kc),
                        in1=rstd[:, 0:n], op0=ALU.mult, op1=ALU.mult), reads=["hx", "rstd", "fgt", "xn"], writes=["xn"])
                S.dma("pool", out.ap()[:, t0 - cfg.CTX:t0 - cfg.CTX + n].rearrange("(k p) t -> p k t", p=128),
                      xn[:, 0:KC * n].rearrange("p (k t) -> p k t", k=KC), reads=["xn"], writes=["out"])

        with ExitStack() as es2:
            phase_lb(es2)
            S.barrier()
        for l in range(L):
            last = (l == L - 1)
            with ExitStack() as es2:
                phase_mod(l, es2)
                S.barrier()
            with ExitStack() as es2:
                phase_ffn(l, 0, 0, xT if l == 0 else hT, es2)
                S.barrier()
            if stop_after == "ffn1":
                break
            with ExitStack() as es2:
                phase_proj(l, es2)
                S.barrier()
            if stop_after == "proj":
                break
            for ph in (phase_rglru, phase_hgrn, phase_mlstm, phase_combine):
                with ExitStack() as es2:
                    ph(l, es2)
                    S.barrier()
            with ExitStack() as es2:
                phase_merge(l, es2, last)
                S.barrier()
            with ExitStack() as es2:
                phase_ffn(l, 1, 2, hT, es2, skip_ctx=last)
                S.barrier()
        with ExitStack() as es2:
            phase_final(es2)
            S.barrier()
        print("instructions emitted:", S.ninst)
    return nc


def wblocks(w, ncols=128):
    K, N = w.shape
    return np.ascontiguousarray(w.reshape(K // 128, 128, N // ncols, ncols).transpose(2, 1, 0, 3)).reshape(N // ncols, 128, (K // 128) * ncols)


def pvec(v):
    v = np.asarray(v, np.float32)
    lead = v.shape[:-1]
    r = v.reshape(*lead, v.shape[-1] // 128, 128)
    r = np.moveaxis(r, -1, 0)
    return np.ascontiguousarray(r).reshape(128, -1)


def make_consts():
    c = np.zeros((128, 1024), np.float32)
    c[:, 0:128] = 1.0
    c[:, 128:256] = np.eye(128, dtype=np.float32)
    i = np.arange(128)
    triU = (i[:, None] <= i[None, :]).astype(np.float32)
    triL = (i[:, None] >= i[None, :]).astype(np.float32)
    c[:, 256:384] = triU
    c[:, 384:512] = triL
    lo = (i <= 63).astype(np.float32)[:, None]
    hi = (i >= 64).astype(np.float32)[:, None]
    c[:, 512:640] = triU - lo
    c[:, 640:641] = lo
    c[:, 641:642] = hi
    c[:, 642:770] = triL - hi
    c[:, 770:771] = hi
    c[:, 771:772] = lo
    j = np.arange(32)
    tU = (j[:, None] <= j[None, :]).astype(np.float32)
    tL = (j[:, None] >= j[None, :]).astype(np.float32)
    lo32 = (j <= 15).astype(np.float32)[:, None]
    hi32 = (j >= 16).astype(np.float32)[:, None]
    c[0:32, 772:804] = tU - lo32
    c[0:32, 804:805] = lo32
    c[0:32, 805:806] = hi32
    c[0:32, 806:838] = tL - hi32
    c[0:32, 838:839] = hi32
    c[0:32, 839:840] = lo32
    return c


def prep_shared(inp, cfg):
    L = cfg.DEPTH
    sh = {}
    sh["modw"] = np.stack([wblocks(inp["mod_w"][l]) for l in range(L)])
    sh["modb"] = pvec(inp["mod_b"][:L])
    sh["normg"] = pvec(inp["norm_g"][:L])
    sh["w1t"] = np.stack([np.stack([wblocks(inp["ffn_w_in"][l, w]) for w in range(2)]) for l in range(L)])
    sh["w2t"] = np.stack([np.stack([wblocks(inp["ffn_w_out"][l, w]) for w in range(2)]) for l in range(L)])
    sh["fing"] = pvec(inp["final_g"])
    win = inp["w_in"]
    sh["wpf"] = np.stack([np.concatenate([wblocks(win[l][:, o:o + w]) for (o, w) in FM_COLS]) for l in range(L)])
    sh["wpt"] = np.stack([np.concatenate([wblocks(win[l][:, o:o + w], 256) for (o, w) in TM_COLS]) for l in range(L)])
    sh["wpg"] = np.stack([wblocks(win[l][:, 4096:4112], 16)[0] for l in range(L)])
    sh["consts"] = make_consts()
    sh["convw"] = pvec(inp["conv_w"][:L])
    sh["convb"] = pvec(inp["conv_b"][:L])
    sh["rgb"] = pvec(inp["rg_gate_b"][:L])
    sh["rglam"] = pvec(inp["rg_lambda"][:L])
    rg = inp["rg_gate_w"][:L].reshape(L, 4, 8, 2, 64, 64)
    bd = np.zeros((L, 4, 8, 128, 128), np.float32)
    bd[:, :, :, 0:64, 0:64] = rg[:, :, :, 0]
    bd[:, :, :, 64:128, 64:128] = rg[:, :, :, 1]
    sh["rgw"] = bd
    sh["lblog"] = np.ascontiguousarray(np.broadcast_to(inp["hgrn_lb_logits"][:L].reshape(1, -1), (128, L * 2 * 1024)))
    sh["mgb"] = np.ascontiguousarray(np.broadcast_to(inp["mlstm_gate_b"][:L].reshape(1, -1), (128, L * 16)))
    sh["wbr"] = np.stack([np.stack([wblocks(inp["branch_w"][l, br]) for br in range(3)]) for l in range(L)])
    sh["wo"] = np.stack([wblocks(inp["w_out"][l]) for l in range(L)])
    return sh


def prep_batch(inp, cfg, b):
    m = {}
    m["xT"] = np.ascontiguousarray(np.concatenate([inp["ctx"][b].T, inp["x"][b].T], axis=1))
    cv = np.stack([inp["c"][b], inp["c_ctx"]], axis=-1)
    m["cvec"] = np.ascontiguousarray(cv.reshape(KC, 128, 2).transpose(1, 0, 2)).reshape(128, KC * 2)
    return m


def kernel(**inputs):
    inp = {k: np.asarray(v) for k, v in inputs.items()}
    cfg = Cfg()
    nc = build(cfg)
    sh = prep_shared(inp, cfg)
    in_maps = []
    for b in range(8):
        m = dict(sh)
        m.update(prep_batch(inp, cfg, b))
        in_maps.append(m)
    res = run_bass_kernel_spmd(nc, in_maps, core_ids=list(range(8)))
    out = np.stack([np.ascontiguousarray(res.results[b]["out"].T) for b in range(8)])
    return out.astype(np.float32)
```

```python
import numpy as np
from contextlib import ExitStack
import concourse.bass as bass
import concourse.mybir as mybir
from concourse.bass_utils import run_bass_kernel_spmd

F32 = mybir.dt.float32
AF = mybir.ActivationFunctionType
ALU = mybir.AluOpType

D = 2048
KC = D // 128
DFF = 5632
FC = DFF // 128
NMOD = 9
EPS = 1e-6
INW = 17424
NFMC = 80
NTMB = 28
NTM = NTMB * 256 + 16
FM_COLS = [(0, 1024), (4112, 1024), (9232, 1024), (10256, 1024), (11280, 6144)]
TM_COLS = [(1024, 1024), (2048, 1024), (3072, 1024), (5136, 1024), (6160, 1024), (7184, 1024), (8208, 1024)]


F32R_ = mybir.dt.float32r


class Sched:
    def __init__(self, nc, es, n_dma_sems=6):
        self.nc = nc
        self.eng = {"pe": nc.tensor, "act": nc.scalar, "dve": nc.vector, "pool": nc.gpsimd, "sp": nc.sync}
        self.semh = {}
        self.cnt = {}
        for e in self.eng:
            self.semh[e] = es.enter_context(nc.semaphore("s_" + e))
            self.cnt[e] = 0
        self.dq = {}
        for q in ("sp", "pool", "act"):
            lst = []
            for i in range(n_dma_sems):
                k = "d_%s%d" % (q, i)
                self.semh[k] = es.enter_context(nc.semaphore(k))
                self.cnt[k] = 0
                lst.append(k)
            self.dq[q] = [lst, 0]
        self.known = {e: {} for e in self.eng}
        self.state = {}
        self.ninst = 0

    IGN = frozenset(["PTfm", "Ptm", "HAB", "YT", "hT", "LBD", "out"])

    def _deps(self, reads, writes):
        reads = [k for k in reads if k not in self.IGN]
        writes = [k for k in writes if k not in self.IGN]
        deps = []
        for k in reads:
            st = self.state.get(k)
            if st and st[0]:
                deps.append(st[0])
        for k in writes:
            st = self.state.get(k)
            if st:
                if st[0]:
                    deps.append(st[0])
                deps.extend(st[1].items())
        return deps

    def _wait(self, e, deps):
        kn = self.known[e]
        for (sk, val) in deps:
            if kn.get(sk, 0) < val:
                self.eng[e].wait_ge(self.semh[sk], val)
                kn[sk] = val

    def _commit(self, tok, reads, writes):
        reads = [k for k in reads if k not in self.IGN]
        writes = [k for k in writes if k not in self.IGN]
        for k in writes:
            self.state[k] = [tok, {}]
        for k in reads:
            if k in writes:
                continue
            st = self.state.setdefault(k, [None, {}])
            if st[1].get(tok[0], 0) < tok[1]:
                st[1][tok[0]] = tok[1]

    def op(self, e, emit, reads=(), writes=()):
        self._wait(e, self._deps(reads, writes))
        ins = emit(self.eng[e])
        self.cnt[e] += 1
        ins.then_inc(self.semh[e], 1)
        self._commit((e, self.cnt[e]), reads, writes)
        self.ninst += 1

    def dma(self, q, out, in_, reads=(), writes=()):
        lst, rr = self.dq[q]
        sk = lst[rr % len(lst)]
        self.dq[q][1] = rr + 1
        deps = self._deps(reads, writes)
        if self.cnt[sk] > 0:
            deps.append((sk, self.cnt[sk]))
        self._wait(q, deps)
        if out.dtype == F32R_:
            self.nc.dge_precook = False
            ins = self.eng[q].dma_start(out=out, in_=in_)
            self.nc.dge_precook = True
        else:
            ins = self.eng[q].dma_start(out=out, in_=in_)
        self.cnt[sk] += 16
        ins.then_inc(self.semh[sk], 16)
        self._commit((sk, self.cnt[sk]), reads, writes)
        self.ninst += 1

    def barrier(self):
        toks = [(k, v) for k, v in self.cnt.items() if v > 0]
        for e in self.eng:
            self._wait(e, toks)
        self.state = {}

    def finish(self, keys):
        self.barrier()


class Cfg:
    def __init__(self, SEQ=4096, CTX=256, DEPTH=4, G=512):
        self.SEQ, self.CTX, self.DEPTH, self.G = SEQ, CTX, DEPTH, G
        self.T = SEQ + CTX
        self.rows = SEQ // 64
        self.groups = [(0, CTX, 1)] + [(CTX + i * G, G, 0) for i in range(SEQ // G)]
        self.G2 = min(1024, SEQ) if SEQ % 1024 == 0 or SEQ < 1024 else G
        self.groups2 = [(0, CTX, 1)] + [(CTX + i * self.G2, self.G2, 0) for i in range(SEQ // self.G2)]


F32R = mybir.dt.float32r


def RR(ap):
    return ap.bitcast(F32R)


import os
FLAGS = {"proj": os.environ.get("RR_PROJ", "1") == "1", "merge": os.environ.get("RR_MERGE", "1") == "1"}


def RRp(ap):
    return RR(ap) if FLAGS["proj"] else ap


def RRm(ap):
    return RR(ap) if FLAGS["merge"] else ap


def col(t, i):
    return t[:, i:i + 1]


def build(cfg, stop_after=None):
    nc = bass.Bass("TRN2", target_bir_lowering=False)
    L, T, G = cfg.DEPTH, cfg.T, cfg.G
    dt = nc.dram_tensor
    I = {}
    def inp(name, shape):
        I[name] = dt(name, list(shape), F32, kind="ExternalInput")
        return I[name]
    xT = inp("xT", [D, T])
    cvec = inp("cvec", [128, KC * 2])
    modw = inp("modw", [L, 144, 128, KC * 128])
    modb = inp("modb", [128, L * 144])
    normg = inp("normg", [128, L * 3 * KC])
    w1t = inp("w1t", [L, 2, 88, 128, KC * 128])
    w2t = inp("w2t", [L, 2, KC, 128, FC * 128])
    fing = inp("fing", [128, KC])
    wpf = inp("wpf", [L, NFMC, 128, KC * 128])
    wpt = inp("wpt", [L, NTMB, 128, KC * 256])
    wpg = inp("wpg", [L, 128, KC * 16])
    convw = inp("convw", [128, L * 4 * 8])
    convb = inp("convb", [128, L * 8])
    rgb = inp("rgb", [128, L * 4 * 8])
    rglam = inp("rglam", [128, L * 2 * 8])
    rgw = inp("rgw", [L, 4, 8, 128, 128])
    lblog = inp("lblog", [128, L * 2 * 1024])
    mgb = inp("mgb", [128, L * 16])
    wbr = inp("wbr", [L, 3, KC, 128, 8 * 128])
    wo = inp("wo", [L, KC, 128, KC * 128])
    YT = dt("YT", [3 * 1024, T], F32, kind="Internal")
    HAB = dt("HAB", [4, T, 1024], F32, kind="Internal")
    LBD = dt("LBD", [L * 2, 128, 1024], F32, kind="Internal")
    PTfm = dt("PTfm", [NFMC * 128, T], F32, kind="Internal")
    Ptm = dt("Ptm", [T, NTM], F32, kind="Internal")
    consts = inp("consts", [128, 1024])
    out = dt("out", [D, cfg.SEQ], F32, kind="ExternalOutput")
    hT = dt("hT", [D, T], F32, kind="Internal")

    es = ExitStack()
    with es:
        S = Sched(nc, es)
        uid = [0]
        def alloc(stack, name, shape):
            uid[0] += 1
            return stack.enter_context(nc.sbuf_tensor("%s_%d" % (name, uid[0]), list(shape), F32))
        sb = lambda name, shape: alloc(es, name, shape)
        cst = sb("cst", [128, 1024])
        ones = cst[:, 0:128]
        ident = cst[:, 128:256]
        sv = sb("sv", [128, KC * 2])
        modt = sb("modt", [128, 144 * 2])
        modbt = sb("modbt", [128, L * 144])
        ngt = sb("ngt", [128, L * 3 * KC])
        fgt = sb("fgt", [128, KC])
        AB = sb("AB", [128, 3 * 2 * 3 * KC])
        ps = [es.enter_context(nc.psum_tensor("ps%d" % i, [128, 512], F32)) for i in range(8)]

        S.dma("sp", RR(cst[:]), RR(consts.ap()), writes=["cst"])
        S.dma("sp", sv[:], cvec.ap(), writes=["sv"])
        S.dma("sp", modbt[:], modb.ap(), writes=["modbt"])
        S.dma("sp", ngt[:], normg.ap(), writes=["ngt"])
        S.dma("sp", fgt[:], fing.ap(), writes=["fgt"])
        S.op("act", lambda e: e.activation(out=sv[:], in_=sv[:], func=AF.Silu), reads=["sv"], writes=["sv"])

        def ABv(s, stream, which):
            o = ((s * 2 + stream) * 3 + which) * KC
            return AB[:, o:o + KC]

        def phase_mod(l, es2):
            wb = [alloc(es2, "mw%d" % i, [128, KC * 128]) for i in range(2)]
            for c in range(144):
                b = wb[c % 2]
                S.dma("sp", b[:], modw.ap()[l, c], writes=["mw%d" % (c % 2)])
                p = ps[c % 2]
                def mm(e, b=b, p=p):
                    for kc in range(KC):
                        r = e.matmul(p[:, 0:2], lhsT=b[:, kc * 128:(kc + 1) * 128], rhs=sv[:, 2 * kc:2 * kc + 2],
                                     start=(kc == 0), stop=(kc == KC - 1))
                    return r
                S.op("pe", mm, reads=["mw%d" % (c % 2), "sv"], writes=["ps%d" % (c % 2)])
                S.op("dve", lambda e, p=p, c=c: e.tensor_scalar(out=modt[:, 2 * c:2 * c + 2], in0=p[:, 0:2],
                                                                scalar1=col(modbt, l * 144 + c), scalar2=None, op0=ALU.add),
                     reads=["ps%d" % (c % 2), "modbt"], writes=["modt"])
            mv = modt[:].rearrange("p (m k s) -> p m k s", m=NMOD, k=KC)
            for s in range(3):
                for st in range(2):
                    g = ngt[:, (l * 3 + s) * KC:(l * 3 + s + 1) * KC]
                    S.op("dve", lambda e, s=s, st=st, g=g: e.scalar_tensor_tensor(
                        out=ABv(s, st, 0), in0=mv[:, 3 * s + 1, :, st], scalar=1.0, in1=g, op0=ALU.add, op1=ALU.mult),
                        reads=["modt", "ngt"], writes=["AB"])
                    S.op("dve", lambda e, s=s, st=st: e.tensor_copy(out=ABv(s, st, 1), in_=mv[:, 3 * s, :, st]),
                         reads=["modt"], writes=["AB"])
                    S.op("dve", lambda e, s=s, st=st: e.tensor_scalar(
                        out=ABv(s, st, 2), in0=mv[:, 3 * s + 2, :, st], scalar1=(1.0 if s == 1 else 0.5), scalar2=None,
                        op0=ALU.mult), reads=["modt"], writes=["AB"])

        def adaln(hx, xn, rstd, n, s, st, pss, tag, sqkeys=("sqb",)):
            S.op("act", lambda e: e.activation(out=RR(xn[:, 0:KC * n]), in_=hx[:, 0:KC * n], func=AF.Square),
                 reads=["hx"], writes=list(sqkeys))
            plist = pss if isinstance(pss, (list, tuple)) else [pss]
            tags = tag if isinstance(tag, (list, tuple)) else [tag]
            for hi, h0 in enumerate(range(0, n, 512)):
                w_ = min(512, n - h0)
                def mm(e, hi=hi, h0=h0, w_=w_):
                    for kc in range(KC):
                        r = e.matmul(plist[hi][:, 0:w_], lhsT=RR(ones), rhs=RR(xn[:, kc * n + h0:kc * n + h0 + w_]), start=(kc == 0), stop=(kc == KC - 1))
                    return r
                S.op("pe", mm, reads=list(sqkeys) + ["cst"], writes=[tags[hi]])
                S.op("dve", lambda e, hi=hi, h0=h0, w_=w_: e.tensor_scalar(out=rstd[:, h0:h0 + w_], in0=plist[hi][:, 0:w_], scalar1=1.0 / D, scalar2=EPS,
                                                                      op0=ALU.mult, op1=ALU.add), reads=[tags[hi], "rstd"], writes=["rstd"])
            S.op("act", lambda e: e.activation(out=rstd[:, 0:n], in_=rstd[:, 0:n], func=AF.Sqrt), reads=["rstd"], writes=["rstd"])
            S.op("dve", lambda e: e.reciprocal(out=rstd[:, 0:n], in_=rstd[:, 0:n]), reads=["rstd"], writes=["rstd"])
            for kc in range(KC):
                S.op("dve", lambda e, kc=kc: e.scalar_tensor_tensor(
                    out=RR(hx[:, kc * n:(kc + 1) * n]), in0=hx[:, kc * n:(kc + 1) * n], scalar=col(ABv(s, st, 0), kc),
                    in1=rstd[:, 0:n], op0=ALU.mult, op1=ALU.mult), reads=["hx", "rstd", "AB"], writes=["xn%d" % kc])
                S.op("act", lambda e, kc=kc: e.activation(
                    out=RR(hx[:, kc * n:(kc + 1) * n]), in_=hx[:, kc * n:(kc + 1) * n], func=AF.Identity,
                    bias=col(ABv(s, st, 1), kc)), reads=["xn%d" % kc, "AB"], writes=["xn%d" % kc])
            return ["xn%d" % kc for kc in range(KC)]

        def load_h(hx, src, t0, n):
            S.dma("pool", RR(hx[:, 0:KC * n]).rearrange("p (k t) -> p k t", k=KC),
                  RR(src.ap()[:, t0:t0 + n]).rearrange("(k p) t -> p k t", p=128), reads=["hT"], writes=["hx"] + ["xn%d" % kc for kc in range(KC)])

        def phase_ffn(l, w, s, src, es2, skip_ctx=False):
            hx = alloc(es2, "hx", [128, KC * G])
            act = alloc(es2, "actb", [128, FC * G])
            rstd = alloc(es2, "rstd", [128, G])
            sg = [alloc(es2, "sg%d" % i, [128, G]) for i in range(2)]
            ho = [alloc(es2, "ho%d" % i, [128, G]) for i in range(2)]
            hr = [alloc(es2, "hr%d" % i, [128, G]) for i in range(2)]
            wb = [alloc(es2, "wb%d" % i, [128, FC * 128]) for i in range(2)]
            xn = hx
            for (t0, n, st) in cfg.groups:
                if skip_ctx and st == 1:
                    continue
                load_h(hx, src, t0, n)
                xk = adaln(hx, act, rstd, n, s, st, ps[7], "ps7", sqkeys=["act%d" % j for j in range(KC)])
                for j in range(FC):
                    b = wb[j % 2]; bk = "wb%d" % (j % 2)
                    S.dma("sp", RR(b[:, 0:2048]), RR(w1t.ap()[l, w, j]), writes=[bk + "a"])
                    S.dma("sp", RR(b[:, 2048:4096]), RR(w1t.ap()[l, w, FC + j]), writes=[bk + "b"])
                    pg, pu = ps[(j % 2) * 2], ps[(j % 2) * 2 + 1]
                    kg, ku = "ps%d" % ((j % 2) * 2), "ps%d" % ((j % 2) * 2 + 1)
                    def mm(e, b=b, p=pg, o=0):
                        for kc in range(KC):
                            r = e.matmul(p[:, 0:n], lhsT=RR(b[:, o + kc * 128:o + (kc + 1) * 128]), rhs=RR(xn[:, kc * n:(kc + 1) * n]),
                                         start=(kc == 0), stop=(kc == KC - 1))
                        return r
                    S.op("pe", mm, reads=[bk + "a"] + xk, writes=[kg])
                    S.op("pe", lambda e, b=b, pu=pu: mm(e, b, pu, 2048), reads=[bk + "b"] + xk, writes=[ku])
                    sgt = sg[j % 2]; sk = "sg%d" % (j % 2)
                    S.op("act", lambda e, sgt=sgt, pg=pg: e.activation(out=sgt[:, 0:n], in_=pg[:, 0:n], func=AF.Silu),
                         reads=[kg], writes=[sk])
                    S.op("dve", lambda e, sgt=sgt, pu=pu, j=j: e.tensor_tensor(out=RR(act[:, j * n:(j + 1) * n]), in0=pu[:, 0:n],
                                                                           in1=sgt[:, 0:n], op=ALU.mult),
                         reads=[sk, ku], writes=["act%d" % j])
                ak = ["act%d" % j for j in range(FC)]
                for fo in range(KC):
                    b = wb[fo % 2]; bk = "wb%d" % (fo % 2)
                    S.dma("sp", RR(b[:]), RR(w2t.ap()[l, w, fo]), writes=[bk + "a", bk + "b"])
                    hrt = hr[fo % 2]; rk = "hr%d" % (fo % 2)
                    S.dma("pool", hrt[:, 0:n], src.ap()[fo * 128:(fo + 1) * 128, t0:t0 + n], writes=[rk])
                    p = ps[4 + fo % 2]; pk = "ps%d" % (4 + fo % 2)
                    def mm2(e, b=b, p=p):
                        for j in range(FC):
                            r = e.matmul(p[:, 0:n], lhsT=RR(b[:, j * 128:(j + 1) * 128]), rhs=RR(act[:, j * n:(j + 1) * n]),
                                         start=(j == 0), stop=(j == FC - 1))
                        return r
                    S.op("pe", mm2, reads=[bk + "a", bk + "b"] + ak, writes=[pk])
                    hot = ho[fo % 2]; hk = "ho%d" % (fo % 2)
                    S.op("dve", lambda e, hot=hot, p=p, fo=fo, hrt=hrt: e.scalar_tensor_tensor(
                        out=hot[:, 0:n], in0=p[:, 0:n], scalar=col(ABv(s, st, 2), fo), in1=hrt[:, 0:n],
                        op0=ALU.mult, op1=ALU.add), reads=[pk, rk, "AB"], writes=[hk])
                    S.dma("pool", hT.ap()[fo * 128:(fo + 1) * 128, t0:t0 + n], hot[:, 0:n], reads=[hk], writes=["hT"])

        def phase_proj(l, es2):
            G2 = cfg.G2
            hx = alloc(es2, "hx", [128, KC * G2])
            xn = alloc(es2, "xn", [128, KC * G2])
            rstd = alloc(es2, "rstd", [128, G2])
            wb = [alloc(es2, "wb%d" % i, [128, KC * 256]) for i in range(2)]
            wg = alloc(es2, "wg", [128, KC * 16])
            so = [alloc(es2, "so%d" % i, [128, max(G2, 256)]) for i in range(2)]
            S.dma("sp", wg[:], wpg.ap()[l], writes=["wg"])
            for (t0, n, st) in cfg.groups2:
                load_h(hx, hT, t0, n)
                xk = adaln(hx, xn, rstd, n, 1, st, [ps[7], ps[6]], ["ps7", "ps6"])
                xn_ = hx
                it = 0
                for c in range(NFMC):
                    b = wb[it % 2]; bk = "wb%d" % (it % 2)
                    S.dma("sp", RRp(b[:, 0:2048]), RRp(wpf.ap()[l, c]), writes=[bk])
                    sot = so[it % 2]; sk = "so%d" % (it % 2)
                    for hi, h0 in enumerate(range(0, n, 512)):
                        w_ = min(512, n - h0)
                        p = ps[(it % 2) * 2 + hi]; pk = "ps%d" % ((it % 2) * 2 + hi)
                        def mm(e, b=b, p=p, h0=h0, w_=w_):
                            for kc in range(KC):
                                r = e.matmul(p[:, 0:w_], lhsT=RRp(b[:, kc * 128:(kc + 1) * 128]), rhs=RRp(xn_[:, kc * n + h0:kc * n + h0 + w_]),
                                             start=(kc == 0), stop=(kc == KC - 1))
                            return r
                        S.op("pe", mm, reads=[bk] + xk, writes=[pk])
                        S.op("act", lambda e, sot=sot, p=p, h0=h0, w_=w_: e.activation(out=sot[:, h0:h0 + w_], in_=p[:, 0:w_], func=AF.Copy),
                             reads=[pk, sk], writes=[sk])
                    S.dma("pool", PTfm.ap()[c * 128:(c + 1) * 128, t0:t0 + n], sot[:, 0:n], reads=[sk], writes=["PTfm"])
                    it += 1
                for cb in range(NTMB + 1):
                    gate = (cb == NTMB)
                    ncol = 16 if gate else 256
                    if gate:
                        b = wg; bk = "wg"
                    else:
                        b = wb[it % 2]; bk = "wb%d" % (it % 2)
                        S.dma("sp", RRp(b[:]), RRp(wpt.ap()[l, cb]), writes=[bk])
                    for ts in range(n // 128):
                        p = ps[4 + it % 2]; pk = "ps%d" % (4 + it % 2)
                        def mm(e, b=b, p=p, ts=ts, ncol=ncol):
                            for kc in range(KC):
                                if ncol >= 256:
                                    r = e.matmul(p[:, 0:ncol], lhsT=RRp(xn_[:, kc * n + ts * 128:kc * n + (ts + 1) * 128]),
                                                 rhs=RRp(b[:, kc * ncol:(kc + 1) * ncol]), start=(kc == 0), stop=(kc == KC - 1))
                                else:
                                    r = e.matmul(p[:, 0:ncol], lhsT=xn_[:, kc * n + ts * 128:kc * n + (ts + 1) * 128],
                                                 rhs=b[:, kc * ncol:(kc + 1) * ncol], start=(kc == 0), stop=(kc == KC - 1))
                            return r
                        S.op("pe", mm, reads=[bk] + xk, writes=[pk])
                        sot = so[it % 2]; sk = "so%d" % (it % 2)
                        S.op("act", lambda e, sot=sot, p=p, ncol=ncol: e.activation(out=sot[:, 0:ncol], in_=p[:, 0:ncol], func=AF.Copy),
                             reads=[pk], writes=[sk])
                        S.dma("pool", Ptm.ap()[t0 + ts * 128:t0 + (ts + 1) * 128, cb * 256:cb * 256 + ncol], sot[:, 0:ncol],
                              reads=[sk], writes=["Ptm"])
                        it += 1

        def rev(ap2d, n):
            a = ap2d.ap
            return bass.AP(ap2d.tensor, ap2d.offset + (n - 1) * a[-1][0], [list(a[0]), [-a[-1][0], n]])

        def chunk_order(d):
            nct, nlt = cfg.CTX // 128, cfg.SEQ // 128
            if d == 0:
                return list(range(nct + nlt))
            return list(range(nct - 1, -1, -1)) + list(range(nct + nlt - 1, nct - 1, -1))

        def phase_lb(es2):
            lg = alloc(es2, "lg", [128, L * 2048])
            sm = alloc(es2, "sm", [128, 2048])
            cum = alloc(es2, "cum", [128, 2048])
            S.dma("sp", lg[:], lblog.ap(), writes=["lg"])
            S.op("dve", lambda e: e.tensor_copy(out=sm[:], in_=lg[:, 0:2048]), reads=["lg"], writes=["sm"])
            for k in range(1, L):
                S.op("dve", lambda e, k=k: e.tensor_tensor(out=sm[:], in0=sm[:], in1=lg[:, k * 2048:(k + 1) * 2048], op=ALU.max),
                     reads=["lg", "sm"], writes=["sm"])
            for k in range(L):
                S.op("dve", lambda e, k=k: e.tensor_tensor(out=lg[:, k * 2048:(k + 1) * 2048], in0=lg[:, k * 2048:(k + 1) * 2048], in1=sm[:],
                                                           op=ALU.subtract), reads=["lg", "sm"], writes=["lg"])
            S.op("act", lambda e: e.activation(out=lg[:], in_=lg[:], func=AF.Exp), reads=["lg"], writes=["lg"])
            S.op("dve", lambda e: e.tensor_copy(out=sm[:], in_=lg[:, 0:2048]), reads=["lg", "sm"], writes=["sm"])
            for k in range(1, L):
                S.op("dve", lambda e, k=k: e.tensor_tensor(out=sm[:], in0=sm[:], in1=lg[:, k * 2048:(k + 1) * 2048], op=ALU.add),
                     reads=["lg", "sm"], writes=["sm"])
            S.op("dve", lambda e: e.reciprocal(out=sm[:], in_=sm[:]), reads=["sm"], writes=["sm"])
            S.op("pool", lambda e: e.memset(cum[:], 0.0), writes=["cum"])
            for l in range(L):
                if l > 0:
                    S.op("dve", lambda e, l=l: e.tensor_tensor(out=lg[:, l * 2048:(l + 1) * 2048], in0=lg[:, l * 2048:(l + 1) * 2048],
                                                               in1=sm[:], op=ALU.mult), reads=["lg", "sm"], writes=["lg"])
                    S.op("dve", lambda e, l=l: e.tensor_tensor(out=cum[:], in0=cum[:], in1=lg[:, l * 2048:(l + 1) * 2048], op=ALU.add),
                         reads=["lg", "cum"], writes=["cum"])
                S.op("dve", lambda e, l=l: e.tensor_scalar(out=lg[:, l * 2048:(l + 1) * 2048], in0=cum[:], scalar1=-1.0, scalar2=1.0,
                                                           op0=ALU.mult, op1=ALU.add), reads=["cum", "lg"], writes=["lg"])
                for d in range(2):
                    S.dma("sp", LBD.ap()[l * 2 + d], lg[:, l * 2048 + d * 1024:l * 2048 + (d + 1) * 1024], reads=["lg"], writes=["LBD"])

        def phase_rglru(l, es2):
            T_, CT, SQ = cfg.T, cfg.CTX, cfg.SEQ
            A_, B_, C_, D_, E_, F_ = [alloc(es2, "rb%d" % i, [128, T_ + 8]) for i in range(6)]
            bdw = alloc(es2, "bdw", [128, 4 * 128])
            cw = alloc(es2, "cw", [128, L * 32]); cb = alloc(es2, "cb", [128, L * 8])
            gb = alloc(es2, "gb", [128, L * 32]); lam = alloc(es2, "lam", [128, L * 16])
            S.dma("sp", cw[:], convw.ap(), writes=["cw"]); S.dma("sp", cb[:], convb.ap(), writes=["cb"])
            S.dma("sp", gb[:], rgb.ap(), writes=["gb"]); S.dma("sp", lam[:], rglam.ap(), writes=["lam"])
            S.op("act", lambda e: e.activation(out=lam[:], in_=lam[:], func=AF.Exp, scale=-1.0), reads=["lam"], writes=["lam"])
            S.op("dve", lambda e: e.tensor_scalar(out=lam[:], in0=lam[:], scalar1=1.0, scalar2=None, op0=ALU.add), reads=["lam"], writes=["lam"])
            S.op("act", lambda e: e.activation(out=lam[:], in_=lam[:], func=AF.Ln), reads=["lam"], writes=["lam"])
            S.op("dve", lambda e: e.tensor_scalar(out=lam[:], in0=lam[:], scalar1=-8.0, scalar2=None, op0=ALU.mult), reads=["lam"], writes=["lam"])
            for cc in range(8):
                S.dma("sp", A_[:, 0:T_], PTfm.ap()[(16 + cc) * 128:(17 + cc) * 128, :], reads=["PTfm"], writes=["rA"])
                S.dma("sp", F_[:, 0:T_], PTfm.ap()[(24 + cc) * 128:(25 + cc) * 128, :], reads=["PTfm"], writes=["rF"])
                S.dma("sp", bdw[:].rearrange("p (a m) -> p a m", a=4), rgw.ap()[l, :, cc].rearrange("a p m -> p a m"), writes=["bdw"])
                S.op("pool", lambda e: e.memset(B_[:], 0.0), writes=["rB"])
                S.op("dve", lambda e: e.tensor_copy(out=B_[:, 2:2 + CT], in_=A_[:, 0:CT]), reads=["rA", "rB"], writes=["rB"])
                S.op("dve", lambda e: e.tensor_copy(out=B_[:, CT + 5:CT + 5 + SQ].rearrange("p (w r) -> p w r", w=64),
                                                    in_=A_[:, CT:T_].rearrange("p (r w) -> p w r", w=64)), reads=["rA", "rB"], writes=["rB"])
                for (oo, oi, n) in ((0, 0, CT), (CT, CT + 3, SQ)):
                    for j in range(4):
                        wj = col(cw, (l * 4 + j) * 8 + cc)
                        if j == 0:
                            S.op("dve", lambda e, oo=oo, oi=oi, n=n, wj=wj: e.tensor_scalar(
                                out=C_[:, oo:oo + n], in0=B_[:, oi:oi + n], scalar1=wj, scalar2=col(cb, l * 8 + cc),
                                op0=ALU.mult, op1=ALU.add), reads=["rB", "cw", "cb"], writes=["rC"])
                        else:
                            S.op("dve", lambda e, oo=oo, oi=oi, n=n, wj=wj, j=j: e.scalar_tensor_tensor(
                                out=C_[:, oo:oo + n], in0=B_[:, oi + j:oi + j + n], scalar=wj, in1=C_[:, oo:oo + n],
                                op0=ALU.mult, op1=ALU.add), reads=["rB", "rC", "cw"], writes=["rC"])
                for d in range(2):
                    for ri, dst, dk in ((0, A_, "rA"), (1, B_, "rB")):
                        for i, tt in enumerate(range(0, T_, 512)):
                            n = min(512, T_ - tt)
                            p = ps[i % 2]; pk = "ps%d" % (i % 2)
                            S.op("pe", lambda e, p=p, tt=tt, n=n, ri=ri: e.matmul(
                                p[:, 0:n], lhsT=bdw[:, (d * 2 + ri) * 128:(d * 2 + ri + 1) * 128], rhs=C_[:, tt:tt + n], start=True, stop=True),
                                reads=["bdw", "rC"], writes=[pk])
                            S.op("act", lambda e, p=p, tt=tt, n=n, ri=ri, dst=dst: e.activation(
                                out=dst[:, tt:tt + n], in_=p[:, 0:n], func=AF.Sigmoid, bias=col(gb, ((l * 2 + d) * 2 + ri) * 8 + cc)),
                                reads=[pk, "gb"], writes=[dk])
                    S.op("act", lambda e: e.activation(out=A_[:, 0:T_], in_=A_[:, 0:T_], func=AF.Exp, scale=col(lam, (l * 2 + d) * 8 + cc)),
                         reads=["rA", "lam"], writes=["rA"])
                    S.op("dve", lambda e: e.tensor_tensor(out=B_[:, 0:T_], in0=B_[:, 0:T_], in1=C_[:, 0:T_], op=ALU.mult),
                         reads=["rB", "rC"], writes=["rB"])
                    hd, hk = (D_, "rD") if d == 0 else (E_, "rE")
                    S.op("pool", lambda e, hd=hd: e.tensor_tensor(out=hd[:, 0:T_], in0=A_[:, 0:T_], in1=A_[:, 0:T_], op=ALU.mult),
                         reads=["rA"], writes=[hk])
                    S.op("dve", lambda e, hd=hd: e.tensor_scalar(out=hd[:, 0:T_], in0=hd[:, 0:T_], scalar1=-1.0, scalar2=1.0,
                                                                 op0=ALU.mult, op1=ALU.add), reads=[hk], writes=[hk])
                    S.op("act", lambda e, hd=hd: e.activation(out=hd[:, 0:T_], in_=hd[:, 0:T_], func=AF.Sqrt), reads=[hk], writes=[hk])
                    S.op("dve", lambda e, hd=hd: e.tensor_tensor(out=B_[:, 0:T_], in0=B_[:, 0:T_], in1=hd[:, 0:T_], op=ALU.mult),
                         reads=["rB", hk], writes=["rB"])
                    if d == 0:
                        S.op("dve", lambda e: e.tensor_tensor_scan(out=D_[:, 0:T_], data0=A_[:, 0:T_], data1=B_[:, 0:T_], initial=0.0,
                                                                   op0=ALU.mult, op1=ALU.add), reads=["rA", "rB", "rD"], writes=["rD"])
                    else:
                        S.op("dve", lambda e: e.tensor_tensor_scan(out=rev(E_[:, 0:CT], CT), data0=rev(A_[:, 0:CT], CT),
                                                                   data1=rev(B_[:, 0:CT], CT), initial=0.0, op0=ALU.mult, op1=ALU.add),
                             reads=["rA", "rB", "rE"], writes=["rE"])
                        S.op("dve", lambda e: e.tensor_tensor_scan(out=rev(E_[:, CT:T_], SQ), data0=rev(A_[:, CT:T_], SQ),
                                                                   data1=rev(B_[:, CT:T_], SQ), initial=E_[:, 0:1], op0=ALU.mult, op1=ALU.add),
                             reads=["rA", "rB", "rE"], writes=["rE"])
                S.op("dve", lambda e: e.tensor_tensor(out=D_[:, 0:T_], in0=D_[:, 0:T_], in1=E_[:, 0:T_], op=ALU.add), reads=["rD", "rE"], writes=["rD"])
                S.op("act", lambda e: e.activation(out=F_[:, 0:T_], in_=F_[:, 0:T_], func=AF.Gelu), reads=["rF"], writes=["rF"])
                S.op("dve", lambda e: e.tensor_tensor(out=E_[:, 0:CT], in0=D_[:, 0:CT], in1=F_[:, 0:CT], op=ALU.mult), reads=["rD", "rF", "rE"], writes=["rE"])
                S.op("dve", lambda e: e.tensor_tensor(out=E_[:, CT:T_].rearrange("p (r w) -> p r w", w=64),
                                                      in0=D_[:, CT:T_].rearrange("p (w r) -> p r w", w=64),
                                                      in1=F_[:, CT:T_].rearrange("p (r w) -> p r w", w=64), op=ALU.mult),
                     reads=["rD", "rF", "rE"], writes=["rE"])
                S.dma("pool", YT.ap()[2048 + cc * 128:2048 + (cc + 1) * 128, :], E_[:, 0:T_], reads=["rE"], writes=["YT"])

        def ap3(t2d, off, dims, npart=128):
            return bass.AP(t2d.tensor, t2d.offset + off, [[t2d.ap[0][0], npart]] + [list(x) for x in dims])

        def phase_hgrn(l, es2):
            P2 = lambda name, shape: [alloc(es2, "%s%d" % (name, i), shape) for i in range(2)]
            ftm = P2("ftm", [32, 1024]); ktm = P2("ktm", [32, 1024]); gtm = P2("gtm", [32, 1024]); vtm = P2("vtm", [32, 1024])
            otl = P2("otl", [32, 1024]); qt = P2("qt", [128, 256]); ex = P2("ex", [128, 272]); ktT = P2("ktT", [128, 256])
            attm = P2("attm", [32, 256]); qfm = P2("qfm", [128, 1024])
            oml = alloc(es2, "oml", [128, 1024]); Sst = alloc(es2, "Sst", [128, 1024]); Sp = alloc(es2, "Sp", [128, 1024])
            id32 = cst[0:32, 128:160]
            items = []
            for d in range(2):
                for cn, ci in enumerate(chunk_order(d)):
                    for sn, sj in enumerate(range(4) if d == 0 else range(3, -1, -1)):
                        items.append((d, ci, sj, cn == 0 and sn == 0, sn == 0))
            cpar = [0]

            def stageA(it, p):
                d, ci, sj, first_of_dir, first_of_chunk = it
                RX = cst[0:32, 772:806] if d == 0 else cst[0:32, 806:840]
                TR = RX[:, 0:32]
                MK = cst[0:32, 256:288] if d == 0 else cst[0:32, 384:416]
                k = lambda nm: "%s%d" % (nm, p)
                if first_of_dir:
                    S.dma("sp", oml[:], LBD.ap()[l * 2 + d], reads=["LBD"], writes=["oml"])
                if first_of_chunk:
                    cpar[0] ^= 1
                    q = qfm[cpar[0]]; qk = "qfm%d" % cpar[0]
                    S.dma("pool", q[:].rearrange("p (h t) -> p h t", h=8),
                          PTfm.ap()[1024:2048, ci * 128:(ci + 1) * 128].rearrange("(h p) t -> p h t", p=128), reads=["PTfm"], writes=[qk])
                    S.op("act", lambda e: e.activation(out=q[:], in_=q[:], func=AF.Silu), reads=[qk], writes=[qk])
                q = qfm[cpar[0]]; qk = "qfm%d" % cpar[0]
                t0 = ci * 128 + sj * 32
                f_, k_, g_, v_, x_, qt_, kT_, at_ = ftm[p], ktm[p], gtm[p], vtm[p], ex[p], qt[p], ktT[p], attm[p]
                S.dma("sp", f_[:], Ptm.ap()[t0:t0 + 32, (3 + d) * 1024:(4 + d) * 1024], reads=["Ptm"], writes=[k("ftm")])
                S.dma("sp", v_[:], Ptm.ap()[t0:t0 + 32, 5 * 1024:6 * 1024], reads=["Ptm"], writes=[k("vtm")])
                S.op("act", lambda e: e.activation(out=f_[:], in_=f_[:], func=AF.Sigmoid, scale=-1.0), reads=[k("ftm")], writes=[k("ftm")])
                S.op("dve", lambda e: e.tensor_tensor(out=k_[:], in0=f_[:], in1=oml[0:32, :], op=ALU.mult), reads=[k("ftm"), "oml"], writes=[k("ktm")])
                S.op("dve", lambda e: e.tensor_scalar(out=g_[:], in0=k_[:], scalar1=-1.0, scalar2=1.0, op0=ALU.mult, op1=ALU.add),
                     reads=[k("ktm")], writes=[k("gtm")])
                S.op("act", lambda e: e.activation(out=g_[:], in_=g_[:], func=AF.Ln), reads=[k("gtm")], writes=[k("gtm")])
                for hf in range(2):
                    S.op("pe", lambda e, hf=hf: e.matmul(ps[hf][0:32, 0:512], lhsT=TR, rhs=g_[:, hf * 512:(hf + 1) * 512], start=True, stop=True),
                         reads=[k("gtm"), "cst"], writes=["ps%d" % hf])
                    S.op("act", lambda e, hf=hf: e.activation(out=f_[:, hf * 512:(hf + 1) * 512], in_=ps[hf][0:32, 0:512], func=AF.Exp, scale=-1.0),
                         reads=["ps%d" % hf], writes=[k("ftm")])
                S.op("dve", lambda e: e.tensor_tensor(out=k_[:], in0=k_[:], in1=f_[:], op=ALU.mult), reads=[k("ktm"), k("ftm")], writes=[k("ktm")])
                def mmrx(e):
                    for h in range(8):
                        r = e.matmul(ps[2][:, h * 34:(h + 1) * 34], lhsT=g_[:, h * 128:(h + 1) * 128], rhs=RX, start=True, stop=True)
                    return r
                S.op("pe", mmrx, reads=[k("gtm"), "cst"], writes=["ps2"])
                S.op("act", lambda e: e.activation(out=x_[:], in_=ps[2][:, 0:272], func=AF.Exp), reads=["ps2"], writes=[k("ex")])
                S.op("dve", lambda e: e.tensor_tensor(out=qt_[:].rearrange("p (h t) -> p h t", h=8),
                                                      in0=q[:].rearrange("p (h t) -> p h t", h=8)[:, :, sj * 32:(sj + 1) * 32],
                                                      in1=x_[:].rearrange("p (h t) -> p h t", h=8)[:, :, 0:32], op=ALU.mult),
                     reads=[qk, k("ex")], writes=[k("qt")])
                def mmtr(e):
                    for h in range(8):
                        r = e.matmul(ps[3][:, h * 32:(h + 1) * 32], lhsT=k_[:, h * 128:(h + 1) * 128], rhs=id32, start=True, stop=True)
                    return r
                S.op("pe", mmtr, reads=[k("ktm"), "cst"], writes=["ps3"])
                S.op("act", lambda e: e.activation(out=kT_[:], in_=ps[3][:, 0:256], func=AF.Copy), reads=["ps3"], writes=[k("ktT")])
                def mmat(e):
                    for h in range(8):
                        r = e.matmul(ps[4][0:32, h * 32:(h + 1) * 32], lhsT=kT_[:, h * 32:(h + 1) * 32], rhs=qt_[:, h * 32:(h + 1) * 32], start=True, stop=True)
                    return r
                S.op("pe", mmat, reads=[k("ktT"), k("qt")], writes=["ps4"])
                S.op("dve", lambda e: e.tensor_tensor(out=at_[:].rearrange("p (h t) -> p h t", h=8),
                                                      in0=ps[4][0:32, 0:256].rearrange("p (h t) -> p h t", h=8),
                                                      in1=ap3(MK, 0, [[0, 8], [1, 32]], npart=32), op=ALU.mult), reads=["ps4", "cst"], writes=[k("attm")])

            def stageB(it, p):
                d, ci, sj, first_of_dir, first_of_chunk = it
                k = lambda nm: "%s%d" % (nm, p)
                t0 = ci * 128 + sj * 32
                k_, v_, x_, qt_, at_, o_ = ktm[p], vtm[p], ex[p], qt[p], attm[p], otl[p]
                if first_of_dir:
                    S.op("pool", lambda e: e.memset(Sst[:], 0.0), writes=["Sst"])
                S.op("dve", lambda e: e.tensor_tensor(out=Sp[:].rearrange("p (h e) -> p h e", h=8),
                                                      in0=Sst[:].rearrange("p (h e) -> p h e", h=8),
                                                      in1=ap3(x_[:], 32, [[34, 8], [0, 128]]), op=ALU.mult), reads=["Sst", k("ex")], writes=["Sp"])
                def mmo(e):
                    for h in range(8):
                        pp = ps[5 + h // 4]; c0 = (h % 4) * 128
                        e.matmul(pp[0:32, c0:c0 + 128], lhsT=at_[:, h * 32:(h + 1) * 32], rhs=v_[:, h * 128:(h + 1) * 128], start=True, stop=False)
                        r = e.matmul(pp[0:32, c0:c0 + 128], lhsT=qt_[:, h * 32:(h + 1) * 32], rhs=Sp[:, h * 128:(h + 1) * 128], start=False, stop=True)
                    return r
                S.op("pe", mmo, reads=[k("attm"), k("vtm"), k("qt"), "Sp"], writes=["ps5", "ps6"])
                for hf in range(2):
                    S.op("act", lambda e, hf=hf: e.activation(out=o_[:, hf * 512:(hf + 1) * 512], in_=ps[5 + hf][0:32, 0:512], func=AF.Copy),
                         reads=["ps%d" % (5 + hf)], writes=[k("otl")])
                S.dma("pool", HAB.ap()[2 + d, t0:t0 + 32, :], o_[:], reads=[k("otl")], writes=["HAB"])
                for hf in range(2):
                    def mmu(e, hf=hf):
                        for h in range(hf * 4, hf * 4 + 4):
                            c0 = (h % 4) * 128
                            r = e.matmul(ps[7][:, c0:c0 + 128], lhsT=k_[:, h * 128:(h + 1) * 128], rhs=v_[:, h * 128:(h + 1) * 128], start=True, stop=True)
                        return r
                    S.op("pe", mmu, reads=[k("ktm"), k("vtm")], writes=["ps7"])
                    S.op("dve", lambda e, hf=hf: e.tensor_tensor(out=Sp[:, hf * 512:(hf + 1) * 512], in0=Sp[:, hf * 512:(hf + 1) * 512],
                                                                 in1=ps[7][:, 0:512], op=ALU.add), reads=["Sp", "ps7"], writes=["Sp"])
                S.op("dve", lambda e: e.tensor_tensor(out=Sst[:].rearrange("p (h e) -> p h e", h=8),
                                                      in0=Sp[:].rearrange("p (h e) -> p h e", h=8),
                                                      in1=ap3(x_[:], 33, [[34, 8], [0, 128]]), op=ALU.mult), reads=["Sp", k("ex")], writes=["Sst"])

            prev = None
            for n_, it in enumerate(items):
                stageA(it, n_ % 2)
                if prev is not None:
                    stageB(*prev)
                prev = (it, n_ % 2)
            stageB(*prev)

        def phase_mlstm(l, es2):
            P2 = lambda name, shape: [alloc(es2, "%s%d" % (name, i), shape) for i in range(2)]
            ktm = P2("aktm", [128, 1024]); vext = P2("vext", [128, 4 * 257]); qfm = P2("aqfm", [128, 1024]); gt = P2("gt", [128, 16])
            lf = P2("lf", [128, 4]); gi = P2("gi", [128, 4]); cbt = P2("cbt", [128, 4]); wl = P2("wl", [128, 4]); dec = P2("dec", [128, 4])
            otl = P2("aotl", [128, 1024])
            lfr = P2("lfr", [128, 128]); Dm = P2("Dm", [128, 128]); eB = P2("eB", [128, 128]); qt = P2("aqt", [128, 256]); kT = P2("kT", [128, 256])
            W = P2("W", [128, 128]); kh = P2("kh", [128, 256]); rec = P2("rec", [128, 1])
            bias = alloc(es2, "gbias", [128, L * 16]); Cx = alloc(es2, "Cx", [128, 8 * 257])
            S.dma("sp", bias[:], mgb.ap(), writes=["gbias"])
            for i in range(2):
                S.op("pool", lambda e, i=i: e.memset(vext[i][:], 1.0), writes=["vext%d" % i])
            items = []
            for d in range(2):
                for cn, ci in enumerate(chunk_order(d)):
                    for h in range(4):
                        items.append((d, ci, h, cn == 0 and h == 0))
            cpar = [0]

            def chunk_setup(d, ci, c):
                t0 = ci * 128
                kc = lambda nm: "%s%d" % (nm, c)
                S.dma("sp", ktm[c][:], Ptm.ap()[t0:t0 + 128, 0:1024], reads=["Ptm"], writes=[kc("aktm")])
                S.dma("sp", vext[c][:].rearrange("p (h e) -> p h e", h=4)[:, :, 0:256],
                      Ptm.ap()[t0:t0 + 128, 1024:2048].rearrange("t (h e) -> t h e", h=4), reads=["Ptm"], writes=[kc("vext")])
                S.dma("sp", gt[c][:], Ptm.ap()[t0:t0 + 128, NTMB * 256:NTMB * 256 + 16], reads=["Ptm"], writes=[kc("gt")])
                S.dma("pool", qfm[c][:].rearrange("p (h t) -> p h t", h=8),
                      PTfm.ap()[0:1024, t0:t0 + 128].rearrange("(h p) t -> p h t", p=128), reads=["PTfm"], writes=[kc("aqfm")])
                g_, lf_, gi_, cb_, wl_, dc_ = gt[c], lf[c], gi[c], cbt[c], wl[c], dec[c]
                TRI = cst[:, 256:384] if d == 0 else cst[:, 384:512]
                S.op("dve", lambda e: e.tensor_tensor(out=g_[:], in0=g_[:], in1=bias[:, l * 16:(l + 1) * 16], op=ALU.add), reads=[kc("gt"), "gbias"], writes=[kc("gt")])
                S.op("dve", lambda e: e.tensor_copy(out=gi_[:], in_=g_[:, d * 8:d * 8 + 4]), reads=[kc("gt")], writes=[kc("gi")])
                S.op("act", lambda e: e.activation(out=lf_[:], in_=g_[:, d * 8 + 4:d * 8 + 8], func=AF.Exp, scale=-1.0), reads=[kc("gt")], writes=[kc("lf")])
                S.op("dve", lambda e: e.tensor_scalar(out=lf_[:], in0=lf_[:], scalar1=1.0, scalar2=None, op0=ALU.add), reads=[kc("lf")], writes=[kc("lf")])
                S.op("act", lambda e: e.activation(out=lf_[:], in_=lf_[:], func=AF.Ln), reads=[kc("lf")], writes=[kc("lf")])
                S.op("dve", lambda e: e.tensor_scalar(out=lf_[:], in0=lf_[:], scalar1=-1.0, scalar2=None, op0=ALU.mult), reads=[kc("lf")], writes=[kc("lf")])
                def mmb(e):
                    e.matmul(ps[0][:, 0:4], lhsT=TRI, rhs=lf_[:], start=True, stop=True)
                    return e.matmul(ps[0][:, 4:8], lhsT=ones, rhs=lf_[:], start=True, stop=True)
                S.op("pe", mmb, reads=[kc("lf"), "cst"], writes=["ps0"])
                S.op("dve", lambda e: e.tensor_tensor(out=cb_[:], in0=gi_[:], in1=ps[0][:, 0:4], op=ALU.subtract), reads=[kc("gi"), "ps0"], writes=[kc("cbt")])
                S.op("dve", lambda e: e.tensor_tensor(out=wl_[:], in0=cb_[:], in1=ps[0][:, 4:8], op=ALU.add), reads=[kc("cbt"), "ps0"], writes=[kc("wl")])
                S.op("act", lambda e: e.activation(out=wl_[:], in_=wl_[:], func=AF.Exp), reads=[kc("wl")], writes=[kc("wl")])
                S.op("dve", lambda e: e.tensor_scalar(out=wl_[:], in0=wl_[:], scalar1=1.0 / 16.0, scalar2=None, op0=ALU.mult), reads=[kc("wl")], writes=[kc("wl")])
                S.op("act", lambda e: e.activation(out=dc_[:], in_=ps[0][:, 4:8], func=AF.Exp), reads=["ps0"], writes=[kc("dec")])

            def stageA(it, p):
                d, ci, h, first_of_dir = it
                if h == 0:
                    cpar[0] ^= 1
                    chunk_setup(d, ci, cpar[0])
                c = cpar[0]
                kc = lambda nm: "%s%d" % (nm, c)
                k = lambda nm: "%s%d" % (nm, p)
                TRI = cst[:, 256:384] if d == 0 else cst[:, 384:512]
                k_, q_, lf_, cb_, wl_ = ktm[c], qfm[c], lf[c], cbt[c], wl[c]
                lfr_, Dm_, eB_, qt_, kT_, W_, kh_ = lfr[p], Dm[p], eB[p], qt[p], kT[p], W[p], kh[p]
                S.op("dve", lambda e: e.tensor_scalar(out=lfr_[:], in0=ones, scalar1=lf_[:, h:h + 1], scalar2=None, op0=ALU.mult),
                     reads=[kc("lf"), "cst"], writes=[k("lfr")])
                S.op("pe", lambda e: e.matmul(ps[1][:, 0:128], lhsT=lfr_[:], rhs=TRI, start=True, stop=True), reads=[k("lfr"), "cst"], writes=["ps1"])
                S.op("act", lambda e: e.activation(out=Dm_[:], in_=ps[1][:, 0:128], func=AF.Exp, bias=cb_[:, h:h + 1]),
                     reads=["ps1", kc("cbt")], writes=[k("Dm")])
                S.op("dve", lambda e: e.tensor_tensor(out=Dm_[:], in0=Dm_[:], in1=TRI, op=ALU.mult), reads=[k("Dm"), "cst"], writes=[k("Dm")])
                S.op("act", lambda e: e.activation(out=eB_[:], in_=ps[1][:, 0:128], func=AF.Exp), reads=["ps1"], writes=[k("eB")])
                for dc in range(2):
                    qs = slice((h * 2 + dc) * 128, (h * 2 + dc + 1) * 128)
                    S.op("dve", lambda e, dc=dc, qs=qs: e.tensor_tensor(out=qt_[:, dc * 128:(dc + 1) * 128], in0=q_[:, qs], in1=eB_[:], op=ALU.mult),
                         reads=[kc("aqfm"), k("eB")], writes=[k("aqt")])
                    S.op("pe", lambda e, dc=dc: e.matmul(ps[2 + dc][:, 0:128], lhsT=k_[:, h * 256 + dc * 128:h * 256 + (dc + 1) * 128], rhs=ident,
                                                         start=True, stop=True), reads=[kc("aktm"), "cst"], writes=["ps%d" % (2 + dc)])
                    S.op("act", lambda e, dc=dc: e.activation(out=kT_[:, dc * 128:(dc + 1) * 128], in_=ps[2 + dc][:, 0:128], func=AF.Copy),
                         reads=["ps%d" % (2 + dc)], writes=[k("kT")])
                def mms(e):
                    e.matmul(ps[4][:, 0:128], lhsT=kT_[:, 0:128], rhs=q_[:, (h * 2) * 128:(h * 2 + 1) * 128], start=True, stop=False)
                    return e.matmul(ps[4][:, 0:128], lhsT=kT_[:, 128:256], rhs=q_[:, (h * 2 + 1) * 128:(h * 2 + 2) * 128], start=False, stop=True)
                S.op("pe", mms, reads=[k("kT"), kc("aqfm")], writes=["ps4"])
                S.op("dve", lambda e: e.scalar_tensor_tensor(out=W_[:], in0=ps[4][:, 0:128], scalar=1.0 / 16.0, in1=Dm_[:], op0=ALU.mult, op1=ALU.mult),
                     reads=["ps4", k("Dm")], writes=[k("W")])
                S.op("dve", lambda e: e.tensor_scalar(out=kh_[:], in0=k_[:, h * 256:(h + 1) * 256], scalar1=wl_[:, h:h + 1], scalar2=None,
                                                      op0=ALU.mult), reads=[kc("aktm"), kc("wl")], writes=[k("kh")])
                return c

            def stageB(it, p, c):
                d, ci, h, first_of_dir = it
                kc = lambda nm: "%s%d" % (nm, c)
                k = lambda nm: "%s%d" % (nm, p)
                v_, dc_, o_ = vext[c], dec[c], otl[c]
                qt_, W_, kh_, rec_ = qt[p], W[p], kh[p], rec[p]
                if first_of_dir:
                    S.op("pool", lambda e: e.memset(Cx[:], 0.0), writes=["Cx"])
                def mmn(e):
                    e.matmul(ps[5][:, 0:257], lhsT=W_[:], rhs=v_[:, h * 257:(h + 1) * 257], start=True, stop=False)
                    e.matmul(ps[5][:, 0:257], lhsT=qt_[:, 0:128], rhs=Cx[:, (h * 2) * 257:(h * 2 + 1) * 257], start=False, stop=False)
                    return e.matmul(ps[5][:, 0:257], lhsT=qt_[:, 128:256], rhs=Cx[:, (h * 2 + 1) * 257:(h * 2 + 2) * 257], start=False, stop=True)
                S.op("pe", mmn, reads=[k("W"), kc("vext"), k("aqt"), "Cx"], writes=["ps5"])
                S.op("act", lambda e: e.activation(out=rec_[:], in_=ps[5][:, 256:257], func=AF.Abs), reads=["ps5"], writes=[k("rec")])
                S.op("dve", lambda e: e.tensor_scalar(out=rec_[:], in0=rec_[:], scalar1=1.0, scalar2=None, op0=ALU.max), reads=[k("rec")], writes=[k("rec")])
                S.op("dve", lambda e: e.reciprocal(out=rec_[:], in_=rec_[:]), reads=[k("rec")], writes=[k("rec")])
                S.op("dve", lambda e: e.tensor_scalar(out=o_[:, h * 256:(h + 1) * 256], in0=ps[5][:, 0:256], scalar1=rec_[:, 0:1], scalar2=None,
                                                      op0=ALU.mult), reads=["ps5", k("rec")], writes=[kc("aotl")])
                for dc in range(2):
                    S.op("pe", lambda e, dc=dc: e.matmul(ps[6 + dc][:, 0:257], lhsT=kh_[:, dc * 128:(dc + 1) * 128], rhs=v_[:, h * 257:(h + 1) * 257],
                                                         start=True, stop=True), reads=[k("kh"), kc("vext")], writes=["ps%d" % (6 + dc)])
                    cs = slice((h * 2 + dc) * 257, (h * 2 + dc + 1) * 257)
                    S.op("dve", lambda e, dc=dc, cs=cs: e.scalar_tensor_tensor(out=Cx[:, cs], in0=Cx[:, cs], scalar=dc_[:, h:h + 1],
                                                                               in1=ps[6 + dc][:, 0:257], op0=ALU.mult, op1=ALU.add),
                         reads=["Cx", kc("dec"), "ps%d" % (6 + dc)], writes=["Cx"])
                if h == 3:
                    S.dma("pool", HAB.ap()[d, ci * 128:(ci + 1) * 128, :], o_[:], reads=[kc("aotl")], writes=["HAB"])

            prev = None
            for n_, it in enumerate(items):
                c = stageA(it, n_ % 2)
                if prev is not None:
                    stageB(*prev)
                prev = (it, n_ % 2, c)
            stageB(*prev)

        def phase_combine(l, es2):
            P2 = lambda name, shape: [alloc(es2, "%s%d" % (name, i), shape) for i in range(2)]
            haL = P2("ha", [128, 1024]); hbL = P2("hb", [128, 1024]); ogL = P2("og", [128, 1024]); sqL = P2("sq", [128, 1024]); ssL = P2("ss", [128, 8])
            yo = [alloc(es2, "yo%d" % i, [128, 128]) for i in range(2)]
            cnt = 0
            for br, (nh, ogblk, fn) in enumerate(((4, 2, AF.Sigmoid), (8, 6, AF.Silu))):
                hd = 1024 // nh
                for ti in range(cfg.T // 128):
                    t0 = ti * 128
                    p_ = cnt % 2; cnt += 1
                    ha, hb, og, sq, ss = haL[p_], hbL[p_], ogL[p_], sqL[p_], ssL[p_]
                    kha, khb, kog, ksq, kss = "ha%d" % p_, "hb%d" % p_, "og%d" % p_, "sq%d" % p_, "ss%d" % p_
                    S.dma("sp", ha[:], HAB.ap()[br * 2, t0:t0 + 128, :], reads=["HAB"], writes=[kha])
                    S.dma("sp", hb[:], HAB.ap()[br * 2 + 1, t0:t0 + 128, :], reads=["HAB"], writes=[khb])
                    S.dma("pool", og[:], Ptm.ap()[t0:t0 + 128, ogblk * 1024:(ogblk + 1) * 1024], reads=["Ptm"], writes=[kog])
                    S.op("dve", lambda e, ha=ha, hb=hb: e.tensor_tensor(out=ha[:], in0=ha[:], in1=hb[:], op=ALU.add), reads=[kha, khb], writes=[kha])
                    S.op("pool", lambda e, ha=ha, sq=sq: e.tensor_tensor(out=sq[:], in0=ha[:], in1=ha[:], op=ALU.mult), reads=[kha], writes=[ksq])
                    S.op("dve", lambda e, nh=nh, sq=sq, ss=ss: e.tensor_reduce(out=ss[:, 0:nh], in_=sq[:].rearrange("p (h e) -> p h e", h=nh),
                                                                 axis=mybir.AxisListType.X, op=ALU.add), reads=[ksq], writes=[kss])
                    S.op("dve", lambda e, nh=nh, hd=hd, ss=ss: e.tensor_scalar(out=ss[:, 0:nh], in0=ss[:, 0:nh], scalar1=1.0 / hd, scalar2=EPS,
                                                                        op0=ALU.mult, op1=ALU.add), reads=[kss], writes=[kss])
                    S.op("act", lambda e, nh=nh, ss=ss: e.activation(out=ss[:, 0:nh], in_=ss[:, 0:nh], func=AF.Sqrt), reads=[kss], writes=[kss])
                    S.op("dve", lambda e, nh=nh, ss=ss: e.reciprocal(out=ss[:, 0:nh], in_=ss[:, 0:nh]), reads=[kss], writes=[kss])
                    S.op("act", lambda e, fn=fn, og=og: e.activation(out=og[:], in_=og[:], func=fn), reads=[kog], writes=[kog])
                    for h in range(nh):
                        S.op("dve", lambda e, h=h, hd=hd, ha=ha, ss=ss, og=og: e.scalar_tensor_tensor(out=ha[:, h * hd:(h + 1) * hd], in0=ha[:, h * hd:(h + 1) * hd],
                                                                                scalar=ss[:, h:h + 1], in1=og[:, h * hd:(h + 1) * hd],
                                                                                op0=ALU.mult, op1=ALU.mult), reads=[kha, kss, kog], writes=[kha])
                    for fc in range(8):
                        p = ps[fc % 2]; pk = "ps%d" % (fc % 2); y = yo[fc % 2]; yk = "yo%d" % (fc % 2)
                        S.op("pe", lambda e, p=p, fc=fc, ha=ha: e.matmul(p[:, 0:128], lhsT=ha[:, fc * 128:(fc + 1) * 128], rhs=ident, start=True, stop=True),
                             reads=[kha, "cst"], writes=[pk])
                        S.op("act", lambda e, p=p, y=y: e.activation(out=y[:], in_=p[:, 0:128], func=AF.Copy), reads=[pk], writes=[yk])
                        S.dma("pool", YT.ap()[br * 1024 + fc * 128:br * 1024 + (fc + 1) * 128, t0:t0 + 128], y[:], reads=[yk], writes=["YT"])

        def phase_merge(l, es2, skip_ctx):
            yt = alloc(es2, "yt", [128, 24 * G]); mT = alloc(es2, "mT", [128, KC * G])
            hr = [alloc(es2, "hrm%d" % i, [128, G]) for i in range(2)]
            mg = [alloc(es2, "mg%d" % i, [128, 3 * G]) for i in range(2)]
            wb = [alloc(es2, "wbm%d" % i, [128, 3 * 1024]) for i in range(2)]
            ho = [alloc(es2, "hom%d" % i, [128, G]) for i in range(2)]
            for (t0, n, st) in cfg.groups:
                if skip_ctx and st == 1:
                    continue
                S.dma("sp", RRm(yt[:, 0:24 * n]).rearrange("p (c t) -> p c t", c=24), RRm(YT.ap()[:, t0:t0 + n]).rearrange("(c p) t -> p c t", p=128),
                      reads=["YT"], writes=["yt"])
                for dc in range(KC):
                    b = wb[dc % 2]; bk = "wbm%d" % (dc % 2); g = mg[dc % 2]; gk = "mg%d" % (dc % 2)
                    for br in range(3):
                        S.dma("sp", RRm(b[:, br * 1024:(br + 1) * 1024]), RRm(wbr.ap()[l, br, dc]), writes=[bk + str(br)])
                        S.dma("pool", g[:, br * n:(br + 1) * n], PTfm.ap()[(32 + br * 16 + dc) * 128:(33 + br * 16 + dc) * 128, t0:t0 + n],
                              reads=["PTfm"], writes=[gk + str(br)])
                    S.op("act", lambda e, g=g: e.activation(out=g[:, 0:3 * n], in_=g[:, 0:3 * n], func=AF.Sigmoid),
                         reads=[gk + str(i) for i in range(3)], writes=[gk + str(i) for i in range(3)])
                    for br in range(3):
                        p = ps[br]; pk = "ps%d" % br
                        def mm(e, b=b, p=p, br=br):
                            for fc in range(8):
                                r = e.matmul(p[:, 0:n], lhsT=RRm(b[:, br * 1024 + fc * 128:br * 1024 + (fc + 1) * 128]),
                                             rhs=RRm(yt[:, (br * 8 + fc) * n:(br * 8 + fc + 1) * n]), start=(fc == 0), stop=(fc == 7))
                            return r
                        S.op("pe", mm, reads=[bk + str(br), "yt"], writes=[pk])
                        if br == 0:
                            S.op("dve", lambda e, p=p, g=g, dc=dc: e.tensor_tensor(out=RRm(mT[:, dc * n:(dc + 1) * n]), in0=p[:, 0:n], in1=g[:, 0:n], op=ALU.mult),
                                 reads=[pk, gk + "0"], writes=["mT%d" % dc])
                        else:
                            S.op("dve", lambda e, p=p, g=g, br=br: e.tensor_tensor(out=g[:, br * n:(br + 1) * n], in0=p[:, 0:n], in1=g[:, br * n:(br + 1) * n], op=ALU.mult),
                                 reads=[pk, gk + str(br)], writes=[gk + str(br)])
                            S.op("dve", lambda e, g=g, br=br, dc=dc: e.tensor_tensor(out=RRm(mT[:, dc * n:(dc + 1) * n]), in0=mT[:, dc * n:(dc + 1) * n],
                                                                                 in1=g[:, br * n:(br + 1) * n], op=ALU.add),
                                 reads=[gk + str(br), "mT%d" % dc], writes=["mT%d" % dc])
                mk_ = ["mT%d" % dc for dc in range(KC)]
                for fo in range(KC):
                    b = wb[fo % 2]; bk = "wbm%d" % (fo % 2)
                    S.dma("sp", RRm(b[:, 0:2048]), RRm(wo.ap()[l, fo]), writes=[bk + "0", bk + "1"])
                    p = ps[4 + fo % 2]; pk = "ps%d" % (4 + fo % 2)
                    def mm2(e, b=b, p=p):
                        for dc in range(KC):
                            r = e.matmul(p[:, 0:n], lhsT=RRm(b[:, dc * 128:(dc + 1) * 128]), rhs=RRm(mT[:, dc * n:(dc + 1) * n]), start=(dc == 0), stop=(dc == KC - 1))
                        return r
                    S.op("pe", mm2, reads=[bk + "0", bk + "1"] + mk_, writes=[pk])
                    hot = ho[fo % 2]; hk = "hom%d" % (fo % 2)
                    hrt = hr[fo % 2]; rk = "hrm%d" % (fo % 2)
                    S.dma("pool", hrt[:, 0:n], hT.ap()[fo * 128:(fo + 1) * 128, t0:t0 + n], writes=[rk])
                    S.op("dve", lambda e, hot=hot, p=p, fo=fo, hrt=hrt: e.scalar_tensor_tensor(
                        out=hot[:, 0:n], in0=p[:, 0:n], scalar=col(ABv(1, st, 2), fo), in1=hrt[:, 0:n],
                        op0=ALU.mult, op1=ALU.add), reads=[pk, rk, "AB"], writes=[hk])
                    S.dma("pool", hT.ap()[fo * 128:(fo + 1) * 128, t0:t0 + n], hot[:, 0:n], reads=[hk], writes=["hT"])

        def phase_final(es2):
            hx = alloc(es2, "hx", [128, KC * G])
            xn = alloc(es2, "xn", [128, KC * G])
            rstd = alloc(es2, "rstd", [128, G])
            for (t0, n, st) in cfg.groups:
                if st == 1:
                    continue
                load_h(hx, hT, t0, n)
                S.op("act", lambda e: e.activation(out=xn[:, 0:KC * n], in_=hx[:, 0:KC * n], func=AF.Square),
                     reads=["hx"], writes=["xn"])
                def mm(e):
                    for kc in range(KC):
                        r = e.matmul(ps[7][:, 0:n], lhsT=ones, rhs=xn[:, kc * n:(kc + 1) * n], start=(kc == 0), stop=(kc == KC - 1))
                    return r
                S.op("pe", mm, reads=["xn", "cst"], writes=["ps7"])
                S.op("dve", lambda e: e.tensor_scalar(out=rstd[:, 0:n], in0=ps[7][:, 0:n], scalar1=1.0 / D, scalar2=EPS,
                                                      op0=ALU.mult, op1=ALU.add), reads=["ps7"], writes=["rstd"])
                S.op("act", lambda e: e.activation(out=rstd[:, 0:n], in_=rstd[:, 0:n], func=AF.Sqrt), reads=["rstd"], writes=["rstd"])
                S.op("dve", lambda e: e.reciprocal(out=rstd[:, 0:n], in_=rstd[:, 0:n]), reads=["rstd"], writes=["rstd"])
                for kc in range(KC):
                    S.op("dve", lambda e, kc=kc: e.scalar_tensor_tensor(
                        out=xn[:, kc * n:(kc + 1) * n], in0=hx[:, kc * n:(kc + 1) * n], scalar=col(fgt, kc),
                        in1=rstd[:, 0:n], op0=ALU.mult, op1=ALU.mult), reads=["hx", "rstd", "fgt", "xn"], writes=["xn"])
                S.dma("pool", out.ap()[:, t0 - cfg.CTX:t0 - cfg.CTX + n].rearrange("(k p) t -> p k t", p=128),
                      xn[:, 0:KC * n].rearrange("p (k t) -> p k t", k=KC), reads=["xn"], writes=["out"])

        with ExitStack() as es2:
            phase_lb(es2)
            S.barrier()
        for l in range(L):
            last = (l == L - 1)
            with ExitStack() as es2:
                phase_mod(l, es2)
                S.barrier()
            with ExitStack() as es2:
                phase_ffn(l, 0, 0, xT if l == 0 else hT, es2)
                S.barrier()
            if stop_after == "ffn1":
                break
            with ExitStack() as es2:
                phase_proj(l, es2)
                S.barrier()
            if stop_after == "proj":
                break
            for ph in (phase_rglru, phase_hgrn, phase_mlstm, phase_combine):
                with ExitStack() as es2:
                    ph(l, es2)
                    S.barrier()
            with ExitStack() as es2:
                phase_merge(l, es2, last)
                S.barrier()
            with ExitStack() as es2:
                phase_ffn(l, 1, 2, hT, es2, skip_ctx=last)
                S.barrier()
        with ExitStack() as es2:
            phase_final(es2)
            S.barrier()
        print("instructions emitted:", S.ninst)
    return nc


def wblocks(w, ncols=128):
    K, N = w.shape
    return np.ascontiguousarray(w.reshape(K // 128, 128, N // ncols, ncols).transpose(2, 1, 0, 3)).reshape(N // ncols, 128, (K // 128) * ncols)


def pvec(v):
    v = np.asarray(v, np.float32)
    lead = v.shape[:-1]
    r = v.reshape(*lead, v.shape[-1] // 128, 128)
    r = np.moveaxis(r, -1, 0)
    return np.ascontiguousarray(r).reshape(128, -1)


def make_consts():
    c = np.zeros((128, 1024), np.float32)
    c[:, 0:128] = 1.0
    c[:, 128:256] = np.eye(128, dtype=np.float32)
    i = np.arange(128)
    triU = (i[:, None] <= i[None, :]).astype(np.float32)
    triL = (i[:, None] >= i[None, :]).astype(np.float32)
    c[:, 256:384] = triU
    c[:, 384:512] = triL
    lo = (i <= 63).astype(np.float32)[:, None]
    hi = (i >= 64).astype(np.float32)[:, None]
    c[:, 512:640] = triU - lo
    c[:, 640:641] = lo
    c[:, 641:642] = hi
    c[:, 642:770] = triL - hi
    c[:, 770:771] = hi
    c[:, 771:772] = lo
    j = np.arange(32)
    tU = (j[:, None] <= j[None, :]).astype(np.float32)
    tL = (j[:, None] >= j[None, :]).astype(np.float32)
    lo32 = (j <= 15).astype(np.float32)[:, None]
    hi32 = (j >= 16).astype(np.float32)[:, None]
    c[0:32, 772:804] = tU - lo32
    c[0:32, 804:805] = lo32
    c[0:32, 805:806] = hi32
    c[0:32, 806:838] = tL - hi32
    c[0:32, 838:839] = hi32
    c[0:32, 839:840] = lo32
    return c


def prep_shared(inp, cfg):
    L = cfg.DEPTH
    sh = {}
    sh["modw"] = np.stack([wblocks(inp["mod_w"][l]) for l in range(L)])
    sh["modb"] = pvec(inp["mod_b"][:L])
    sh["normg"] = pvec(inp["norm_g"][:L])
    sh["w1t"] = np.stack([np.stack([wblocks(inp["ffn_w_in"][l, w]) for w in range(2)]) for l in range(L)])
    sh["w2t"] = np.stack([np.stack([wblocks(inp["ffn_w_out"][l, w]) for w in range(2)]) for l in range(L)])
    sh["fing"] = pvec(inp["final_g"])
    win = inp["w_in"]
    sh["wpf"] = np.stack([np.concatenate([wblocks(win[l][:, o:o + w]) for (o, w) in FM_COLS]) for l in range(L)])
    sh["wpt"] = np.stack([np.concatenate([wblocks(win[l][:, o:o + w], 256) for (o, w) in TM_COLS]) for l in range(L)])
    sh["wpg"] = np.stack([wblocks(win[l][:, 4096:4112], 16)[0] for l in range(L)])
    sh["consts"] = make_consts()
    sh["convw"] = pvec(inp["conv_w"][:L])
    sh["convb"] = pvec(inp["conv_b"][:L])
    sh["rgb"] = pvec(inp["rg_gate_b"][:L])
    sh["rglam"] = pvec(inp["rg_lambda"][:L])
    rg = inp["rg_gate_w"][:L].reshape(L, 4, 8, 2, 64, 64)
    bd = np.zeros((L, 4, 8, 128, 128), np.float32)
    bd[:, :, :, 0:64, 0:64] = rg[:, :, :, 0]
    bd[:, :, :, 64:128, 64:128] = rg[:, :, :, 1]
    sh["rgw"] = bd
    sh["lblog"] = np.ascontiguousarray(np.broadcast_to(inp["hgrn_lb_logits"][:L].reshape(1, -1), (128, L * 2 * 1024)))
    sh["mgb"] = np.ascontiguousarray(np.broadcast_to(inp["mlstm_gate_b"][:L].reshape(1, -1), (128, L * 16)))
    sh["wbr"] = np.stack([np.stack([wblocks(inp["branch_w"][l, br]) for br in range(3)]) for l in range(L)])
    sh["wo"] = np.stack([wblocks(inp["w_out"][l]) for l in range(L)])
    return sh


def prep_batch(inp, cfg, b):
    m = {}
    m["xT"] = np.ascontiguousarray(np.concatenate([inp["ctx"][b].T, inp["x"][b].T], axis=1))
    cv = np.stack([inp["c"][b], inp["c_ctx"]], axis=-1)
    m["cvec"] = np.ascontiguousarray(cv.reshape(KC, 128, 2).transpose(1, 0, 2)).reshape(128, KC * 2)
    return m


def kernel(**inputs):
    inp = {k: np.asarray(v) for k, v in inputs.items()}
    cfg = Cfg()
    nc = build(cfg)
    sh = prep_shared(inp, cfg)
    in_maps = []
    for b in range(8):
        m = dict(sh)
        m.update(prep_batch(inp, cfg, b))
        in_maps.append(m)
    res = run_bass_kernel_spmd(nc, in_maps, core_ids=list(range(8)))
    out = np.stack([np.ascontiguousarray(res.results[b]["out"].T) for b in range(8)])
    return out.astype(np.float32)
```
